# Optimizing a Trainium2 kernel written in Bass

```python
import jax, jax.numpy as jnp
from jax import lax
import numpy as np

D_MODEL = 2048
BATCH = 4
SEQ = 4096
DEPTH = 2
DEC_BATCH = 16
DEC_SEQ = 16
PAST_LEN = 2048

CHUNK = 64
Q_BLOCK = 128
MLA_HEADS = 8
QK_NOPE = 128
QK_ROPE = 64
V_HEAD = 128
Q_LORA = 512
KV_LORA = 512
MLA_QK = QK_NOPE + QK_ROPE
MLA_SCALE = MLA_QK ** -0.5
RET_HEADS = 8
RET_DK = 128
RET_DV = 256
RET_QK_W = RET_HEADS * RET_DK
RET_V_W = RET_HEADS * RET_DV
RET_K_SCALE = RET_DK ** -0.5
D_FF = 5632
ROPE_THETA = 10000.0
NORM_EPS = 1e-6
GN_EPS = 1e-5
IN_WIDTH = Q_LORA + KV_LORA + QK_ROPE + 2 * RET_QK_W + 2 * RET_V_W + 2 * D_MODEL

kernel_name = "mla_retention_gated_macaron_stream_step"


def _in_offsets():
    sizes = (Q_LORA, KV_LORA, QK_ROPE, RET_QK_W, RET_QK_W, RET_V_W, RET_V_W, D_MODEL, D_MODEL)
    offs, acc = [], 0
    for s in sizes[:-1]:
        acc += s
        offs.append(acc)
    return offs


def rms_norm(x, g):
    xf = x.astype(jnp.float32)
    y = xf * lax.rsqrt(jnp.mean(xf * xf, axis=-1, keepdims=True) + NORM_EPS)
    return (y * g.astype(jnp.float32)).astype(x.dtype)


def head_group_norm(x):
    xf = x.astype(jnp.float32)
    mu = jnp.mean(xf, axis=-1, keepdims=True)
    var = jnp.mean(jnp.square(xf - mu), axis=-1, keepdims=True)
    return (xf - mu) * lax.rsqrt(var + GN_EPS)


def swiglu(x, w13, w2):
    a, b = jnp.split(x @ w13, 2, axis=-1)
    return (jax.nn.silu(a) * b) @ w2


def rope(x, pos):
    d = x.shape[-1]
    inv = ROPE_THETA ** (-jnp.arange(0, d, 2, dtype=jnp.float32) / d)
    ang = pos.astype(jnp.float32)[:, None] * inv[None, :]
    shape = (pos.shape[0],) + (1,) * (x.ndim - 3) + (d // 2,)
    cos = jnp.cos(ang).reshape(shape)
    sin = jnp.sin(ang).reshape(shape)
    xf = x.astype(jnp.float32)
    x1, x2 = xf[..., : d // 2], xf[..., d // 2:]
    return jnp.concatenate([x1 * cos - x2 * sin, x2 * cos + x1 * sin], axis=-1).astype(x.dtype)


def mla_prompt(q_nope, q_rope, c, kr, pos, w_uk, w_uv):
    B, T, H, _ = q_nope.shape
    nb = T // Q_BLOCK
    k_nope = jnp.einsum('bkr,rhd->bkhd', c, w_uk)
    v = jnp.einsum('bkr,rhd->bkhd', c, w_uv)
    chunk_id = pos // CHUNK

    def block(args):
        qn, qr, qc = args
        s = (jnp.einsum('bqhd,bkhd->bhqk', qn, k_nope)
             + jnp.einsum('bqhe,bke->bhqk', qr, kr)).astype(jnp.float32) * MLA_SCALE
        mask = chunk_id[None, :] <= qc[:, None]
        p = jax.nn.softmax(jnp.where(mask[None, None], s, -jnp.inf), axis=-1).astype(v.dtype)
        return jnp.einsum('bhqk,bkhd->bqhd', p, v)

    def to_blocks(a):
        return jnp.moveaxis(a.reshape((B, nb, Q_BLOCK) + a.shape[2:]), 1, 0)

    out = lax.map(block, (to_blocks(q_nope), to_blocks(q_rope), chunk_id.reshape(nb, Q_BLOCK)))
    return jnp.moveaxis(out, 0, 1).reshape(B, T, H, V_HEAD)


def mla_sample(q_nope, q_rope, c_all, kr_all, w_uk, w_uv):
    q_lat = jnp.einsum('bqhd,rhd->bqhr', q_nope, w_uk)
    s = (jnp.einsum('bqhr,bkr->bhqk', q_lat, c_all)
         + jnp.einsum('bqhe,bke->bhqk', q_rope, kr_all)).astype(jnp.float32) * MLA_SCALE
    p = jax.nn.softmax(s, axis=-1).astype(c_all.dtype)
    o_lat = jnp.einsum('bhqk,bkr->bqhr', p, c_all)
    return jnp.einsum('bqhr,rhd->bqhd', o_lat, w_uv)


def retention(q, k, v, s0):
    B, T, H, _ = q.shape
    cs = min(CHUNK, T)
    n = T // cs
    lg = jnp.log1p(-jnp.exp2(-5.0 - jnp.arange(H, dtype=jnp.float32)))
    j = jnp.arange(cs, dtype=jnp.float32)
    diff = j[:, None] - j[None, :]
    dmask = jnp.where(diff[None] >= 0, jnp.exp(jnp.maximum(diff, 0.0)[None] * lg[:, None, None]), 0.0)
    q_dec = jnp.exp((j + 1.0)[None] * lg[:, None])[..., None]
    k_dec = jnp.exp((cs - 1.0 - j)[None] * lg[:, None])[..., None]
    c_dec = jnp.exp(cs * lg)[:, None, None]

    def to_chunks(a):
        a = a.astype(jnp.float32)
        return a.reshape(B, n, cs, H, a.shape[-1]).transpose(1, 0, 3, 2, 4)

    def step(S, inp):
        qc, kc, vc = inp
        att = jnp.einsum('bhid,bhjd->bhij', qc, kc) * dmask
        o = jnp.einsum('bhij,bhje->bhie', att, vc) + jnp.einsum('bhid,bhde->bhie', qc * q_dec, S)
        S = S * c_dec + jnp.einsum('bhjd,bhje->bhde', kc * k_dec, vc)
        return S, o

    S, o = lax.scan(step, s0.astype(jnp.float32), (to_chunks(q), to_chunks(k), to_chunks(v)))
    o = o.transpose(1, 0, 3, 2, 4).reshape(B, T, H, v.shape[-1])
    return o, S


def trunk_layer(x, pos, past_c, past_kr, ret_s0, ffn1_norm, ffn1_w13, ffn1_w2, mix_norm, w_in,
                q_norm, kv_norm, w_uq, w_uk, w_uv, w_mla_out, w_ret_out, w_out,
                ffn2_norm, ffn2_w13, ffn2_w2):
    B, T, _ = x.shape
    x = x + 0.5 * swiglu(rms_norm(x, ffn1_norm), ffn1_w13, ffn1_w2)
    u = rms_norm(x, mix_norm)
    q_lat, c_kv, k_r, r_q, r_k, r_v, r_g, g_mla, g_ret = jnp.split(u @ w_in, _in_offsets(), axis=-1)
    q = (rms_norm(q_lat, q_norm) @ w_uq).reshape(B, T, MLA_HEADS, MLA_QK)
    q_nope = q[..., :QK_NOPE]
    q_rope = rope(q[..., QK_NOPE:], pos)
    c_kv = rms_norm(c_kv, kv_norm)
    k_r = rope(k_r, pos)
    if past_c is None:
        a = mla_prompt(q_nope, q_rope, c_kv, k_r, pos, w_uk, w_uv)
    else:
        a = mla_sample(q_nope, q_rope, jnp.concatenate([past_c, c_kv], axis=1),
                       jnp.concatenate([past_kr, k_r], axis=1), w_uk, w_uv)
    a = a.reshape(B, T, MLA_HEADS * V_HEAD) @ w_mla_out
    r_q = rope(r_q.reshape(B, T, RET_HEADS, RET_DK), pos)
    r_k = rope(r_k.reshape(B, T, RET_HEADS, RET_DK), pos) * RET_K_SCALE
    r_v = r_v.reshape(B, T, RET_HEADS, RET_DV)
    if ret_s0 is None:
        ret_s0 = jnp.zeros((B, RET_HEADS, RET_DK, RET_DV), jnp.float32)
    r, s_new = retention(r_q, r_k, r_v, ret_s0)
    r = head_group_norm(r).reshape(B, T, RET_V_W).astype(x.dtype)
    r = (jax.nn.silu(r_g) * r) @ w_ret_out
    x = x + (jax.nn.sigmoid(g_mla) * a + jax.nn.sigmoid(g_ret) * r) @ w_out
    x = x + 0.5 * swiglu(rms_norm(x, ffn2_norm), ffn2_w13, ffn2_w2)
    return x, c_kv, k_r, s_new.astype(x.dtype)


def setup_inputs(seed: int = 0) -> dict:
    key = jax.random.key(seed)
    ks = jax.random.split(key, 24)
    f32 = jnp.float32

    def w(k, shape, fan_in):
        return jax.random.normal(k, shape, f32) * (fan_in ** -0.5)

    def gain(k, shape):
        return 1.0 + 0.01 * jax.random.normal(k, shape, f32)

    return {
        "x_prompt": jax.random.normal(ks[0], (BATCH, SEQ, D_MODEL), f32),
        "x_sample": jax.random.normal(ks[1], (DEC_BATCH, DEC_SEQ, D_MODEL), f32),
        "cache_ckv": jax.random.normal(ks[2], (DEPTH, DEC_BATCH, PAST_LEN, KV_LORA), f32),
        "cache_krope": jax.random.normal(ks[3], (DEPTH, DEC_BATCH, PAST_LEN, QK_ROPE), f32),
        "state_ret": jax.random.normal(ks[4], (DEPTH, DEC_BATCH, RET_HEADS, RET_DK, RET_DV), f32),
        "ffn1_norm": gain(ks[5], (DEPTH, D_MODEL)),
        "ffn1_w13": w(ks[6], (DEPTH, D_MODEL, 2 * D_FF), D_MODEL),
        "ffn1_w2": w(ks[7], (DEPTH, D_FF, D_MODEL), D_FF),
        "mix_norm": gain(ks[8], (DEPTH, D_MODEL)),
        "w_in": w(ks[9], (DEPTH, D_MODEL, IN_WIDTH), D_MODEL),
        "q_norm": gain(ks[10], (DEPTH, Q_LORA)),
        "kv_norm": gain(ks[11], (DEPTH, KV_LORA)),
        "w_uq": w(ks[12], (DEPTH, Q_LORA, MLA_HEADS * MLA_QK), Q_LORA),
        "w_uk": w(ks[13], (DEPTH, KV_LORA, MLA_HEADS, QK_NOPE), KV_LORA),
        "w_uv": w(ks[14], (DEPTH, KV_LORA, MLA_HEADS, V_HEAD), KV_LORA),
        "w_mla_out": w(ks[15], (DEPTH, MLA_HEADS * V_HEAD, D_MODEL), MLA_HEADS * V_HEAD),
        "w_ret_out": w(ks[16], (DEPTH, RET_V_W, D_MODEL), RET_V_W),
        "w_out": w(ks[17], (DEPTH, D_MODEL, D_MODEL), D_MODEL),
        "ffn2_norm": gain(ks[18], (DEPTH, D_MODEL)),
        "ffn2_w13": w(ks[19], (DEPTH, D_MODEL, 2 * D_FF), D_MODEL),
        "ffn2_w2": w(ks[20], (DEPTH, D_FF, D_MODEL), D_FF),
        "final_norm": gain(ks[21], (D_MODEL,)),
    }


def reference(x_prompt, x_sample, cache_ckv, cache_krope, state_ret, ffn1_norm, ffn1_w13, ffn1_w2,
              mix_norm, w_in, q_norm, kv_norm, w_uq, w_uk, w_uv, w_mla_out, w_ret_out, w_out,
              ffn2_norm, ffn2_w13, ffn2_w2, final_norm):
    past = cache_ckv.shape[2]
    pos_p = jnp.arange(x_prompt.shape[1])
    pos_s = past + jnp.arange(x_sample.shape[1])
    hp, hs = x_prompt, x_sample
    ckv_p, kr_p, ret_p, ckv_s, kr_s, ret_s = [], [], [], [], [], []
    for l in range(DEPTH):
        lw = (ffn1_norm[l], ffn1_w13[l], ffn1_w2[l], mix_norm[l], w_in[l], q_norm[l], kv_norm[l],
              w_uq[l], w_uk[l], w_uv[l], w_mla_out[l], w_ret_out[l], w_out[l],
              ffn2_norm[l], ffn2_w13[l], ffn2_w2[l])
        hp, c1, k1, s1 = trunk_layer(hp, pos_p, None, None, None, *lw)
        hs, c2, k2, s2 = trunk_layer(hs, pos_s, cache_ckv[l], cache_krope[l], state_ret[l], *lw)
        ckv_p.append(c1); kr_p.append(k1); ret_p.append(s1)
        ckv_s.append(c2); kr_s.append(k2); ret_s.append(s2)
    y_prompt = rms_norm(hp, final_norm)
    y_sample = rms_norm(hs, final_norm)
    return (y_prompt, y_sample, jnp.stack(ckv_p), jnp.stack(kr_p), jnp.stack(ret_p),
            jnp.stack(ckv_s), jnp.stack(kr_s), jnp.stack(ret_s))
```

```python
import math
from contextlib import ExitStack
import numpy as np
import ml_dtypes
import concourse.bass as bass
import concourse.mybir as mybir
from concourse.bass_utils import run_bass_kernel_spmd

F32 = mybir.dt.float32
BF16 = mybir.dt.bfloat16
AF = mybir.ActivationFunctionType
ALU = mybir.AluOpType

D = 2048
KC = 16
DFF = 5632
FC = 44
NL = 2
TP = 2048
TS = 32
TT = TP + TS
SEQ = 4096
PAST = 2048
INW = 11328
O_QL, O_CKV, O_KR, O_RQ, O_RK, O_RV, O_RG, O_GM, O_GR = 0, 512, 1024, 1088, 2112, 3136, 5184, 7232, 9280
MLA_SCALE = 192 ** -0.5
TILES = [(0, 512), (512, 512), (1024, 512), (1536, 512), (2048, 32)]
NEG = -30000.0


class Sem:
    def __init__(self, h):
        self.h = h
        self.n = 0


class Ins:
    __slots__ = ("eng", "fn", "waits", "signal", "idx", "dma", "sigidx")


class Prog:
    def __init__(self, nc, es):
        self.nc = nc
        self.es = es
        self.eng = {"pe": nc.tensor, "act": nc.scalar, "dve": nc.vector, "pool": nc.gpsimd, "sp": nc.sync}
        self.esem = {e: es.enter_context(nc.semaphore("s_" + e)) for e in self.eng}
        self.lists = []
        self.count = {e: 0 for e in self.eng}
        self.known = {e: {x: 0 for x in self.eng} for e in self.eng}
        self.knownd = {e: {} for e in self.eng}
        self.lastw = {}
        self.readers = {}
        self.last_comp = {e: None for e in self.eng}
        self.dsems = []
        self.ges = es
        self.freesems = []
        self.nsem = 0
        self.in_phase = False
        self.phase_sems = []
        self.sig = {e: 0 for e in self.eng}

    def newsem(self, name):
        if self.freesems:
            s = self.freesems.pop()
        else:
            self.nsem += 1
            s = Sem(self.ges.enter_context(self.nc.semaphore("d%d" % self.nsem)))
            self.dsems.append(s)
        if self.in_phase:
            self.phase_sems.append(s)
        return s

    def begin_phase(self, ph):
        self.es = ph
        self.in_phase = True
        self.phase_sems = []

    def end_phase(self):
        self.barrier()
        self.emit()
        self.freesems.extend(self.phase_sems)
        self.phase_sems = []
        self.in_phase = False
        self.es = self.ges

    def _deps(self, E, reads, writes):
        cmax = {}
        dmax = {}

        def add(ev):
            if ev[0] == "c":
                J = ev[1]
                if J.eng == E and E == "pe":
                    return
                if J.eng not in cmax or cmax[J.eng].idx < J.idx:
                    cmax[J.eng] = J
            else:
                s, v = ev[1], ev[2]
                if dmax.get(s, 0) < v:
                    dmax[s] = v

        for r in reads:
            w = self.lastw.get(r)
            if w is not None:
                add(w)
        for w_ in writes:
            w = self.lastw.get(w_)
            if w is not None:
                add(w)
            for ev in self.readers.get(w_, {}).values():
                add(ev)
        waits = []
        for X, J in cmax.items():
            if self.known[E][X] >= J.idx:
                continue
            self.known[E][X] = J.idx
            J.signal = True
            waits.append(("c", J))
        for s, v in dmax.items():
            if self.knownd[E].get(s, 0) >= v:
                continue
            self.knownd[E][s] = v
            waits.append(("d", s, v))
        return waits

    def _record(self, ev, reads, writes):
        key = ("c", ev[1].eng) if ev[0] == "c" else ("d", ev[1])
        for r in reads:
            self.readers.setdefault(r, {})[key] = ev
        for w in writes:
            self.lastw[w] = ev
            self.readers[w] = {}

    def op(self, E, fn, reads=(), writes=()):
        pr = [r for r in reads if isinstance(r, str) and r.startswith("ps") and r not in writes]
        if pr:
            writes = list(writes) + pr
        I = Ins()
        I.eng = E
        I.fn = fn
        I.signal = False
        I.dma = None
        self.count[E] += 1
        I.idx = self.count[E]
        I.waits = self._deps(E, reads, writes)
        self._record(("c", I), reads, writes)
        self.lists.append(I)
        self.last_comp[E] = I
        return I

    def dma(self, Q, fn, sem, reads=(), writes=()):
        I = Ins()
        I.eng = Q
        I.fn = fn
        I.signal = False
        self.count[Q] += 1
        I.idx = self.count[Q]
        I.waits = self._deps(Q, reads, writes)
        sem.n += 16
        I.dma = (sem, sem.n)
        self._record(("d", sem, sem.n), reads, writes)
        self.lists.append(I)
        return I

    def barrier(self):
        nc = self.nc
        waits = []
        for X in self.eng:
            J = self.last_comp[X]
            if J is None or X == "sp":
                continue
            if self.known["sp"][X] >= J.idx:
                continue
            self.known["sp"][X] = J.idx
            J.signal = True
            waits.append(("c", J))
        for s in self.dsems:
            if s.n > self.knownd["sp"].get(s, 0):
                self.knownd["sp"][s] = s.n
                waits.append(("d", s, s.n))
        I = Ins()
        I.eng = "sp"
        I.fn = lambda: nc.sync.nop()
        I.signal = True
        I.dma = None
        self.count["sp"] += 1
        I.idx = self.count["sp"]
        I.waits = waits
        self.lists.append(I)
        self.last_comp["sp"] = I
        for X in self.eng:
            if X == "sp":
                continue
            J = Ins()
            J.eng = X
            eng = self.eng[X]
            J.fn = (lambda e: (lambda: e.nop()))(eng)
            J.signal = False
            J.dma = None
            self.count[X] += 1
            J.idx = self.count[X]
            J.waits = [("c", I)]
            self.lists.append(J)
            self.last_comp[X] = J
        self.lastw = {}
        self.readers = {}
        for E in self.eng:
            for X in self.eng:
                self.known[E][X] = self.count[X]
            for s in self.dsems:
                self.knownd[E][s] = s.n

    def emit(self):
        sig = self.sig
        for I in self.lists:
            eng = self.eng[I.eng]
            for ev in I.waits:
                if ev[0] == "c":
                    eng.wait_ge(self.esem[ev[1].eng], ev[1].sigidx)
                else:
                    eng.wait_ge(ev[1].h, ev[2])
            bi = I.fn()
            if I.dma is not None:
                bi.then_inc(I.dma[0].h, 16)
            elif I.signal:
                sig[I.eng] += 1
                I.sigidx = sig[I.eng]
                bi.then_inc(self.esem[I.eng], 1)
        self.lists = []


class Buf:
    uid = 0

    def __init__(self, P, name, shape, dtype, psum=False):
        nc = P.nc
        self.name = name
        Buf.uid += 1
        tn = "%s_%d" % (name, Buf.uid)
        self.t = P.es.enter_context(nc.psum_tensor(tn, shape, dtype) if psum else nc.sbuf_tensor(tn, shape, dtype))
        self.P = P
        self._ld = None
        self._st = None

    @property
    def ld(self):
        if self._ld is None:
            self._ld = self.P.newsem("l_" + self.name)
        return self._ld

    @property
    def st(self):
        if self._st is None:
            self._st = self.P.newsem("t_" + self.name)
        return self._st


def build(stage=99, ntl=5, exch=True, nlay=NL, segs="q,c,kr,rq,rk,rv,g"):
    segs = set(segs.split(","))
    nc = bass.Bass("TRN2", target_bir_lowering=False)
    es = ExitStack()
    P = Prog(nc, es)
    dt_in = lambda n, s, d=F32: nc.dram_tensor(n, s, d, kind="ExternalInput").ap()
    dt_out = lambda n, s, d=F32: nc.dram_tensor(n, s, d, kind="ExternalOutput").ap()
    dt_scr = lambda n, s, d=BF16: nc.dram_tensor(n, s, d).ap()

    xin = dt_in("xin", [TT, D])
    WSHP = {"ffn1_w13": [D, 2 * DFF], "ffn1_w2": [DFF, D], "w_in": [D, INW],
            "w_uq": [512, 1536], "w_uk": [512, 1024], "w_uv": [512, 1024],
            "w_mla_out": [1024, D], "w_ret_out": [D, D], "w_out": [D, D],
            "ffn2_w13": [D, 2 * DFF], "ffn2_w2": [DFF, D]}

    class _W(dict):
        def __missing__(self, nm):
            self[nm] = dt_in(nm, [nlay] + WSHP[nm])
            return self[nm]
    W = _W()
    norms_d = dt_in("norms", [128, 128])
    mtab_d = dt_in("mtab", [2, 128, TT])
    rtab_d = dt_in("rtab", [8, 4, 128, TT])
    cmat_d = dt_in("cmat", [4, 128, 128], BF16)
    cf32_d = dt_in("cf32", [2, 128, 128])
    flags_d = dt_in("flags", [128, 2])
    cache_c = dt_in("cache_c", [NL, 2, PAST, 512])
    cache_k = dt_in("cache_k", [NL, 2, PAST, 64])
    state_d = dt_in("state", [NL, 2, 8, 128, 256])

    y_o = dt_out("y", [TT, D])
    ckv_o = dt_out("ckv", [NL, TT, 512])
    kr_o = dt_out("krope", [NL, TT, 64])
    ret_o = dt_out("ret", [NL, 3, 8, 128, 256])

    xT = dt_scr("xT", [D, TT], F32)
    qnT = dt_scr("qnT", [1024, TT])
    qrT = dt_scr("qrT", [512, TT])
    kvT = dt_scr("kvT", [640, TT])
    kx = dt_scr("kx", [5, 1024, 256], F32)
    kxG = dt_scr("kxG", [5, 2048, 256], F32)
    rqT = dt_scr("rqT", [1024, TT])
    rkT = dt_scr("rkT", [1024, TT])
    rk = dt_scr("rk", [TT, 1024])
    rv = dt_scr("rv", [TT, 2048])
    rgT = dt_scr("rgT", [D, TT])
    gmT = dt_scr("gmT", [D, TT])
    grT = dt_scr("grT", [D, TT])
    aoT = dt_scr("aoT", [1024, TT])
    rnT = dt_scr("rnT", [D, TT])
    stL = dt_scr("stL", [1024, 256], F32)
    stG = dt_scr("stG", [2048, 256], F32)

    ps = [Buf(P, f"ps{i}", [128, 512], F32, psum=True) for i in range(8)]
    norms = Buf(P, "norms_sb", [128, 128], F32)
    cmat = Buf(P, "cmat_sb", [128, 4, 128], BF16)
    cf32 = Buf(P, "cf32_sb", [128, 2, 128], F32)
    flags = Buf(P, "flags_sb", [128, 2], F32)
    wb = [Buf(P, f"wb{i}", [128, 8192], BF16) for i in range(4)]
    csem = P.newsem("csem")
    for b_ in wb:
        b_.ld
    P.dma("sp", lambda: nc.sync.dma_start(out=norms.t[:], in_=norms_d[:, :]), csem, writes=["norms"])
    P.dma("sp", lambda: nc.sync.dma_start(out=cmat.t[:], in_=cmat_d.rearrange("c p n -> p c n")), csem, writes=["cmat"])
    P.dma("sp", lambda: nc.sync.dma_start(out=cf32.t[:], in_=cf32_d.rearrange("c p n -> p c n")), csem, writes=["cf32"])
    P.dma("sp", lambda: nc.sync.dma_start(out=flags.t[:], in_=flags_d[:, :]), csem, writes=["flags"])
    ones = cmat.t[:, 0, :]
    RmT = cmat.t[:, 1, :]
    RrT = cmat.t[:, 2, :]
    ident = cf32.t[:, 0, :]
    cmask = cf32.t[:, 1, :]
    wrr = [0]

    def MM(out, lhsT, rhs, start, stop, reads, writes):
        return P.op("pe", lambda: nc.tensor.matmul(out, lhsT=lhsT, rhs=rhs, start=start, stop=stop), reads, writes)

    def TR(out, in_, idn, reads, writes):
        return P.op("pe", lambda: nc.tensor.transpose(out=out, in_=in_, identity=idn), reads, writes)

    def ACT(out, in_, func, reads, writes, scale=1.0, bias=None):
        if bias is None:
            return P.op("act", lambda: nc.scalar.activation(out=out, in_=in_, func=func, scale=scale), reads, writes)
        return P.op("act", lambda: nc.scalar.activation(out=out, in_=in_, func=func, scale=scale, bias=bias), reads, writes)

    def TT_(out, a, b, op, reads, writes, eng="dve"):
        e = nc.vector if eng == "dve" else nc.gpsimd
        return P.op(eng, lambda: e.tensor_tensor(out=out, in0=a, in1=b, op=op), reads, writes)

    def TS(out, a, s1, s2, op0, op1, reads, writes):
        if op1 is None:
            return P.op("dve", lambda: nc.vector.tensor_scalar(out=out, in0=a, scalar1=s1, scalar2=None, op0=op0), reads, writes)
        return P.op("dve", lambda: nc.vector.tensor_scalar(out=out, in0=a, scalar1=s1, scalar2=s2, op0=op0, op1=op1), reads, writes)

    def STT(out, a, s, b, op0, op1, reads, writes):
        return P.op("dve", lambda: nc.vector.scalar_tensor_tensor(out=out, in0=a, scalar=s, in1=b, op0=op0, op1=op1), reads, writes)

    def LD(dst_ap, src_ap, buf, key, reads=()):
        return P.dma("sp", lambda: nc.sync.dma_start(out=dst_ap, in_=src_ap), buf.ld, reads=list(reads), writes=[key])

    def ST(dst_ap, src_ap, buf, key, writes=()):
        return P.dma("sp", lambda: nc.sync.dma_start(out=dst_ap, in_=src_ap), buf.st, reads=[key], writes=list(writes))

    def WLD(src_ap, kc, nb):
        i = wrr[0] % len(wb)
        wrr[0] += 1
        b = wb[i]
        view = b.t[:, 0:kc * nb].rearrange("p (k n) -> p k n", k=kc)
        P.dma("pool", lambda: nc.gpsimd.dma_start(out=view, in_=src_ap), b.ld, writes=[b.name])
        return view, b.name

    def wview(w_l, c0, nb):
        return w_l.rearrange("(k p) n -> p k n", p=128)[:, :, c0:c0 + nb]

    def rmsnorm_fm(src, skey, kcn, N, gcol, dst, dkey, sq, sqkey, pbank, rstd, dim):
        ACT(sq.t[:, 0:kcn, 0:N], src[:, 0:kcn, 0:N], AF.Square, [skey], [sqkey])
        for k in range(kcn):
            MM(pbank.t[:, 0:N], ones, sq.t[:, k, 0:N], k == 0, k == kcn - 1, [sqkey, "cmat"], [pbank.name])
        ACT(rstd.t[:, 0:N], pbank.t[:, 0:N], AF.Sqrt, [pbank.name], [rstd.name], scale=1.0 / dim, bias=1e-6)
        P.op("dve", lambda: nc.vector.reciprocal(out=rstd.t[:, 0:N], in_=rstd.t[:, 0:N]), [rstd.name], [rstd.name])
        for k in range(kcn):
            STT(dst[:, k, 0:N], src[:, k, 0:N], norms.t[:, gcol + k:gcol + k + 1], rstd.t[:, 0:N], ALU.mult, ALU.mult,
                [skey, rstd.name, "norms"], [dkey])

    tiles = TILES[:ntl] if ntl < 5 else TILES
    if ntl < 5 and ntl > 0 and False:
        pass

    def blocks(N):
        return [(b * 128, min(128, N - b * 128)) for b in range((N + 127) // 128)]

    with ExitStack() as ph:
        P.begin_phase(ph)
        xtm = [Buf(P, f"in_xtm{i}", [128, D], F32) for i in range(2)]
        xfm = Buf(P, "in_xfm", [128, KC, 512], F32)
        for (t0, N) in tiles:
            for bi, (b0, nt) in enumerate(blocks(N)):
                xb = xtm[bi % 2]
                LD(xb.t[0:nt, :], xin[t0 + b0:t0 + b0 + nt, :], xb, xb.name)
                for g in range(4):
                    pb = ps[g % 2]
                    for kk in range(4):
                        k = 4 * g + kk
                        TR(pb.t[:, kk * 128:kk * 128 + nt], xb.t[0:nt, k * 128:(k + 1) * 128], ident[0:nt, 0:nt],
                           [xb.name, "cf32"], [pb.name])
                    ACT(xfm.t[:, 4 * g:4 * g + 4, b0:b0 + nt],
                        pb.t[:, :].rearrange("p (k n) -> p k n", k=4)[:, :, 0:nt], AF.Copy, [pb.name], ["xfm"])
            ST(xT.rearrange("(k p) t -> p k t", p=128)[:, :, t0:t0 + N], xfm.t[:, :, 0:N], xfm, "xfm", writes=["xT"])
        P.end_phase()

    def ffn_phase(l, w13, w2, gcol):
        with ExitStack() as ph:
            P.begin_phase(ph)
            xs = [Buf(P, f"f_x{i}", [128, KC, 512], F32) for i in range(2)]
            xn = Buf(P, "f_xn", [128, KC, 512], BF16)
            g = Buf(P, "f_g", [128, FC, 512], BF16)
            rstd = Buf(P, "f_rstd", [128, 512], F32)
            sa = [Buf(P, f"f_sa{i}", [128, 512], F32) for i in range(2)]
            xT_v = xT.rearrange("(k p) t -> p k t", p=128)

            def load(ti):
                t0, N = tiles[ti]
                b = xs[ti % 2]
                LD(b.t[:, :, 0:N], xT_v[:, :, t0:t0 + N], b, b.name, reads=["xT"])
            load(0)
            for ti, (t0, N) in enumerate(tiles):
                x = xs[ti % 2]
                if ti + 1 < len(tiles):
                    load(ti + 1)
                rmsnorm_fm(x.t, x.name, KC, N, gcol, xn.t, "f_xn", g, "f_g", ps[6], rstd, D)
                for jb in range(11):
                    wa, ka = WLD(wview(w13, 512 * jb, 512), KC, 512)
                    wbb, kb = WLD(wview(w13, DFF + 512 * jb, 512), KC, 512)
                    for jj in range(4):
                        j = 4 * jb + jj
                        pa, pb = ps[j % 2], ps[2 + j % 2]
                        for k in range(KC):
                            MM(pa.t[:, 0:N], wa[:, k, jj * 128:(jj + 1) * 128], xn.t[:, k, 0:N], k == 0, k == KC - 1,
                               [ka, "f_xn"], [pa.name])
                        for k in range(KC):
                            MM(pb.t[:, 0:N], wbb[:, k, jj * 128:(jj + 1) * 128], xn.t[:, k, 0:N], k == 0, k == KC - 1,
                               [kb, "f_xn"], [pb.name])
                        s = sa[j % 2]
                        ACT(s.t[:, 0:N], pa.t[:, 0:N], AF.Silu, [pa.name], [s.name])
                        TT_(g.t[:, j, 0:N], s.t[:, 0:N], pb.t[:, 0:N], ALU.mult, [s.name, pb.name], ["f_g"])
                for n in range(KC):
                    w2v, k2 = WLD(wview(w2, 128 * n, 128), FC, 128)
                    pc = ps[4 + n % 2]
                    for k in range(FC):
                        MM(pc.t[:, 0:N], w2v[:, k, :], g.t[:, k, 0:N], k == 0, k == FC - 1, [k2, "f_g"], [pc.name])
                    STT(x.t[:, n, 0:N], pc.t[:, 0:N], 0.5, x.t[:, n, 0:N], ALU.mult, ALU.add, [pc.name, x.name], [x.name])
                ST(xT_v[:, :, t0:t0 + N], x.t[:, :, 0:N], x, x.name, writes=["xT"])
            P.end_phase()

    def proj_phase(l):
        win = W["w_in"][l]
        nb = 56 * l
        with ExitStack() as ph:
            P.begin_phase(ph)
            xs = [Buf(P, f"p_x{i}", [128, KC, 512], F32) for i in range(1)]
            u = Buf(P, "p_u", [128, KC, 512], BF16)
            sq = Buf(P, "p_sq", [128, KC, 512], BF16)
            rstd = Buf(P, "p_rstd", [128, 512], F32)
            lat = Buf(P, "p_lat", [128, 4, 512], F32)
            latn = lat
            qn = Buf(P, "p_qn", [128, 4, 512], BF16)
            cb16 = Buf(P, "p_cb16", [128, 4, 512], BF16)
            wq = Buf(P, "p_wq", [128, 4, 1536], BF16)
            wkr = Buf(P, "p_wkr", [128, KC, 128], BF16)
            mt = [Buf(P, f"p_mt{i}", [128, 2, 512], F32) for i in range(1)]
            rt = [Buf(P, f"p_rt{i}", [128, 4, 512], F32) for i in range(1)]
            xb = [Buf(P, f"p_xb{i}", [128, 512], BF16) for i in range(2)]
            t1 = [Buf(P, f"p_t1{i}", [128, 512], F32) for i in range(1)]
            t2 = [Buf(P, f"p_t2{i}", [128, 512], F32) for i in range(1)]
            kf = [Buf(P, f"p_kf{i}", [128, 512], F32) for i in range(1)]
            so = [Buf(P, f"p_so{i}", [128, 512], BF16) for i in range(4)]
            sf = [Buf(P, f"p_sf{i}", [128, 512], F32) for i in range(2)]
            sorr = [0]
            xT_v = xT.rearrange("(k p) t -> p k t", p=128)
            wuq = W["w_uq"][l].rearrange("(k p) (h c) -> p k h c", p=128, c=192)
            for k in range(4):
                P.dma("pool", lambda k=k: nc.gpsimd.dma_start(
                    out=wq.t[:, k, 0:1024].rearrange("p (h c) -> p h c", c=128), in_=wuq[:, k, :, 0:128]), wq.ld, writes=["p_wq"])
                P.dma("pool", lambda k=k: nc.gpsimd.dma_start(
                    out=wq.t[:, k, 1024:1536].rearrange("p (h c) -> p h c", c=64), in_=wuq[:, k, :, 128:192]), wq.ld, writes=["p_wq"])
            for hh in range(2):
                P.dma("pool", lambda hh=hh: nc.gpsimd.dma_start(
                    out=wkr.t[:, :, hh * 64:(hh + 1) * 64], in_=wview(win, O_KR, 64)), wkr.ld, writes=["p_wkr"])

            def load(ti):
                t0, N = tiles[ti]
                b = xs[0]
                LD(b.t[:, :, 0:N], xT_v[:, :, t0:t0 + N], b, b.name, reads=["xT"])

            def stage_store(dst_ap, src_psum, pkey, N, func=AF.Copy, scale=1.0, extra_reads=()):
                s = so[sorr[0] % 4]
                sorr[0] += 1
                ACT(s.t[:, 0:N], src_psum, func, [pkey] + list(extra_reads), [s.name], scale=scale)
                ST(dst_ap, s.t[:, 0:N], s, s.name)

            def rope_fm(pb, N, RT, cosap, sinap, tabkey, i, outbuf):
                x16 = xb[i % 2]
                ACT(x16.t[:, 0:N], pb.t[:, 0:N], AF.Copy, [pb.name], [x16.name])
                pr = ps[6 + i % 2]
                MM(pr.t[:, 0:N], RT, x16.t[:, 0:N], True, True, [x16.name, "cmat"], [pr.name])
                a = t1[0]
                b = t2[0]
                TT_(a.t[:, 0:N], pb.t[:, 0:N], cosap, ALU.mult, [pb.name, tabkey], [a.name])
                TT_(b.t[:, 0:N], pr.t[:, 0:N], sinap, ALU.mult, [pr.name, tabkey], [b.name])
                TT_(outbuf.t[:, 0:N], a.t[:, 0:N], b.t[:, 0:N], ALU.add, [a.name, b.name], [outbuf.name])

            load(0)
            for ti, (t0, N) in enumerate(tiles):
                x = xs[0]
                m = mt[0]
                LD(m.t[:, :, 0:N], mtab_d.rearrange("c p t -> p c t")[:, :, t0:t0 + N], m, m.name)
                rmsnorm_fm(x.t, x.name, KC, N, nb + 16, u.t, "p_u", sq, "p_sq", ps[5], rstd, D)
                if ti + 1 < len(tiles):
                    load(ti + 1)
                for seg, (c0, gc) in enumerate([(O_QL, nb + 48), (O_CKV, nb + 52)]):
                    if ("q" if seg == 0 else "c") not in segs:
                        continue
                    wv, wk = WLD(wview(win, c0, 512), KC, 512)
                    for j in range(4):
                        pb = ps[j % 2]
                        for k in range(KC):
                            MM(pb.t[:, 0:N], wv[:, k, j * 128:(j + 1) * 128], u.t[:, k, 0:N], k == 0, k == KC - 1, [wk, "p_u"], [pb.name])
                        ACT(lat.t[:, j, 0:N], pb.t[:, 0:N], AF.Copy, [pb.name], ["p_lat"])
                    rmsnorm_fm(lat.t, "p_lat", 4, N, gc, latn.t, "p_lat", sq, "p_sq", ps[5], rstd, 512)
                    if seg == 0:
                        ACT(qn.t[:, :, 0:N], latn.t[:, :, 0:N], AF.Copy, ["p_lat"], ["p_qn"])
                        for h in range(8):
                            pb = ps[h % 2]
                            for k in range(4):
                                MM(pb.t[:, 0:N], wq.t[:, k, 128 * h:128 * h + 128], qn.t[:, k, 0:N], k == 0, k == 3, ["p_wq", "p_qn"], [pb.name])
                            stage_store(qnT[128 * h:128 * h + 128, t0:t0 + N], pb.t[:, 0:N], pb.name, N)
                        for mm_ in range(4):
                            pb = ps[2 + mm_ % 2]
                            for k in range(4):
                                MM(pb.t[:, 0:N], wq.t[:, k, 1024 + 128 * mm_:1024 + 128 * mm_ + 128], qn.t[:, k, 0:N], k == 0, k == 3,
                                   ["p_wq", "p_qn"], [pb.name])
                            o = kf[0]
                            rope_fm(pb, N, RmT, m.t[:, 0, 0:N], m.t[:, 1, 0:N], m.name, mm_, o)
                            s = so[sorr[0] % 4]
                            sorr[0] += 1
                            ACT(s.t[:, 0:N], o.t[:, 0:N], AF.Copy, [o.name], [s.name])
                            ST(qrT[128 * mm_:128 * mm_ + 128, t0:t0 + N], s.t[:, 0:N], s, s.name)
                    else:
                        ACT(cb16.t[:, :, 0:N], latn.t[:, :, 0:N], AF.Copy, ["p_lat"], ["p_cb16"])
                        ST(kvT[0:512, :].rearrange("(k p) t -> p k t", p=128)[:, :, t0:t0 + N], cb16.t[:, :, 0:N], cb16, "p_cb16", writes=["kvT"])
                        if t0 < TP:
                            for k in range(4):
                                ST(kx[k].rearrange("(r q) c -> r (q c)", q=8)[:, t0:t0 + N], latn.t[:, k, 0:N], lat, "p_lat")
                        for bi, (b0, nt) in enumerate(blocks(N)):
                            pb = ps[2 + bi % 2]
                            for k in range(4):
                                TR(pb.t[0:nt, k * 128:(k + 1) * 128], latn.t[:, k, b0:b0 + nt], ident, ["p_lat", "cf32"], [pb.name])
                            s = sf[bi % 2]
                            ACT(s.t[0:nt, :], pb.t[0:nt, :], AF.Copy, [pb.name], [s.name])
                            ST(ckv_o[l, t0 + b0:t0 + b0 + nt, :], s.t[0:nt, :], s, s.name)
                pb = ps[0]
                if "kr" not in segs:
                    continue
                for k in range(KC):
                    MM(pb.t[:, 0:N], wkr.t[:, k, :], u.t[:, k, 0:N], k == 0, k == KC - 1, ["p_wkr", "p_u"], [pb.name])
                o = kf[0]
                rope_fm(pb, N, RmT, m.t[:, 0, 0:N], m.t[:, 1, 0:N], m.name, 0, o)
                s = so[sorr[0] % 4]
                sorr[0] += 1
                ACT(s.t[:, 0:N], o.t[:, 0:N], AF.Copy, [o.name], [s.name])
                ST(kvT[512:640, t0:t0 + N], s.t[:, 0:N], s, s.name, writes=["kvT"])
                if t0 < TP:
                    ST(kx[4].rearrange("(r q) c -> r (q c)", q=8)[:, t0:t0 + N], o.t[:, 0:N], o, o.name)
                for bi, (b0, nt) in enumerate(blocks(N)):
                    pb2 = ps[2 + bi % 2]
                    TR(pb2.t[0:nt, 0:128], o.t[:, b0:b0 + nt], ident, [o.name, "cf32"], [pb2.name])
                    s2 = sf[bi % 2]
                    ACT(s2.t[0:nt, 0:64], pb2.t[0:nt, 0:64], AF.Copy, [pb2.name], [s2.name])
                    ST(kr_o[l, t0 + b0:t0 + b0 + nt, :], s2.t[0:nt, 0:64], s2, s2.name)
                for which, c0 in enumerate([O_RQ, O_RK]):
                    if ("rq", "rk")[which] not in segs:
                        continue
                    for hb in range(2):
                        wv, wk = WLD(wview(win, c0 + 512 * hb, 512), KC, 512)
                        for jj in range(4):
                            h = 4 * hb + jj
                            r = rt[0]
                            LD(r.t[:, :, 0:N], rtab_d[h].rearrange("f p t -> p f t")[:, :, t0:t0 + N], r, r.name)
                            pb = ps[h % 2]
                            for k in range(KC):
                                MM(pb.t[:, 0:N], wv[:, k, jj * 128:(jj + 1) * 128], u.t[:, k, 0:N], k == 0, k == KC - 1, [wk, "p_u"], [pb.name])
                            o = kf[0]
                            rope_fm(pb, N, RrT, r.t[:, 2 * which, 0:N], r.t[:, 2 * which + 1, 0:N], r.name, h, o)
                            s = so[sorr[0] % 4]
                            sorr[0] += 1
                            ACT(s.t[:, 0:N], o.t[:, 0:N], AF.Copy, [o.name], [s.name])
                            dstT = rqT if which == 0 else rkT
                            ST(dstT[128 * h:128 * h + 128, t0:t0 + N], s.t[:, 0:N], s, s.name)
                            if which == 1:
                                for bi, (b0, nt) in enumerate(blocks(N)):
                                    pb2 = ps[2 + bi % 2]
                                    TR(pb2.t[0:nt, 0:128], o.t[:, b0:b0 + nt], ident, [o.name, "cf32"], [pb2.name])
                                    s2 = so[sorr[0] % 4]
                                    sorr[0] += 1
                                    cd_ = (1.0 - 2.0 ** (-5.0 - h)) ** (128 if N == 512 else 16)
                                    ACT(s2.t[0:nt, 0:128], pb2.t[0:nt, 0:128], AF.Copy, [pb2.name], [s2.name], scale=float(cd_))
                                    ST(rk[t0 + b0:t0 + b0 + nt, 128 * h:128 * h + 128], s2.t[0:nt, 0:128], s2, s2.name)
                for cb in range(4):
                    if "rv" not in segs:
                        continue
                    wv, wk = WLD(wview(win, O_RV + 512 * cb, 512), KC, 512)
                    for bi, (b0, nt) in enumerate(blocks(N)):
                        pb = ps[bi % 2]
                        for k in range(KC):
                            MM(pb.t[0:nt, :], u.t[:, k, b0:b0 + nt], wv[:, k, :], k == 0, k == KC - 1, [wk, "p_u"], [pb.name])
                        s = so[sorr[0] % 4]
                        sorr[0] += 1
                        ACT(s.t[0:nt, :], pb.t[0:nt, :], AF.Copy, [pb.name], [s.name])
                        ST(rv[t0 + b0:t0 + b0 + nt, 512 * cb:512 * cb + 512], s.t[0:nt, :], s, s.name)
                for c0, dstT, fn in [(O_GM, gmT, AF.Sigmoid), (O_GR, grT, AF.Sigmoid), (O_RG, rgT, AF.Silu)]:
                    if "g" not in segs:
                        continue
                    for cb in range(4):
                        wv, wk = WLD(wview(win, c0 + 512 * cb, 512), KC, 512)
                        for jj in range(4):
                            j = 4 * cb + jj
                            pb = ps[j % 2]
                            for k in range(KC):
                                MM(pb.t[:, 0:N], wv[:, k, jj * 128:(jj + 1) * 128], u.t[:, k, 0:N], k == 0, k == KC - 1, [wk, "p_u"], [pb.name])
                            stage_store(dstT[128 * j:128 * j + 128, t0:t0 + N], pb.t[:, 0:N], pb.name, N, func=fn)
            P.end_phase()


    GROUPS = [[0, 1], [2, 3], [4, 5], [6, 7]]
    GAM = [1.0 - 2.0 ** (-5.0 - h) for h in range(8)]

    def exch_phase(pairs, nm):
        with ExitStack() as ph:
            P.begin_phase(ph)
            for (src, dst) in pairs:
                P.op("pool", lambda src=src, dst=dst: nc.gpsimd.collective_compute(
                    "AllGather", ALU.bypass, replica_groups=GROUPS, ins=[src], outs=[dst]), [nm], [nm + "G"])
            P.end_phase()

    def att_phase(l):
        wuk, wuv = W["w_uk"][l], W["w_uv"][l]
        with ExitStack() as ph:
            P.begin_phase(ph)
            cT = Buf(P, "a_cT", [128, 4, 4096], BF16)
            krT_ = Buf(P, "a_krT", [128, 4096], BF16)
            KT = [Buf(P, f"a_KT{i}", [128, 4096], BF16) for i in range(2)]
            Vh = [Buf(P, f"a_V{i}", [128, 32, 128], BF16) for i in range(2)]
            qn_ = [Buf(P, f"a_qn{i}", [128, 512], BF16) for i in range(2)]
            qr_ = [Buf(P, f"a_qr{i}", [128, 512], BF16) for i in range(2)]
            pt = [Buf(P, f"a_pt{i}", [128, 512], BF16) for i in range(3)]
            rden = [Buf(P, f"a_rd{i}", [128, 512], F32) for i in range(2)]
            aob = [Buf(P, f"a_ao{i}", [128, 512], BF16) for i in range(2)]
            cc = [Buf(P, f"a_cc{i}", [128, 512], F32) for i in range(2)]
            ck = [Buf(P, f"a_ck{i}", [128, 128], F32) for i in range(2)]
            ctr = [0, 0, 0]

            def build_kv(h, nkeys):
                kt, vh = KT[h % 2], Vh[h % 2]
                wk_, kk_ = WLD(wview(wuk, 128 * h, 128), 4, 128)
                wv_, kv_ = WLD(wview(wuv, 128 * h, 128), 4, 128)
                c0 = 0
                while c0 < nkeys:
                    n = min(512, nkeys - c0)
                    pb = ps[6 + ctr[0] % 2]
                    ctr[0] += 1
                    for k in range(4):
                        MM(pb.t[:, 0:n], wk_[:, k, :], cT.t[:, k, c0:c0 + n], k == 0, k == 3, [kk_, "a_cT"], [pb.name])
                    ACT(kt.t[:, c0:c0 + n], pb.t[:, 0:n], AF.Copy, [pb.name], [kt.name])
                    c0 += n
                nblk = (nkeys + 127) // 128
                for g0 in range(0, nblk, 4):
                    pb = ps[6 + ctr[0] % 2]
                    ctr[0] += 1
                    gl = min(4, nblk - g0)
                    for gi in range(gl):
                        kb = g0 + gi
                        nk = min(128, nkeys - kb * 128)
                        for k in range(4):
                            MM(pb.t[0:nk, gi * 128:(gi + 1) * 128], cT.t[:, k, kb * 128:kb * 128 + nk], wv_[:, k, :], k == 0, k == 3,
                               [kv_, "a_cT"], [pb.name])
                    full = [gi for gi in range(gl) if min(128, nkeys - (g0 + gi) * 128) == 128]
                    if full:
                        nf = len(full)
                        ACT(vh.t[:, g0:g0 + nf, :], pb.t[:, 0:nf * 128].rearrange("p (g d) -> p g d", g=nf), AF.Copy, [pb.name], [vh.name])
                    if len(full) < gl:
                        gi = gl - 1
                        nk = nkeys - (g0 + gi) * 128
                        ACT(vh.t[0:nk, g0 + gi, :], pb.t[0:nk, gi * 128:(gi + 1) * 128], AF.Copy, [pb.name], [vh.name])

            def attend(h, q0, NQ, nkeys, nctx, causal_qt):
                kt, vh = KT[h % 2], Vh[h % 2]
                qn, qr = qn_[ctr[1] % 2], qr_[ctr[1] % 2]
                po_, pd_ = ps[2 + ctr[1] % 2], ps[4 + ctr[1] % 2]
                rd, ao = rden[ctr[1] % 2], aob[ctr[1] % 2]
                ctr[1] += 1
                LD(qn.t[:, 0:NQ], qnT[128 * h:128 * h + 128, q0:q0 + NQ], qn, qn.name)
                LD(qr.t[:, 0:NQ], qrT[128 * (h // 2):128 * (h // 2) + 128, q0:q0 + NQ], qr, qr.name)
                p0 = 64 * (h % 2)
                if causal_qt is None:
                    nvis = (nkeys + 127) // 128
                else:
                    nvis = nctx // 128 + 4 * causal_qt + 4
                for kb in range(nvis):
                    nk = min(128, nkeys - kb * 128)
                    off = 0
                    isctx = kb * 128 < nctx
                    if causal_qt is not None and not isctx:
                        j = kb - nctx // 128
                        if j >= 4 * causal_qt:
                            off = 128 * (j - 4 * causal_qt)
                    pS = ps[ctr[2] % 2]
                    p_ = pt[ctr[2] % 3]
                    ctr[2] += 1
                    MM(pS.t[0:nk, off:NQ], kt.t[:, kb * 128:kb * 128 + nk], qn.t[:, off:NQ], True, False, [kt.name, qn.name], [pS.name])
                    MM(pS.t[0:nk, off:NQ], krT_.t[p0:p0 + 64, kb * 128:kb * 128 + nk], qr.t[p0:p0 + 64, off:NQ], False, True,
                       ["a_krT", qr.name], [pS.name])
                    if isctx and causal_qt is not None:
                        ACT(p_.t[0:nk, off:NQ], pS.t[0:nk, off:NQ], AF.Exp, [pS.name, "flags"], [p_.name], scale=MLA_SCALE, bias=flags.t[0:nk, 0:1])
                    else:
                        ACT(p_.t[0:nk, off:NQ], pS.t[0:nk, off:NQ], AF.Exp, [pS.name], [p_.name], scale=MLA_SCALE)
                    if causal_qt is not None and not isctx and (kb - nctx // 128) >= 4 * causal_qt:
                        P.op("dve", lambda p_=p_, off=off: nc.vector.memset(p_.t[64:128, off:off + 64], 0.0), [], [p_.name])
                    first, last = kb == 0, kb == nvis - 1
                    MM(po_.t[:, off:NQ], vh.t[0:nk, kb, :], p_.t[0:nk, off:NQ], first, last, [vh.name, p_.name], [po_.name])
                    MM(pd_.t[:, off:NQ], ones[0:nk, :], p_.t[0:nk, off:NQ], first, last, ["cmat", p_.name], [pd_.name])
                P.op("dve", lambda: nc.vector.reciprocal(out=rd.t[:, 0:NQ], in_=pd_.t[:, 0:NQ]), [pd_.name], [rd.name])
                TT_(ao.t[:, 0:NQ], po_.t[:, 0:NQ], rd.t[:, 0:NQ], ALU.mult, [po_.name, rd.name], [ao.name])
                ST(aoT[128 * h:128 * h + 128, q0:q0 + NQ], ao.t[:, 0:NQ], ao, ao.name)

            if ntl >= 4:
                for k in range(4):
                    P.dma("pool", lambda k=k: nc.gpsimd.dma_start(
                        out=cT.t[:, k, 0:2048], in_=kxG[k][0:1024, :].rearrange("(r q) c -> r (q c)", q=8)), cT.ld, writes=["a_cT"])
                LD(cT.t[:, :, 2048:4096], kvT[0:512, :].rearrange("(k p) t -> p k t", p=128)[:, :, 0:2048], cT, "a_cT")
                P.dma("pool", lambda: nc.gpsimd.dma_start(
                    out=krT_.t[:, 0:2048], in_=kxG[4][0:1024, :].rearrange("(r q) c -> r (q c)", q=8)), krT_.ld, writes=["a_krT"])
                LD(krT_.t[:, 2048:4096], kvT[512:640, 0:2048], krT_, "a_krT")
                for h in range(8):
                    build_kv(h, 4096)
                    for qt in range(4):
                        attend(h, 512 * qt, 512, 4096, 2048, qt)
            if ntl >= 5:
                for sidx in range(2):
                    for kb in range(16):
                        c_ = cc[kb % 2]
                        LD(c_.t[:, :], cache_c[l, sidx, kb * 128:(kb + 1) * 128, :], c_, c_.name)
                        pb = ps[6 + kb % 2]
                        for k in range(4):
                            TR(pb.t[:, k * 128:(k + 1) * 128], c_.t[:, k * 128:(k + 1) * 128], ident, [c_.name, "cf32"], [pb.name])
                        ACT(cT.t[:, 0:4, kb * 128:(kb + 1) * 128], pb.t[:, :].rearrange("p (k n) -> p k n", k=4), AF.Copy, [pb.name], ["a_cT"])
                        k_ = ck[kb % 2]
                        LD(k_.t[:, 0:64], cache_k[l, sidx, kb * 128:(kb + 1) * 128, :], k_, k_.name)
                        LD(k_.t[:, 64:128], cache_k[l, sidx, kb * 128:(kb + 1) * 128, :], k_, k_.name)
                        pb2 = ps[4 + kb % 2]
                        TR(pb2.t[:, 0:128], k_.t[:, :], ident, [k_.name, "cf32"], [pb2.name])
                        ACT(krT_.t[:, kb * 128:(kb + 1) * 128], pb2.t[:, 0:128], AF.Copy, [pb2.name], ["a_krT"])
                    q0 = TP + 16 * sidx
                    LD(cT.t[:, :, 2048:2064], kvT[0:512, :].rearrange("(k p) t -> p k t", p=128)[:, :, q0:q0 + 16], cT, "a_cT")
                    LD(krT_.t[:, 2048:2064], kvT[512:640, q0:q0 + 16], krT_, "a_krT")
                    for h in range(8):
                        build_kv(h, 2064)
                        attend(h, q0, 16, 2064, 2048, None)
            P.end_phase()

    def ret_phase(l, full):
        with ExitStack() as ph:
            P.begin_phase(ph)
            kt_ = [Buf(P, f"r_kt{i}", [128, 16, 128], BF16) for i in range(2)]
            vt_ = [Buf(P, f"r_vt{i}", [128, 16, 256], BF16) for i in range(2)]
            S = [Buf(P, f"r_S{i}", [128, 256], F32) for i in range(2)]
            if full:
                qT_ = [Buf(P, f"r_qT{i}", [128, 2048], BF16) for i in range(2)]
                kT_ = [Buf(P, f"r_kT{i}", [128, 2048], BF16) for i in range(2)]
                S16 = [Buf(P, f"r_S16{i}", [128, 256], BF16) for i in range(2)]
                at = [Buf(P, f"r_at{i}", [128, 128], BF16) for i in range(2)]
                o32 = Buf(P, "r_o32", [128, 2, 512], F32)
                o16 = Buf(P, "r_o16", [128, 2, 512], BF16)
                q16 = Buf(P, "r_q16", [128, 2, 512], BF16)
                mean = Buf(P, "r_mean", [128, 512], F32)
                var = Buf(P, "r_var", [128, 512], F32)
                rg_ = [Buf(P, f"r_rg{i}", [128, 2, 512], BF16) for i in range(2)]
                rn_ = [Buf(P, f"r_rn{i}", [128, 2, 512], BF16) for i in range(2)]
            cnt = [0]

            def scan(h, t0, nblk, nt, cdec, s0_src, s0_flag, out_dst):
                i = cnt[0] % 2
                cnt[0] += 1
                kt, vt, S_ = kt_[i], vt_[i], S[i]
                LD(kt.t[0:nt, 0:nblk, :], rk[t0:t0 + nblk * nt, 128 * h:128 * h + 128].rearrange("(n p) d -> p n d", p=nt), kt, kt.name)
                LD(vt.t[0:nt, 0:nblk, :], rv[t0:t0 + nblk * nt, 256 * h:256 * h + 256].rearrange("(n p) d -> p n d", p=nt), vt, vt.name)
                if s0_src is None:
                    P.op("dve", lambda: nc.vector.memset(S_.t[:, :], 0.0), [], [S_.name])
                else:
                    LD(S_.t[:, :], s0_src, S_, S_.name)
                    if s0_flag:
                        TS(S_.t[:, :], S_.t[:, :], flags.t[:, 1:2], None, ALU.mult, None, [S_.name, "flags"], [S_.name])
                if full:
                    qT, kT, s16 = qT_[i], kT_[i], S16[i]
                    W_ = nblk * nt
                    LD(qT.t[:, 0:W_], rqT[128 * h:128 * h + 128, t0:t0 + W_], qT, qT.name)
                    LD(kT.t[:, 0:W_], rkT[128 * h:128 * h + 128, t0:t0 + W_], kT, kT.name)
                    ACT(s16.t[:, :], S_.t[:, :], AF.Copy, [S_.name], [s16.name])
                for n in range(nblk):
                    if full:
                        c0 = n * nt
                        g4 = n % 4
                        pa = ps[n % 2]
                        MM(pa.t[0:nt, 0:nt], kT.t[:, c0:c0 + nt], qT.t[:, c0:c0 + nt], True, True, [kT.name, qT.name], [pa.name])
                        a_ = at[n % 2]
                        TT_(a_.t[0:nt, 0:nt], pa.t[0:nt, 0:nt], cmask[0:nt, 0:nt], ALU.mult, [pa.name, "cf32"], [a_.name])
                        for e in range(2):
                            po_ = ps[2 + e]
                            MM(po_.t[:, g4 * 128:g4 * 128 + nt], vt.t[0:nt, n, e * 128:(e + 1) * 128], a_.t[0:nt, 0:nt], True, False,
                               [vt.name, a_.name], [po_.name])
                            MM(po_.t[:, g4 * 128:g4 * 128 + nt], s16.t[:, e * 128:(e + 1) * 128], qT.t[:, c0:c0 + nt], False, True,
                               [s16.name, qT.name], [po_.name])
                    pst = ps[4 + n % 2]
                    MM(pst.t[:, 0:256], kt.t[0:nt, n, :], vt.t[0:nt, n, :], True, True, [kt.name, vt.name], [pst.name])
                    STT(S_.t[:, :], S_.t[:, :], float(cdec), pst.t[:, 0:256], ALU.mult, ALU.add, [S_.name, pst.name], [S_.name])
                    if full:
                        ACT(s16.t[:, :], S_.t[:, :], AF.Copy, [S_.name], [s16.name])
                        if n % 4 == 3 or n == nblk - 1:
                            nb0 = n - (n % 4)
                            Wd = (n % 4) * 128 + nt
                            cg = t0 + nb0 * nt
                            rg = rg_[(n // 4) % 2]
                            rn = rn_[(n // 4) % 2]
                            LD(rg.t[:, :, 0:Wd], rgT[256 * h:256 * h + 256, :].rearrange("(e p) t -> p e t", p=128)[:, :, cg:cg + Wd], rg, rg.name)
                            for e in range(2):
                                ACT(o32.t[:, e, 0:Wd], ps[2 + e].t[:, 0:Wd], AF.Copy, [ps[2 + e].name], ["r_o32"])
                            ACT(o16.t[:, :, 0:Wd], o32.t[:, :, 0:Wd], AF.Copy, ["r_o32"], ["r_o16"])
                            ACT(q16.t[:, :, 0:Wd], o32.t[:, :, 0:Wd], AF.Square, ["r_o32"], ["r_q16"])
                            pm, pq = ps[6], ps[7]
                            for e in range(2):
                                MM(pm.t[:, 0:Wd], ones, o16.t[:, e, 0:Wd], e == 0, e == 1, ["cmat", "r_o16"], [pm.name])
                            for e in range(2):
                                MM(pq.t[:, 0:Wd], ones, q16.t[:, e, 0:Wd], e == 0, e == 1, ["cmat", "r_q16"], [pq.name])
                            TS(mean.t[:, 0:Wd], pm.t[:, 0:Wd], 1.0 / 256, None, ALU.mult, None, [pm.name], ["r_mean"])
                            TT_(var.t[:, 0:Wd], mean.t[:, 0:Wd], mean.t[:, 0:Wd], ALU.mult, ["r_mean"], ["r_var"])
                            STT(var.t[:, 0:Wd], pq.t[:, 0:Wd], 1.0 / 256, var.t[:, 0:Wd], ALU.mult, ALU.subtract, [pq.name, "r_var"], ["r_var"])
                            ACT(var.t[:, 0:Wd], var.t[:, 0:Wd], AF.Sqrt, ["r_var"], ["r_var"], scale=1.0, bias=1e-5)
                            P.op("dve", lambda Wd=Wd: nc.vector.reciprocal(out=var.t[:, 0:Wd], in_=var.t[:, 0:Wd]), ["r_var"], ["r_var"])
                            for e in range(2):
                                TT_(o32.t[:, e, 0:Wd], o32.t[:, e, 0:Wd], mean.t[:, 0:Wd], ALU.subtract, ["r_o32", "r_mean"], ["r_o32"])
                                TT_(o32.t[:, e, 0:Wd], o32.t[:, e, 0:Wd], var.t[:, 0:Wd], ALU.mult, ["r_o32", "r_var"], ["r_o32"])
                                TT_(rn.t[:, e, 0:Wd], o32.t[:, e, 0:Wd], rg.t[:, e, 0:Wd], ALU.mult, ["r_o32", rg.name], [rn.name])
                            ST(rnT[256 * h:256 * h + 256, :].rearrange("(e p) t -> p e t", p=128)[:, :, cg:cg + Wd], rn.t[:, :, 0:Wd], rn, rn.name)
                ST(out_dst, S_.t[:, :], S_, S_.name)

            for h in range(8):
                g = GAM[h]
                if ntl >= 4:
                    if full:
                        scan(h, 0, 16, 128, g ** 128, stG[128 * h:128 * h + 128, :], True, ret_o[l, 0, h])
                    else:
                        scan(h, 0, 16, 128, g ** 128, None, False, stL[128 * h:128 * h + 128, :])
                if full and ntl >= 5:
                    for sidx in range(2):
                        scan(h, TP + 16 * sidx, 1, 16, g ** 16, state_d[l, sidx, h], False, ret_o[l, 1 + sidx, h])
            P.end_phase()

    def out_phase(l):
        wmo, wro, wo = W["w_mla_out"][l], W["w_ret_out"][l], W["w_out"][l]
        with ExitStack() as ph:
            P.begin_phase(ph)
            x = Buf(P, "o_x", [128, KC, 512], F32)
            ao = Buf(P, "o_ao", [128, 8, 512], BF16)
            rn = Buf(P, "o_rn", [128, KC, 512], BF16)
            gm = Buf(P, "o_gm", [128, KC, 512], BF16)
            gr = Buf(P, "o_gr", [128, KC, 512], BF16)
            m = Buf(P, "o_m", [128, KC, 512], F32)
            m16 = Buf(P, "o_m16", [128, KC, 512], BF16)
            tmp = [Buf(P, f"o_tmp{i}", [128, 512], F32) for i in range(2)]
            fm = lambda T_, k: T_.rearrange("(k p) t -> p k t", p=128)
            for ti, (t0, N) in enumerate(tiles):
                LD(x.t[:, :, 0:N], fm(xT, KC)[:, :, t0:t0 + N], x, "o_x", reads=["xT"])
                LD(ao.t[:, :, 0:N], fm(aoT, 8)[:, :, t0:t0 + N], ao, "o_ao")
                LD(rn.t[:, :, 0:N], fm(rnT, KC)[:, :, t0:t0 + N], rn, "o_rn")
                LD(gm.t[:, :, 0:N], fm(gmT, KC)[:, :, t0:t0 + N], gm, "o_gm")
                LD(gr.t[:, :, 0:N], fm(grT, KC)[:, :, t0:t0 + N], gr, "o_gr")
                for cb in range(4):
                    wv, wk = WLD(wview(wmo, 512 * cb, 512), 8, 512)
                    for jj in range(4):
                        n = 4 * cb + jj
                        pb = ps[n % 2]
                        for k in range(8):
                            MM(pb.t[:, 0:N], wv[:, k, jj * 128:(jj + 1) * 128], ao.t[:, k, 0:N], k == 0, k == 7, [wk, "o_ao"], [pb.name])
                        TT_(m.t[:, n, 0:N], pb.t[:, 0:N], gm.t[:, n, 0:N], ALU.mult, [pb.name, "o_gm"], ["o_m"])
                for cb in range(4):
                    wv, wk = WLD(wview(wro, 512 * cb, 512), KC, 512)
                    for jj in range(4):
                        n = 4 * cb + jj
                        pb = ps[2 + n % 2]
                        for k in range(KC):
                            MM(pb.t[:, 0:N], wv[:, k, jj * 128:(jj + 1) * 128], rn.t[:, k, 0:N], k == 0, k == KC - 1, [wk, "o_rn"], [pb.name])
                        t_ = tmp[n % 2]
                        TT_(t_.t[:, 0:N], pb.t[:, 0:N], gr.t[:, n, 0:N], ALU.mult, [pb.name, "o_gr"], [t_.name])
                        TT_(m16.t[:, n, 0:N], t_.t[:, 0:N], m.t[:, n, 0:N], ALU.add, [t_.name, "o_m"], ["o_m16"])
                for cb in range(4):
                    wv, wk = WLD(wview(wo, 512 * cb, 512), KC, 512)
                    for jj in range(4):
                        n = 4 * cb + jj
                        pb = ps[4 + n % 2]
                        for k in range(KC):
                            MM(pb.t[:, 0:N], wv[:, k, jj * 128:(jj + 1) * 128], m16.t[:, k, 0:N], k == 0, k == KC - 1, [wk, "o_m16"], [pb.name])
                        TT_(x.t[:, n, 0:N], pb.t[:, 0:N], x.t[:, n, 0:N], ALU.add, [pb.name, "o_x"], ["o_x"])
                ST(fm(xT, KC)[:, :, t0:t0 + N], x.t[:, :, 0:N], x, "o_x", writes=["xT"])
            P.end_phase()

    def final_phase():
        with ExitStack() as ph:
            P.begin_phase(ph)
            xs = [Buf(P, f"z_x{i}", [128, KC, 512], F32) for i in range(2)]
            xn = Buf(P, "z_xn", [128, KC, 512], F32)
            sq = Buf(P, "z_sq", [128, KC, 512], BF16)
            rstd = Buf(P, "z_rstd", [128, 512], F32)
            yo = [Buf(P, f"z_yo{i}", [128, D], F32) for i in range(2)]
            xT_v = xT.rearrange("(k p) t -> p k t", p=128)
            for ti, (t0, N) in enumerate(tiles):
                x = xs[ti % 2]
                LD(x.t[:, :, 0:N], xT_v[:, :, t0:t0 + N], x, x.name, reads=["xT"])
                rmsnorm_fm(x.t, x.name, KC, N, 112, xn.t, "z_xn", sq, "z_sq", ps[6], rstd, D)
                for bi, (b0, nt) in enumerate(blocks(N)):
                    yb = yo[bi % 2]
                    for g in range(4):
                        pb = ps[g % 2]
                        for kk in range(4):
                            k = 4 * g + kk
                            TR(pb.t[0:nt, kk * 128:(kk + 1) * 128], xn.t[:, k, b0:b0 + nt], ident, ["z_xn", "cf32"], [pb.name])
                        ACT(yb.t[0:nt, 512 * g:512 * g + 512], pb.t[0:nt, :], AF.Copy, [pb.name], [yb.name])
                    ST(y_o[t0 + b0:t0 + b0 + nt, :], yb.t[0:nt, :], yb, yb.name)
            P.end_phase()

    for l in range(nlay):
        if stage >= 1 and stage != 3:
            ffn_phase(l, W["ffn1_w13"][l], W["ffn1_w2"][l], 56 * l + 0)
        if stage == 11:
            ffn_phase(l, W["ffn2_w13"][l], W["ffn2_w2"][l], 56 * l + 32)
            break
        if stage >= 2:
            proj_phase(l)
        if stage <= 3:
            break
        exch_phase([(kx[c], kxG[c]) for c in range(5)], "kx")
        if stage >= 5:
            att_phase(l)
        if stage >= 6:
            ret_phase(l, False)
            exch_phase([(stL[:, :], stG[:, :])], "stL")
            ret_phase(l, True)
        if stage >= 7:
            out_phase(l)
        if stage >= 8:
            ffn_phase(l, W["ffn2_w13"][l], W["ffn2_w2"][l], 56 * l + 32)
    if stage >= 9 or stage <= 1:
        final_phase()
    P.barrier()
    P.emit()
    es.close()
    return nc, list(W.keys())


def _host_tables(core):
    half = core % 2
    pos = np.concatenate([half * TP + np.arange(TP), np.tile(PAST + np.arange(16), 2)]).astype(np.float64)
    inv_m = 10000.0 ** (-np.arange(0, 64, 2, dtype=np.float64) / 64)
    fm = np.tile(inv_m, 4)
    ang = fm[:, None] * pos[None, :]
    mtab = np.stack([np.cos(ang), np.sin(ang)]).astype(np.float32)
    inv_r = 10000.0 ** (-np.arange(0, 128, 2, dtype=np.float64) / 128)
    fr = np.tile(inv_r, 2)
    angr = fr[:, None] * pos[None, :]
    cr, sr = np.cos(angr), np.sin(angr)
    ib = np.concatenate([np.arange(TP) % 128, np.tile(np.arange(16), 2)]).astype(np.float64)
    rtab = np.zeros((8, 4, 128, TT), np.float32)
    for h in range(8):
        lg = np.log1p(-np.exp2(-5.0 - h))
        dq = np.exp((ib + 1.0) * lg)
        dk = np.exp(-(ib + 1.0) * lg) * (128 ** -0.5)
        rtab[h, 0] = cr * dq[None]
        rtab[h, 1] = sr * dq[None]
        rtab[h, 2] = cr * dk[None]
        rtab[h, 3] = sr * dk[None]
    return mtab, rtab


def _consts():
    cm = np.zeros((4, 128, 128), np.float32)
    cm[0] = 1.0
    Rm = np.zeros((128, 128), np.float32)
    for blk in range(2):
        o = 64 * blk
        for m in range(32):
            Rm[o + m, o + m + 32] = -1.0
            Rm[o + m + 32, o + m] = 1.0
    Rr = np.zeros((128, 128), np.float32)
    for m in range(64):
        Rr[m, m + 64] = -1.0
        Rr[m + 64, m] = 1.0
    cm[1] = Rm.T
    cm[2] = Rr.T
    cf = np.zeros((2, 128, 128), np.float32)
    cf[0] = np.eye(128)
    cf[1] = (np.arange(128)[None, :] >= np.arange(128)[:, None]).astype(np.float32)
    return cm.astype(ml_dtypes.bfloat16), cf


STAGE = 99
NTL = 5


def _run(inputs, stage=STAGE, ntl=NTL, trace=False, nlay=NL, segs="q,c,kr,rq,rk,rv,g"):
    f = lambda k: np.ascontiguousarray(np.asarray(inputs[k], dtype=np.float32))
    x_prompt, x_sample = f("x_prompt"), f("x_sample")
    cache_ckv, cache_krope, state_ret = f("cache_ckv"), f("cache_krope"), f("state_ret")
    nc, wused = build(stage=stage, ntl=ntl, nlay=nlay, segs=segs)
    cm, cf = _consts()
    norms = np.zeros((128, 128), np.float32)
    col = lambda v: v.reshape(-1, 128).T
    for l in range(NL):
        b = 56 * l
        norms[:, b:b + 16] = col(f("ffn1_norm")[l])
        norms[:, b + 16:b + 32] = col(f("mix_norm")[l])
        norms[:, b + 32:b + 48] = col(f("ffn2_norm")[l])
        norms[:, b + 48:b + 52] = col(f("q_norm")[l])
        norms[:, b + 52:b + 56] = col(f("kv_norm")[l])
    norms[:, 112:128] = col(f("final_norm"))
    shared = {}
    for k in wused:
        a = f(k)
        if k in ("w_uk", "w_uv"):
            a = a.reshape(NL, 512, 1024)
        shared[k] = np.ascontiguousarray(a[:nlay])
    in_maps = []
    for c in range(8):
        b, hf = c // 2, c % 2
        mtab, rtab = _host_tables(c)
        xin = np.concatenate([x_prompt[b, hf * TP:(hf + 1) * TP], x_sample[2 * c:2 * c + 2].reshape(TS, D)], axis=0)
        flags = np.zeros((128, 2), np.float32)
        flags[:, 0] = 0.0 if hf == 1 else NEG
        flags[:, 1] = 1.0 if hf == 1 else 0.0
        m = dict(shared)
        m.update(xin=np.ascontiguousarray(xin), norms=norms, mtab=mtab, rtab=rtab, cmat=cm, cf32=cf, flags=flags,
                 cache_c=np.ascontiguousarray(cache_ckv[:, 2 * c:2 * c + 2]),
                 cache_k=np.ascontiguousarray(cache_krope[:, 2 * c:2 * c + 2]),
                 state=np.ascontiguousarray(state_ret[:, 2 * c:2 * c + 2]))
        in_maps.append(m)
    res = run_bass_kernel_spmd(nc, in_maps, core_ids=list(range(8)), trace=trace)
    R = res.results
    y_prompt = np.zeros((4, SEQ, D), np.float32)
    y_sample = np.zeros((16, 16, D), np.float32)
    ckv_p = np.zeros((NL, 4, SEQ, 512), np.float32)
    kr_p = np.zeros((NL, 4, SEQ, 64), np.float32)
    ret_p = np.zeros((NL, 4, 8, 128, 256), np.float32)
    ckv_s = np.zeros((NL, 16, 16, 512), np.float32)
    kr_s = np.zeros((NL, 16, 16, 64), np.float32)
    ret_s = np.zeros((NL, 16, 8, 128, 256), np.float32)
    for c in range(8):
        b, hf = c // 2, c % 2
        r = R[c]
        y_prompt[b, hf * TP:(hf + 1) * TP] = r["y"][:TP]
        y_sample[2 * c:2 * c + 2] = r["y"][TP:].reshape(2, 16, D)
        ckv_p[:, b, hf * TP:(hf + 1) * TP] = r["ckv"][:, :TP]
        kr_p[:, b, hf * TP:(hf + 1) * TP] = r["krope"][:, :TP]
        ckv_s[:, 2 * c:2 * c + 2] = r["ckv"][:, TP:].reshape(NL, 2, 16, 512)
        kr_s[:, 2 * c:2 * c + 2] = r["krope"][:, TP:].reshape(NL, 2, 16, 64)
        if hf == 1:
            ret_p[:, b] = r["ret"][:, 0]
        ret_s[:, 2 * c:2 * c + 2] = r["ret"][:, 1:3]
    return (y_prompt, y_sample, ckv_p, kr_p, ret_p, ckv_s, kr_s, ret_s), res


def kernel(**inputs):
    outs, _ = _run(inputs)
    return outs
```

```python
import math
from contextlib import ExitStack
import numpy as np
import ml_dtypes
import concourse.bass as bass
import concourse.mybir as mybir
from concourse.bass_utils import run_bass_kernel_spmd

F32 = mybir.dt.float32
BF16 = mybir.dt.bfloat16
AF = mybir.ActivationFunctionType
ALU = mybir.AluOpType

D = 2048
KC = 16
DFF = 5632
FC = 44
NL = 2
TP = 2048
TS = 32
TT = TP + TS
SEQ = 4096
PAST = 2048
INW = 11328
O_QL, O_CKV, O_KR, O_RQ, O_RK, O_RV, O_RG, O_GM, O_GR = 0, 512, 1024, 1088, 2112, 3136, 5184, 7232, 9280
MLA_SCALE = 192 ** -0.5
TILES = [(0, 512), (512, 512), (1024, 512), (1536, 512), (2048, 32)]
NEG = -30000.0


class Sem:
    def __init__(self, h):
        self.h = h
        self.n = 0


class Ins:
    __slots__ = ("eng", "fn", "waits", "signal", "idx", "dma", "sigidx")


class Prog:
    def __init__(self, nc, es):
        self.nc = nc
        self.es = es
        self.eng = {"pe": nc.tensor, "act": nc.scalar, "dve": nc.vector, "pool": nc.gpsimd, "sp": nc.sync}
        self.esem = {e: es.enter_context(nc.semaphore("s_" + e)) for e in self.eng}
        self.lists = []
        self.count = {e: 0 for e in self.eng}
        self.known = {e: {x: 0 for x in self.eng} for e in self.eng}
        self.knownd = {e: {} for e in self.eng}
        self.lastw = {}
        self.readers = {}
        self.last_comp = {e: None for e in self.eng}
        self.dsems = []
        self.ges = es
        self.freesems = []
        self.nsem = 0
        self.in_phase = False
        self.phase_sems = []
        self.sig = {e: 0 for e in self.eng}

    def newsem(self, name):
        if self.freesems:
            s = self.freesems.pop()
        else:
            self.nsem += 1
            s = Sem(self.ges.enter_context(self.nc.semaphore("d%d" % self.nsem)))
            self.dsems.append(s)
        if self.in_phase:
            self.phase_sems.append(s)
        return s

    def begin_phase(self, ph):
        self.es = ph
        self.in_phase = True
        self.phase_sems = []

    def end_phase(self):
        self.barrier()
        self.emit()
        self.freesems.extend(self.phase_sems)
        self.phase_sems = []
        self.in_phase = False
        self.es = self.ges

    def _deps(self, E, reads, writes):
        cmax = {}
        dmax = {}

        def add(ev):
            if ev[0] == "c":
                J = ev[1]
                if J.eng == E and E == "pe":
                    return
                if J.eng not in cmax or cmax[J.eng].idx < J.idx:
                    cmax[J.eng] = J
            else:
                s, v = ev[1], ev[2]
                if dmax.get(s, 0) < v:
                    dmax[s] = v

        for r in reads:
            w = self.lastw.get(r)
            if w is not None:
                add(w)
        for w_ in writes:
            w = self.lastw.get(w_)
            if w is not None:
                add(w)
            for ev in self.readers.get(w_, {}).values():
                add(ev)
        waits = []
        for X, J in cmax.items():
            if self.known[E][X] >= J.idx:
                continue
            self.known[E][X] = J.idx
            J.signal = True
            waits.append(("c", J))
        for s, v in dmax.items():
            if self.knownd[E].get(s, 0) >= v:
                continue
            self.knownd[E][s] = v
            waits.append(("d", s, v))
        return waits

    def _record(self, ev, reads, writes):
        key = ("c", ev[1].eng) if ev[0] == "c" else ("d", ev[1])
        for r in reads:
            self.readers.setdefault(r, {})[key] = ev
        for w in writes:
            self.lastw[w] = ev
            self.readers[w] = {}

    def op(self, E, fn, reads=(), writes=()):
        pr = [r for r in reads if isinstance(r, str) and r.startswith("ps") and r not in writes]
        if pr:
            writes = list(writes) + pr
        I = Ins()
        I.eng = E
        I.fn = fn
        I.signal = False
        I.dma = None
        self.count[E] += 1
        I.idx = self.count[E]
        I.waits = self._deps(E, reads, writes)
        self._record(("c", I), reads, writes)
        self.lists.append(I)
        self.last_comp[E] = I
        return I

    def dma(self, Q, fn, sem, reads=(), writes=()):
        I = Ins()
        I.eng = Q
        I.fn = fn
        I.signal = False
        self.count[Q] += 1
        I.idx = self.count[Q]
        I.waits = self._deps(Q, reads, writes)
        sem.n += 16
        I.dma = (sem, sem.n)
        self._record(("d", sem, sem.n), reads, writes)
        self.lists.append(I)
        return I

    def barrier(self):
        nc = self.nc
        waits = []
        for X in self.eng:
            J = self.last_comp[X]
            if J is None or X == "sp":
                continue
            if self.known["sp"][X] >= J.idx:
                continue
            self.known["sp"][X] = J.idx
            J.signal = True
            waits.append(("c", J))
        for s in self.dsems:
            if s.n > self.knownd["sp"].get(s, 0):
                self.knownd["sp"][s] = s.n
                waits.append(("d", s, s.n))
        I = Ins()
        I.eng = "sp"
        I.fn = lambda: nc.sync.nop()
        I.signal = True
        I.dma = None
        self.count["sp"] += 1
        I.idx = self.count["sp"]
        I.waits = waits
        self.lists.append(I)
        self.last_comp["sp"] = I
        for X in self.eng:
            if X == "sp":
                continue
            J = Ins()
            J.eng = X
            eng = self.eng[X]
            J.fn = (lambda e: (lambda: e.nop()))(eng)
            J.signal = False
            J.dma = None
            self.count[X] += 1
            J.idx = self.count[X]
            J.waits = [("c", I)]
            self.lists.append(J)
            self.last_comp[X] = J
        self.lastw = {}
        self.readers = {}
        for E in self.eng:
            for X in self.eng:
                self.known[E][X] = self.count[X]
            for s in self.dsems:
                self.knownd[E][s] = s.n

    def emit(self):
        sig = self.sig
        for I in self.lists:
            eng = self.eng[I.eng]
            for ev in I.waits:
                if ev[0] == "c":
                    eng.wait_ge(self.esem[ev[1].eng], ev[1].sigidx)
                else:
                    eng.wait_ge(ev[1].h, ev[2])
            bi = I.fn()
            if I.dma is not None:
                bi.then_inc(I.dma[0].h, 16)
            elif I.signal:
                sig[I.eng] += 1
                I.sigidx = sig[I.eng]
                bi.then_inc(self.esem[I.eng], 1)
        self.lists = []


class Buf:
    uid = 0

    def __init__(self, P, name, shape, dtype, psum=False):
        nc = P.nc
        self.name = name
        Buf.uid += 1
        tn = "%s_%d" % (name, Buf.uid)
        self.t = P.es.enter_context(nc.psum_tensor(tn, shape, dtype) if psum else nc.sbuf_tensor(tn, shape, dtype))
        self.P = P
        self._ld = None
        self._st = None

    @property
    def ld(self):
        if self._ld is None:
            self._ld = self.P.newsem("l_" + self.name)
        return self._ld

    @property
    def st(self):
        if self._st is None:
            self._st = self.P.newsem("t_" + self.name)
        return self._st


def build(stage=99, ntl=5, exch=True, nlay=NL, segs="q,c,kr,rq,rk,rv,g"):
    segs = set(segs.split(","))
    nc = bass.Bass("TRN2", target_bir_lowering=False)
    es = ExitStack()
    P = Prog(nc, es)
    dt_in = lambda n, s, d=F32: nc.dram_tensor(n, s, d, kind="ExternalInput").ap()
    dt_out = lambda n, s, d=F32: nc.dram_tensor(n, s, d, kind="ExternalOutput").ap()
    dt_scr = lambda n, s, d=BF16: nc.dram_tensor(n, s, d).ap()

    xin = dt_in("xin", [TT, D])
    WSHP = {"ffn1_w13": [D, 2 * DFF], "ffn1_w2": [DFF, D], "w_in": [D, INW],
            "w_uq": [512, 1536], "w_uk": [512, 1024], "w_uv": [512, 1024],
            "w_mla_out": [1024, D], "w_ret_out": [D, D], "w_out": [D, D],
            "ffn2_w13": [D, 2 * DFF], "ffn2_w2": [DFF, D]}

    class _W(dict):
        def __missing__(self, nm):
            self[nm] = dt_in(nm, [nlay] + WSHP[nm])
            return self[nm]
    W = _W()
    norms_d = dt_in("norms", [128, 128])
    mtab_d = dt_in("mtab", [2, 128, TT])
    rtab_d = dt_in("rtab", [8, 4, 128, TT])
    cmat_d = dt_in("cmat", [4, 128, 128], BF16)
    cf32_d = dt_in("cf32", [2, 128, 128])
    flags_d = dt_in("flags", [128, 2])
    cache_c = dt_in("cache_c", [NL, 2, PAST, 512])
    cache_k = dt_in("cache_k", [NL, 2, PAST, 64])
    state_d = dt_in("state", [NL, 2, 8, 128, 256])

    y_o = dt_out("y", [TT, D])
    ckv_o = dt_out("ckv", [NL, TT, 512])
    kr_o = dt_out("krope", [NL, TT, 64])
    ret_o = dt_out("ret", [NL, 3, 8, 128, 256])

    xT = dt_scr("xT", [D, TT], F32)
    qnT = dt_scr("qnT", [1024, TT])
    qrT = dt_scr("qrT", [512, TT])
    kvT = dt_scr("kvT", [640, TT])
    kx = dt_scr("kx", [5, 1024, 256], F32)
    kxG = dt_scr("kxG", [5, 2048, 256], F32)
    rqT = dt_scr("rqT", [1024, TT])
    rkT = dt_scr("rkT", [1024, TT])
    rk = dt_scr("rk", [TT, 1024])
    rv = dt_scr("rv", [TT, 2048])
    rgT = dt_scr("rgT", [D, TT])
    gmT = dt_scr("gmT", [D, TT])
    grT = dt_scr("grT", [D, TT])
    aoT = dt_scr("aoT", [1024, TT])
    rnT = dt_scr("rnT", [D, TT])
    stL = dt_scr("stL", [1024, 256], F32)
    stG = dt_scr("stG", [2048, 256], F32)

    ps = [Buf(P, f"ps{i}", [128, 512], F32, psum=True) for i in range(8)]
    norms = Buf(P, "norms_sb", [128, 128], F32)
    cmat = Buf(P, "cmat_sb", [128, 4, 128], BF16)
    cf32 = Buf(P, "cf32_sb", [128, 2, 128], F32)
    flags = Buf(P, "flags_sb", [128, 2], F32)
    wb = [Buf(P, f"wb{i}", [128, 8192], BF16) for i in range(4)]
    csem = P.newsem("csem")
    for b_ in wb:
        b_.ld
    P.dma("sp", lambda: nc.sync.dma_start(out=norms.t[:], in_=norms_d[:, :]), csem, writes=["norms"])
    P.dma("sp", lambda: nc.sync.dma_start(out=cmat.t[:], in_=cmat_d.rearrange("c p n -> p c n")), csem, writes=["cmat"])
    P.dma("sp", lambda: nc.sync.dma_start(out=cf32.t[:], in_=cf32_d.rearrange("c p n -> p c n")), csem, writes=["cf32"])
    P.dma("sp", lambda: nc.sync.dma_start(out=flags.t[:], in_=flags_d[:, :]), csem, writes=["flags"])
    ones = cmat.t[:, 0, :]
    RmT = cmat.t[:, 1, :]
    RrT = cmat.t[:, 2, :]
    ident = cf32.t[:, 0, :]
    cmask = cf32.t[:, 1, :]
    wrr = [0]

    def MM(out, lhsT, rhs, start, stop, reads, writes):
        return P.op("pe", lambda: nc.tensor.matmul(out, lhsT=lhsT, rhs=rhs, start=start, stop=stop), reads, writes)

    def TR(out, in_, idn, reads, writes):
        return P.op("pe", lambda: nc.tensor.transpose(out=out, in_=in_, identity=idn), reads, writes)

    def ACT(out, in_, func, reads, writes, scale=1.0, bias=None):
        if bias is None:
            return P.op("act", lambda: nc.scalar.activation(out=out, in_=in_, func=func, scale=scale), reads, writes)
        return P.op("act", lambda: nc.scalar.activation(out=out, in_=in_, func=func, scale=scale, bias=bias), reads, writes)

    def TT_(out, a, b, op, reads, writes, eng="dve"):
        e = nc.vector if eng == "dve" else nc.gpsimd
        return P.op(eng, lambda: e.tensor_tensor(out=out, in0=a, in1=b, op=op), reads, writes)

    def TS(out, a, s1, s2, op0, op1, reads, writes):
        if op1 is None:
            return P.op("dve", lambda: nc.vector.tensor_scalar(out=out, in0=a, scalar1=s1, scalar2=None, op0=op0), reads, writes)
        return P.op("dve", lambda: nc.vector.tensor_scalar(out=out, in0=a, scalar1=s1, scalar2=s2, op0=op0, op1=op1), reads, writes)

    def STT(out, a, s, b, op0, op1, reads, writes):
        return P.op("dve", lambda: nc.vector.scalar_tensor_tensor(out=out, in0=a, scalar=s, in1=b, op0=op0, op1=op1), reads, writes)

    def LD(dst_ap, src_ap, buf, key, reads=()):
        return P.dma("sp", lambda: nc.sync.dma_start(out=dst_ap, in_=src_ap), buf.ld, reads=list(reads), writes=[key])

    def ST(dst_ap, src_ap, buf, key, writes=()):
        return P.dma("sp", lambda: nc.sync.dma_start(out=dst_ap, in_=src_ap), buf.st, reads=[key], writes=list(writes))

    def WLD(src_ap, kc, nb):
        i = wrr[0] % len(wb)
        wrr[0] += 1
        b = wb[i]
        view = b.t[:, 0:kc * nb].rearrange("p (k n) -> p k n", k=kc)
        P.dma("pool", lambda: nc.gpsimd.dma_start(out=view, in_=src_ap), b.ld, writes=[b.name])
        return view, b.name

    def wview(w_l, c0, nb):
        return w_l.rearrange("(k p) n -> p k n", p=128)[:, :, c0:c0 + nb]

    def rmsnorm_fm(src, skey, kcn, N, gcol, dst, dkey, sq, sqkey, pbank, rstd, dim):
        ACT(sq.t[:, 0:kcn, 0:N], src[:, 0:kcn, 0:N], AF.Square, [skey], [sqkey])
        for k in range(kcn):
            MM(pbank.t[:, 0:N], ones, sq.t[:, k, 0:N], k == 0, k == kcn - 1, [sqkey, "cmat"], [pbank.name])
        ACT(rstd.t[:, 0:N], pbank.t[:, 0:N], AF.Sqrt, [pbank.name], [rstd.name], scale=1.0 / dim, bias=1e-6)
        P.op("dve", lambda: nc.vector.reciprocal(out=rstd.t[:, 0:N], in_=rstd.t[:, 0:N]), [rstd.name], [rstd.name])
        for k in range(kcn):
            STT(dst[:, k, 0:N], src[:, k, 0:N], norms.t[:, gcol + k:gcol + k + 1], rstd.t[:, 0:N], ALU.mult, ALU.mult,
                [skey, rstd.name, "norms"], [dkey])

    tiles = TILES[:ntl] if ntl < 5 else TILES
    if ntl < 5 and ntl > 0 and False:
        pass

    def blocks(N):
        return [(b * 128, min(128, N - b * 128)) for b in range((N + 127) // 128)]

    with ExitStack() as ph:
        P.begin_phase(ph)
        xtm = [Buf(P, f"in_xtm{i}", [128, D], F32) for i in range(2)]
        xfm = Buf(P, "in_xfm", [128, KC, 512], F32)
        for (t0, N) in tiles:
            for bi, (b0, nt) in enumerate(blocks(N)):
                xb = xtm[bi % 2]
                LD(xb.t[0:nt, :], xin[t0 + b0:t0 + b0 + nt, :], xb, xb.name)
                for g in range(4):
                    pb = ps[g % 2]
                    for kk in range(4):
                        k = 4 * g + kk
                        TR(pb.t[:, kk * 128:kk * 128 + nt], xb.t[0:nt, k * 128:(k + 1) * 128], ident[0:nt, 0:nt],
                           [xb.name, "cf32"], [pb.name])
                    ACT(xfm.t[:, 4 * g:4 * g + 4, b0:b0 + nt],
                        pb.t[:, :].rearrange("p (k n) -> p k n", k=4)[:, :, 0:nt], AF.Copy, [pb.name], ["xfm"])
            ST(xT.rearrange("(k p) t -> p k t", p=128)[:, :, t0:t0 + N], xfm.t[:, :, 0:N], xfm, "xfm", writes=["xT"])
        P.end_phase()

    def ffn_phase(l, w13, w2, gcol):
        with ExitStack() as ph:
            P.begin_phase(ph)
            xs = [Buf(P, f"f_x{i}", [128, KC, 512], F32) for i in range(2)]
            xn = Buf(P, "f_xn", [128, KC, 512], BF16)
            g = Buf(P, "f_g", [128, FC, 512], BF16)
            rstd = Buf(P, "f_rstd", [128, 512], F32)
            sa = [Buf(P, f"f_sa{i}", [128, 512], F32) for i in range(2)]
            xT_v = xT.rearrange("(k p) t -> p k t", p=128)

            def load(ti):
                t0, N = tiles[ti]
                b = xs[ti % 2]
                LD(b.t[:, :, 0:N], xT_v[:, :, t0:t0 + N], b, b.name, reads=["xT"])
            load(0)
            for ti, (t0, N) in enumerate(tiles):
                x = xs[ti % 2]
                if ti + 1 < len(tiles):
                    load(ti + 1)
                rmsnorm_fm(x.t, x.name, KC, N, gcol, xn.t, "f_xn", g, "f_g", ps[6], rstd, D)
                for jb in range(11):
                    wa, ka = WLD(wview(w13, 512 * jb, 512), KC, 512)
                    wbb, kb = WLD(wview(w13, DFF + 512 * jb, 512), KC, 512)
                    for jj in range(4):
                        j = 4 * jb + jj
                        pa, pb = ps[j % 2], ps[2 + j % 2]
                        for k in range(KC):
                            MM(pa.t[:, 0:N], wa[:, k, jj * 128:(jj + 1) * 128], xn.t[:, k, 0:N], k == 0, k == KC - 1,
                               [ka, "f_xn"], [pa.name])
                        for k in range(KC):
                            MM(pb.t[:, 0:N], wbb[:, k, jj * 128:(jj + 1) * 128], xn.t[:, k, 0:N], k == 0, k == KC - 1,
                               [kb, "f_xn"], [pb.name])
                        s = sa[j % 2]
                        ACT(s.t[:, 0:N], pa.t[:, 0:N], AF.Silu, [pa.name], [s.name])
                        TT_(g.t[:, j, 0:N], s.t[:, 0:N], pb.t[:, 0:N], ALU.mult, [s.name, pb.name], ["f_g"])
                for n in range(KC):
                    w2v, k2 = WLD(wview(w2, 128 * n, 128), FC, 128)
                    pc = ps[4 + n % 2]
                    for k in range(FC):
                        MM(pc.t[:, 0:N], w2v[:, k, :], g.t[:, k, 0:N], k == 0, k == FC - 1, [k2, "f_g"], [pc.name])
                    STT(x.t[:, n, 0:N], pc.t[:, 0:N], 0.5, x.t[:, n, 0:N], ALU.mult, ALU.add, [pc.name, x.name], [x.name])
                ST(xT_v[:, :, t0:t0 + N], x.t[:, :, 0:N], x, x.name, writes=["xT"])
            P.end_phase()

    def proj_phase(l):
        win = W["w_in"][l]
        nb = 56 * l
        with ExitStack() as ph:
            P.begin_phase(ph)
            xs = [Buf(P, f"p_x{i}", [128, KC, 512], F32) for i in range(1)]
            u = Buf(P, "p_u", [128, KC, 512], BF16)
            sq = Buf(P, "p_sq", [128, KC, 512], BF16)
            rstd = Buf(P, "p_rstd", [128, 512], F32)
            lat = Buf(P, "p_lat", [128, 4, 512], F32)
            latn = lat
            qn = Buf(P, "p_qn", [128, 4, 512], BF16)
            cb16 = Buf(P, "p_cb16", [128, 4, 512], BF16)
            wq = Buf(P, "p_wq", [128, 4, 1536], BF16)
            wkr = Buf(P, "p_wkr", [128, KC, 128], BF16)
            mt = [Buf(P, f"p_mt{i}", [128, 2, 512], F32) for i in range(1)]
            rt = [Buf(P, f"p_rt{i}", [128, 4, 512], F32) for i in range(1)]
            xb = [Buf(P, f"p_xb{i}", [128, 512], BF16) for i in range(2)]
            t1 = [Buf(P, f"p_t1{i}", [128, 512], F32) for i in range(1)]
            t2 = [Buf(P, f"p_t2{i}", [128, 512], F32) for i in range(1)]
            kf = [Buf(P, f"p_kf{i}", [128, 512], F32) for i in range(1)]
            so = [Buf(P, f"p_so{i}", [128, 512], BF16) for i in range(4)]
            sf = [Buf(P, f"p_sf{i}", [128, 512], F32) for i in range(2)]
            sorr = [0]
            xT_v = xT.rearrange("(k p) t -> p k t", p=128)
            wuq = W["w_uq"][l].rearrange("(k p) (h c) -> p k h c", p=128, c=192)
            for k in range(4):
                P.dma("pool", lambda k=k: nc.gpsimd.dma_start(
                    out=wq.t[:, k, 0:1024].rearrange("p (h c) -> p h c", c=128), in_=wuq[:, k, :, 0:128]), wq.ld, writes=["p_wq"])
                P.dma("pool", lambda k=k: nc.gpsimd.dma_start(
                    out=wq.t[:, k, 1024:1536].rearrange("p (h c) -> p h c", c=64), in_=wuq[:, k, :, 128:192]), wq.ld, writes=["p_wq"])
            for hh in range(2):
                P.dma("pool", lambda hh=hh: nc.gpsimd.dma_start(
                    out=wkr.t[:, :, hh * 64:(hh + 1) * 64], in_=wview(win, O_KR, 64)), wkr.ld, writes=["p_wkr"])

            def load(ti):
                t0, N = tiles[ti]
                b = xs[0]
                LD(b.t[:, :, 0:N], xT_v[:, :, t0:t0 + N], b, b.name, reads=["xT"])

            def stage_store(dst_ap, src_psum, pkey, N, func=AF.Copy, scale=1.0, extra_reads=()):
                s = so[sorr[0] % 4]
                sorr[0] += 1
                ACT(s.t[:, 0:N], src_psum, func, [pkey] + list(extra_reads), [s.name], scale=scale)
                ST(dst_ap, s.t[:, 0:N], s, s.name)

            def rope_fm(pb, N, RT, cosap, sinap, tabkey, i, outbuf):
                x16 = xb[i % 2]
                ACT(x16.t[:, 0:N], pb.t[:, 0:N], AF.Copy, [pb.name], [x16.name])
                pr = ps[6 + i % 2]
                MM(pr.t[:, 0:N], RT, x16.t[:, 0:N], True, True, [x16.name, "cmat"], [pr.name])
                a = t1[0]
                b = t2[0]
                TT_(a.t[:, 0:N], pb.t[:, 0:N], cosap, ALU.mult, [pb.name, tabkey], [a.name])
                TT_(b.t[:, 0:N], pr.t[:, 0:N], sinap, ALU.mult, [pr.name, tabkey], [b.name])
                TT_(outbuf.t[:, 0:N], a.t[:, 0:N], b.t[:, 0:N], ALU.add, [a.name, b.name], [outbuf.name])

            load(0)
            for ti, (t0, N) in enumerate(tiles):
                x = xs[0]
                m = mt[0]
                LD(m.t[:, :, 0:N], mtab_d.rearrange("c p t -> p c t")[:, :, t0:t0 + N], m, m.name)
                rmsnorm_fm(x.t, x.name, KC, N, nb + 16, u.t, "p_u", sq, "p_sq", ps[5], rstd, D)
                if ti + 1 < len(tiles):
                    load(ti + 1)
                for seg, (c0, gc) in enumerate([(O_QL, nb + 48), (O_CKV, nb + 52)]):
                    if ("q" if seg == 0 else "c") not in segs:
                        continue
                    wv, wk = WLD(wview(win, c0, 512), KC, 512)
                    for j in range(4):
                        pb = ps[j % 2]
                        for k in range(KC):
                            MM(pb.t[:, 0:N], wv[:, k, j * 128:(j + 1) * 128], u.t[:, k, 0:N], k == 0, k == KC - 1, [wk, "p_u"], [pb.name])
                        ACT(lat.t[:, j, 0:N], pb.t[:, 0:N], AF.Copy, [pb.name], ["p_lat"])
                    rmsnorm_fm(lat.t, "p_lat", 4, N, gc, latn.t, "p_lat", sq, "p_sq", ps[5], rstd, 512)
                    if seg == 0:
                        ACT(qn.t[:, :, 0:N], latn.t[:, :, 0:N], AF.Copy, ["p_lat"], ["p_qn"])
                        for h in range(8):
                            pb = ps[h % 2]
                            for k in range(4):
                                MM(pb.t[:, 0:N], wq.t[:, k, 128 * h:128 * h + 128], qn.t[:, k, 0:N], k == 0, k == 3, ["p_wq", "p_qn"], [pb.name])
                            stage_store(qnT[128 * h:128 * h + 128, t0:t0 + N], pb.t[:, 0:N], pb.name, N)
                        for mm_ in range(4):
                            pb = ps[2 + mm_ % 2]
                            for k in range(4):
                                MM(pb.t[:, 0:N], wq.t[:, k, 1024 + 128 * mm_:1024 + 128 * mm_ + 128], qn.t[:, k, 0:N], k == 0, k == 3,
                                   ["p_wq", "p_qn"], [pb.name])
                            o = kf[0]
                            rope_fm(pb, N, RmT, m.t[:, 0, 0:N], m.t[:, 1, 0:N], m.name, mm_, o)
                            s = so[sorr[0] % 4]
                            sorr[0] += 1
                            ACT(s.t[:, 0:N], o.t[:, 0:N], AF.Copy, [o.name], [s.name])
                            ST(qrT[128 * mm_:128 * mm_ + 128, t0:t0 + N], s.t[:, 0:N], s, s.name)
                    else:
                        ACT(cb16.t[:, :, 0:N], latn.t[:, :, 0:N], AF.Copy, ["p_lat"], ["p_cb16"])
                        ST(kvT[0:512, :].rearrange("(k p) t -> p k t", p=128)[:, :, t0:t0 + N], cb16.t[:, :, 0:N], cb16, "p_cb16", writes=["kvT"])
                        if t0 < TP:
                            for k in range(4):
                                ST(kx[k].rearrange("(r q) c -> r (q c)", q=8)[:, t0:t0 + N], latn.t[:, k, 0:N], lat, "p_lat")
                        for bi, (b0, nt) in enumerate(blocks(N)):
                            pb = ps[2 + bi % 2]
                            for k in range(4):
                                TR(pb.t[0:nt, k * 128:(k + 1) * 128], latn.t[:, k, b0:b0 + nt], ident, ["p_lat", "cf32"], [pb.name])
                            s = sf[bi % 2]
                            ACT(s.t[0:nt, :], pb.t[0:nt, :], AF.Copy, [pb.name], [s.name])
                            ST(ckv_o[l, t0 + b0:t0 + b0 + nt, :], s.t[0:nt, :], s, s.name)
                pb = ps[0]
                if "kr" not in segs:
                    continue
                for k in range(KC):
                    MM(pb.t[:, 0:N], wkr.t[:, k, :], u.t[:, k, 0:N], k == 0, k == KC - 1, ["p_wkr", "p_u"], [pb.name])
                o = kf[0]
                rope_fm(pb, N, RmT, m.t[:, 0, 0:N], m.t[:, 1, 0:N], m.name, 0, o)
                s = so[sorr[0] % 4]
                sorr[0] += 1
                ACT(s.t[:, 0:N], o.t[:, 0:N], AF.Copy, [o.name], [s.name])
                ST(kvT[512:640, t0:t0 + N], s.t[:, 0:N], s, s.name, writes=["kvT"])
                if t0 < TP:
                    ST(kx[4].rearrange("(r q) c -> r (q c)", q=8)[:, t0:t0 + N], o.t[:, 0:N], o, o.name)
                for bi, (b0, nt) in enumerate(blocks(N)):
                    pb2 = ps[2 + bi % 2]
                    TR(pb2.t[0:nt, 0:128], o.t[:, b0:b0 + nt], ident, [o.name, "cf32"], [pb2.name])
                    s2 = sf[bi % 2]
                    ACT(s2.t[0:nt, 0:64], pb2.t[0:nt, 0:64], AF.Copy, [pb2.name], [s2.name])
                    ST(kr_o[l, t0 + b0:t0 + b0 + nt, :], s2.t[0:nt, 0:64], s2, s2.name)
                for which, c0 in enumerate([O_RQ, O_RK]):
                    if ("rq", "rk")[which] not in segs:
                        continue
                    for hb in range(2):
                        wv, wk = WLD(wview(win, c0 + 512 * hb, 512), KC, 512)
                        for jj in range(4):
                            h = 4 * hb + jj
                            r = rt[0]
                            LD(r.t[:, :, 0:N], rtab_d[h].rearrange("f p t -> p f t")[:, :, t0:t0 + N], r, r.name)
                            pb = ps[h % 2]
                            for k in range(KC):
                                MM(pb.t[:, 0:N], wv[:, k, jj * 128:(jj + 1) * 128], u.t[:, k, 0:N], k == 0, k == KC - 1, [wk, "p_u"], [pb.name])
                            o = kf[0]
                            rope_fm(pb, N, RrT, r.t[:, 2 * which, 0:N], r.t[:, 2 * which + 1, 0:N], r.name, h, o)
                            s = so[sorr[0] % 4]
                            sorr[0] += 1
                            ACT(s.t[:, 0:N], o.t[:, 0:N], AF.Copy, [o.name], [s.name])
                            dstT = rqT if which == 0 else rkT
                            ST(dstT[128 * h:128 * h + 128, t0:t0 + N], s.t[:, 0:N], s, s.name)
                            if which == 1:
                                for bi, (b0, nt) in enumerate(blocks(N)):
                                    pb2 = ps[2 + bi % 2]
                                    TR(pb2.t[0:nt, 0:128], o.t[:, b0:b0 + nt], ident, [o.name, "cf32"], [pb2.name])
                                    s2 = so[sorr[0] % 4]
                                    sorr[0] += 1
                                    cd_ = (1.0 - 2.0 ** (-5.0 - h)) ** (128 if N == 512 else 16)
                                    ACT(s2.t[0:nt, 0:128], pb2.t[0:nt, 0:128], AF.Copy, [pb2.name], [s2.name], scale=float(cd_))
                                    ST(rk[t0 + b0:t0 + b0 + nt, 128 * h:128 * h + 128], s2.t[0:nt, 0:128], s2, s2.name)
                for cb in range(4):
                    if "rv" not in segs:
                        continue
                    wv, wk = WLD(wview(win, O_RV + 512 * cb, 512), KC, 512)
                    for bi, (b0, nt) in enumerate(blocks(N)):
                        pb = ps[bi % 2]
                        for k in range(KC):
                            MM(pb.t[0:nt, :], u.t[:, k, b0:b0 + nt], wv[:, k, :], k == 0, k == KC - 1, [wk, "p_u"], [pb.name])
                        s = so[sorr[0] % 4]
                        sorr[0] += 1
                        ACT(s.t[0:nt, :], pb.t[0:nt, :], AF.Copy, [pb.name], [s.name])
                        ST(rv[t0 + b0:t0 + b0 + nt, 512 * cb:512 * cb + 512], s.t[0:nt, :], s, s.name)
                for c0, dstT, fn in [(O_GM, gmT, AF.Sigmoid), (O_GR, grT, AF.Sigmoid), (O_RG, rgT, AF.Silu)]:
                    if "g" not in segs:
                        continue
                    for cb in range(4):
                        wv, wk = WLD(wview(win, c0 + 512 * cb, 512), KC, 512)
                        for jj in range(4):
                            j = 4 * cb + jj
                            pb = ps[j % 2]
                            for k in range(KC):
                                MM(pb.t[:, 0:N], wv[:, k, jj * 128:(jj + 1) * 128], u.t[:, k, 0:N], k == 0, k == KC - 1, [wk, "p_u"], [pb.name])
                            stage_store(dstT[128 * j:128 * j + 128, t0:t0 + N], pb.t[:, 0:N], pb.name, N, func=fn)
            P.end_phase()


    GROUPS = [[0, 1], [2, 3], [4, 5], [6, 7]]
    GAM = [1.0 - 2.0 ** (-5.0 - h) for h in range(8)]

    def exch_phase(pairs, nm):
        with ExitStack() as ph:
            P.begin_phase(ph)
            for (src, dst) in pairs:
                P.op("pool", lambda src=src, dst=dst: nc.gpsimd.collective_compute(
                    "AllGather", ALU.bypass, replica_groups=GROUPS, ins=[src], outs=[dst]), [nm], [nm + "G"])
            P.end_phase()

    def att_phase(l):
        wuk, wuv = W["w_uk"][l], W["w_uv"][l]
        with ExitStack() as ph:
            P.begin_phase(ph)
            cT = Buf(P, "a_cT", [128, 4, 4096], BF16)
            krT_ = Buf(P, "a_krT", [128, 4096], BF16)
            KT = [Buf(P, f"a_KT{i}", [128, 4096], BF16) for i in range(2)]
            Vh = [Buf(P, f"a_V{i}", [128, 32, 128], BF16) for i in range(2)]
            qn_ = [Buf(P, f"a_qn{i}", [128, 512], BF16) for i in range(2)]
            qr_ = [Buf(P, f"a_qr{i}", [128, 512], BF16) for i in range(2)]
            pt = [Buf(P, f"a_pt{i}", [128, 512], BF16) for i in range(6)]
            rden = [Buf(P, f"a_rd{i}", [128, 512], F32) for i in range(2)]
            aob = [Buf(P, f"a_ao{i}", [128, 512], BF16) for i in range(2)]
            cc = [Buf(P, f"a_cc{i}", [128, 512], F32) for i in range(2)]
            ck = [Buf(P, f"a_ck{i}", [128, 128], F32) for i in range(2)]
            ctr = [0, 0, 0]

            def build_kv(h, nkeys):
                kt, vh = KT[h % 2], Vh[h % 2]
                wk_, kk_ = WLD(wview(wuk, 128 * h, 128), 4, 128)
                wv_, kv_ = WLD(wview(wuv, 128 * h, 128), 4, 128)
                c0 = 0
                while c0 < nkeys:
                    n = min(512, nkeys - c0)
                    pb = ps[6 + ctr[0] % 2]
                    ctr[0] += 1
                    for k in range(4):
                        MM(pb.t[:, 0:n], wk_[:, k, :], cT.t[:, k, c0:c0 + n], k == 0, k == 3, [kk_, "a_cT"], [pb.name])
                    ACT(kt.t[:, c0:c0 + n], pb.t[:, 0:n], AF.Copy, [pb.name], [kt.name])
                    c0 += n
                nblk = (nkeys + 127) // 128
                for g0 in range(0, nblk, 4):
                    pb = ps[6 + ctr[0] % 2]
                    ctr[0] += 1
                    gl = min(4, nblk - g0)
                    for gi in range(gl):
                        kb = g0 + gi
                        nk = min(128, nkeys - kb * 128)
                        for k in range(4):
                            MM(pb.t[0:nk, gi * 128:(gi + 1) * 128], cT.t[:, k, kb * 128:kb * 128 + nk], wv_[:, k, :], k == 0, k == 3,
                               [kv_, "a_cT"], [pb.name])
                    full = [gi for gi in range(gl) if min(128, nkeys - (g0 + gi) * 128) == 128]
                    if full:
                        nf = len(full)
                        ACT(vh.t[:, g0:g0 + nf, :], pb.t[:, 0:nf * 128].rearrange("p (g d) -> p g d", g=nf), AF.Copy, [pb.name], [vh.name])
                    if len(full) < gl:
                        gi = gl - 1
                        nk = nkeys - (g0 + gi) * 128
                        ACT(vh.t[0:nk, g0 + gi, :], pb.t[0:nk, gi * 128:(gi + 1) * 128], AF.Copy, [pb.name], [vh.name])

            def attend(h, q0, NQ, nkeys, nctx, causal_qt):
                kt, vh = KT[h % 2], Vh[h % 2]
                qn, qr = qn_[ctr[1] % 2], qr_[ctr[1] % 2]
                po_, pd_ = ps[2 + ctr[1] % 2], ps[4 + ctr[1] % 2]
                rd, ao = rden[ctr[1] % 2], aob[ctr[1] % 2]
                ctr[1] += 1
                LD(qn.t[:, 0:NQ], qnT[128 * h:128 * h + 128, q0:q0 + NQ], qn, qn.name)
                LD(qr.t[:, 0:NQ], qrT[128 * (h // 2):128 * (h // 2) + 128, q0:q0 + NQ], qr, qr.name)
                p0 = 64 * (h % 2)
                if causal_qt is None:
                    nvis = (nkeys + 127) // 128
                else:
                    nvis = nctx // 128 + 4 * causal_qt + 4
                steps = []
                for kb in range(nvis):
                    nk = min(128, nkeys - kb * 128)
                    off = 0
                    isctx = kb * 128 < nctx
                    diag = False
                    if causal_qt is not None and not isctx:
                        j = kb - nctx // 128
                        if j >= 4 * causal_qt:
                            off = 128 * (j - 4 * causal_qt)
                            diag = True
                    steps.append((kb, nk, off, isctx, diag))
                SB = [ps[0], ps[1], ps[6], ps[7]]

                def emit_S(i):
                    kb, nk, off, isctx, diag = steps[i]
                    pS = SB[ctr[2] % 4]
                    p_ = pt[ctr[2] % len(pt)]
                    ctr[2] += 1
                    MM(pS.t[0:nk, off:NQ], kt.t[:, kb * 128:kb * 128 + nk], qn.t[:, off:NQ], True, False, [kt.name, qn.name], [pS.name])
                    MM(pS.t[0:nk, off:NQ], krT_.t[p0:p0 + 64, kb * 128:kb * 128 + nk], qr.t[p0:p0 + 64, off:NQ], False, True,
                       ["a_krT", qr.name], [pS.name])
                    if isctx and causal_qt is not None:
                        ACT(p_.t[0:nk, off:NQ], pS.t[0:nk, off:NQ], AF.Exp, [pS.name, "flags"], [p_.name], scale=MLA_SCALE, bias=flags.t[0:nk, 0:1])
                    else:
                        ACT(p_.t[0:nk, off:NQ], pS.t[0:nk, off:NQ], AF.Exp, [pS.name], [p_.name], scale=MLA_SCALE)
                    if diag:
                        P.op("dve", lambda p_=p_, off=off: nc.vector.memset(p_.t[64:128, off:off + 64], 0.0), [], [p_.name])
                    return p_

                def emit_PV(i, p_):
                    kb, nk, off, isctx, diag = steps[i]
                    first, last = i == 0, i == len(steps) - 1
                    MM(po_.t[:, off:NQ], vh.t[0:nk, kb, :], p_.t[0:nk, off:NQ], first, last, [vh.name, p_.name], [po_.name])
                    MM(pd_.t[:, off:NQ], ones[0:nk, :], p_.t[0:nk, off:NQ], first, last, ["cmat", p_.name], [pd_.name])

                LOOK = 2
                pend = {}
                for i in range(min(LOOK, len(steps))):
                    pend[i] = emit_S(i)
                for i in range(len(steps)):
                    if i + LOOK < len(steps):
                        pend[i + LOOK] = emit_S(i + LOOK)
                    emit_PV(i, pend.pop(i))
                P.op("dve", lambda: nc.vector.reciprocal(out=rd.t[:, 0:NQ], in_=pd_.t[:, 0:NQ]), [pd_.name], [rd.name])
                TT_(ao.t[:, 0:NQ], po_.t[:, 0:NQ], rd.t[:, 0:NQ], ALU.mult, [po_.name, rd.name], [ao.name])
                ST(aoT[128 * h:128 * h + 128, q0:q0 + NQ], ao.t[:, 0:NQ], ao, ao.name)

            if ntl >= 4:
                for k in range(4):
                    P.dma("pool", lambda k=k: nc.gpsimd.dma_start(
                        out=cT.t[:, k, 0:2048], in_=kxG[k][0:1024, :].rearrange("(r q) c -> r (q c)", q=8)), cT.ld, writes=["a_cT"])
                LD(cT.t[:, :, 2048:4096], kvT[0:512, :].rearrange("(k p) t -> p k t", p=128)[:, :, 0:2048], cT, "a_cT")
                P.dma("pool", lambda: nc.gpsimd.dma_start(
                    out=krT_.t[:, 0:2048], in_=kxG[4][0:1024, :].rearrange("(r q) c -> r (q c)", q=8)), krT_.ld, writes=["a_krT"])
                LD(krT_.t[:, 2048:4096], kvT[512:640, 0:2048], krT_, "a_krT")
                for h in range(8):
                    build_kv(h, 4096)
                    for qt in range(4):
                        attend(h, 512 * qt, 512, 4096, 2048, qt)
            if ntl >= 5:
                for sidx in range(2):
                    for kb in range(16):
                        c_ = cc[kb % 2]
                        LD(c_.t[:, :], cache_c[l, sidx, kb * 128:(kb + 1) * 128, :], c_, c_.name)
                        pb = ps[6 + kb % 2]
                        for k in range(4):
                            TR(pb.t[:, k * 128:(k + 1) * 128], c_.t[:, k * 128:(k + 1) * 128], ident, [c_.name, "cf32"], [pb.name])
                        ACT(cT.t[:, 0:4, kb * 128:(kb + 1) * 128], pb.t[:, :].rearrange("p (k n) -> p k n", k=4), AF.Copy, [pb.name], ["a_cT"])
                        k_ = ck[kb % 2]
                        LD(k_.t[:, 0:64], cache_k[l, sidx, kb * 128:(kb + 1) * 128, :], k_, k_.name)
                        LD(k_.t[:, 64:128], cache_k[l, sidx, kb * 128:(kb + 1) * 128, :], k_, k_.name)
                        pb2 = ps[4 + kb % 2]
                        TR(pb2.t[:, 0:128], k_.t[:, :], ident, [k_.name, "cf32"], [pb2.name])
                        ACT(krT_.t[:, kb * 128:(kb + 1) * 128], pb2.t[:, 0:128], AF.Copy, [pb2.name], ["a_krT"])
                    q0 = TP + 16 * sidx
                    LD(cT.t[:, :, 2048:2064], kvT[0:512, :].rearrange("(k p) t -> p k t", p=128)[:, :, q0:q0 + 16], cT, "a_cT")
                    LD(krT_.t[:, 2048:2064], kvT[512:640, q0:q0 + 16], krT_, "a_krT")
                    for h in range(8):
                        build_kv(h, 2064)
                        attend(h, q0, 16, 2064, 2048, None)
            P.end_phase()

    def ret_phase(l, full):
        with ExitStack() as ph:
            P.begin_phase(ph)
            kt_ = [Buf(P, f"r_kt{i}", [128, 16, 128], BF16) for i in range(2)]
            vt_ = [Buf(P, f"r_vt{i}", [128, 16, 256], BF16) for i in range(2)]
            S = [Buf(P, f"r_S{i}", [128, 256], F32) for i in range(2)]
            if full:
                qT_ = [Buf(P, f"r_qT{i}", [128, 2048], BF16) for i in range(2)]
                kT_ = [Buf(P, f"r_kT{i}", [128, 2048], BF16) for i in range(2)]
                S16 = [Buf(P, f"r_S16{i}", [128, 256], BF16) for i in range(2)]
                at = [Buf(P, f"r_at{i}", [128, 128], BF16) for i in range(2)]
                o32 = Buf(P, "r_o32", [128, 2, 512], F32)
                o16 = Buf(P, "r_o16", [128, 2, 512], BF16)
                q16 = Buf(P, "r_q16", [128, 2, 512], BF16)
                mean = Buf(P, "r_mean", [128, 512], F32)
                var = Buf(P, "r_var", [128, 512], F32)
                rg_ = [Buf(P, f"r_rg{i}", [128, 2, 512], BF16) for i in range(2)]
                rn_ = [Buf(P, f"r_rn{i}", [128, 2, 512], BF16) for i in range(2)]
            cnt = [0]

            def scan(h, t0, nblk, nt, cdec, s0_src, s0_flag, out_dst):
                i = cnt[0] % 2
                cnt[0] += 1
                kt, vt, S_ = kt_[i], vt_[i], S[i]
                LD(kt.t[0:nt, 0:nblk, :], rk[t0:t0 + nblk * nt, 128 * h:128 * h + 128].rearrange("(n p) d -> p n d", p=nt), kt, kt.name)
                LD(vt.t[0:nt, 0:nblk, :], rv[t0:t0 + nblk * nt, 256 * h:256 * h + 256].rearrange("(n p) d -> p n d", p=nt), vt, vt.name)
                if s0_src is None:
                    P.op("dve", lambda: nc.vector.memset(S_.t[:, :], 0.0), [], [S_.name])
                else:
                    LD(S_.t[:, :], s0_src, S_, S_.name)
                    if s0_flag:
                        TS(S_.t[:, :], S_.t[:, :], flags.t[:, 1:2], None, ALU.mult, None, [S_.name, "flags"], [S_.name])
                if full:
                    qT, kT, s16 = qT_[i], kT_[i], S16[i]
                    W_ = nblk * nt
                    LD(qT.t[:, 0:W_], rqT[128 * h:128 * h + 128, t0:t0 + W_], qT, qT.name)
                    LD(kT.t[:, 0:W_], rkT[128 * h:128 * h + 128, t0:t0 + W_], kT, kT.name)
                    ACT(s16.t[:, :], S_.t[:, :], AF.Copy, [S_.name], [s16.name])
                for n in range(nblk):
                    if full:
                        c0 = n * nt
                        g4 = n % 4
                        pa = ps[n % 2]
                        MM(pa.t[0:nt, 0:nt], kT.t[:, c0:c0 + nt], qT.t[:, c0:c0 + nt], True, True, [kT.name, qT.name], [pa.name])
                        a_ = at[n % 2]
                        TT_(a_.t[0:nt, 0:nt], pa.t[0:nt, 0:nt], cmask[0:nt, 0:nt], ALU.mult, [pa.name, "cf32"], [a_.name])
                        for e in range(2):
                            po_ = ps[2 + e]
                            MM(po_.t[:, g4 * 128:g4 * 128 + nt], vt.t[0:nt, n, e * 128:(e + 1) * 128], a_.t[0:nt, 0:nt], True, False,
                               [vt.name, a_.name], [po_.name])
                            MM(po_.t[:, g4 * 128:g4 * 128 + nt], s16.t[:, e * 128:(e + 1) * 128], qT.t[:, c0:c0 + nt], False, True,
                               [s16.name, qT.name], [po_.name])
                    pst = ps[4 + n % 2]
                    MM(pst.t[:, 0:256], kt.t[0:nt, n, :], vt.t[0:nt, n, :], True, True, [kt.name, vt.name], [pst.name])
                    STT(S_.t[:, :], S_.t[:, :], float(cdec), pst.t[:, 0:256], ALU.mult, ALU.add, [S_.name, pst.name], [S_.name])
                    if full:
                        ACT(s16.t[:, :], S_.t[:, :], AF.Copy, [S_.name], [s16.name])
                        if n % 4 == 3 or n == nblk - 1:
                            nb0 = n - (n % 4)
                            Wd = (n % 4) * 128 + nt
                            cg = t0 + nb0 * nt
                            rg = rg_[(n // 4) % 2]
                            rn = rn_[(n // 4) % 2]
                            LD(rg.t[:, :, 0:Wd], rgT[256 * h:256 * h + 256, :].rearrange("(e p) t -> p e t", p=128)[:, :, cg:cg + Wd], rg, rg.name)
                            for e in range(2):
                                ACT(o32.t[:, e, 0:Wd], ps[2 + e].t[:, 0:Wd], AF.Copy, [ps[2 + e].name], ["r_o32"])
                            ACT(o16.t[:, :, 0:Wd], o32.t[:, :, 0:Wd], AF.Copy, ["r_o32"], ["r_o16"])
                            ACT(q16.t[:, :, 0:Wd], o32.t[:, :, 0:Wd], AF.Square, ["r_o32"], ["r_q16"])
                            pm, pq = ps[6], ps[7]
                            for e in range(2):
                                MM(pm.t[:, 0:Wd], ones, o16.t[:, e, 0:Wd], e == 0, e == 1, ["cmat", "r_o16"], [pm.name])
                            for e in range(2):
                                MM(pq.t[:, 0:Wd], ones, q16.t[:, e, 0:Wd], e == 0, e == 1, ["cmat", "r_q16"], [pq.name])
                            TS(mean.t[:, 0:Wd], pm.t[:, 0:Wd], 1.0 / 256, None, ALU.mult, None, [pm.name], ["r_mean"])
                            TT_(var.t[:, 0:Wd], mean.t[:, 0:Wd], mean.t[:, 0:Wd], ALU.mult, ["r_mean"], ["r_var"])
                            STT(var.t[:, 0:Wd], pq.t[:, 0:Wd], 1.0 / 256, var.t[:, 0:Wd], ALU.mult, ALU.subtract, [pq.name, "r_var"], ["r_var"])
                            ACT(var.t[:, 0:Wd], var.t[:, 0:Wd], AF.Sqrt, ["r_var"], ["r_var"], scale=1.0, bias=1e-5)
                            P.op("dve", lambda Wd=Wd: nc.vector.reciprocal(out=var.t[:, 0:Wd], in_=var.t[:, 0:Wd]), ["r_var"], ["r_var"])
                            for e in range(2):
                                TT_(o32.t[:, e, 0:Wd], o32.t[:, e, 0:Wd], mean.t[:, 0:Wd], ALU.subtract, ["r_o32", "r_mean"], ["r_o32"])
                                TT_(o32.t[:, e, 0:Wd], o32.t[:, e, 0:Wd], var.t[:, 0:Wd], ALU.mult, ["r_o32", "r_var"], ["r_o32"])
                                TT_(rn.t[:, e, 0:Wd], o32.t[:, e, 0:Wd], rg.t[:, e, 0:Wd], ALU.mult, ["r_o32", rg.name], [rn.name])
                            ST(rnT[256 * h:256 * h + 256, :].rearrange("(e p) t -> p e t", p=128)[:, :, cg:cg + Wd], rn.t[:, :, 0:Wd], rn, rn.name)
                ST(out_dst, S_.t[:, :], S_, S_.name)

            for h in range(8):
                g = GAM[h]
                if ntl >= 4:
                    if full:
                        scan(h, 0, 16, 128, g ** 128, stG[128 * h:128 * h + 128, :], True, ret_o[l, 0, h])
                    else:
                        scan(h, 0, 16, 128, g ** 128, None, False, stL[128 * h:128 * h + 128, :])
                if full and ntl >= 5:
                    for sidx in range(2):
                        scan(h, TP + 16 * sidx, 1, 16, g ** 16, state_d[l, sidx, h], False, ret_o[l, 1 + sidx, h])
            P.end_phase()

    def out_phase(l):
        wmo, wro, wo = W["w_mla_out"][l], W["w_ret_out"][l], W["w_out"][l]
        with ExitStack() as ph:
            P.begin_phase(ph)
            x = Buf(P, "o_x", [128, KC, 512], F32)
            ao = Buf(P, "o_ao", [128, 8, 512], BF16)
            rn = Buf(P, "o_rn", [128, KC, 512], BF16)
            gm = Buf(P, "o_gm", [128, KC, 512], BF16)
            gr = Buf(P, "o_gr", [128, KC, 512], BF16)
            m = Buf(P, "o_m", [128, KC, 512], F32)
            m16 = Buf(P, "o_m16", [128, KC, 512], BF16)
            tmp = [Buf(P, f"o_tmp{i}", [128, 512], F32) for i in range(2)]
            fm = lambda T_, k: T_.rearrange("(k p) t -> p k t", p=128)
            for ti, (t0, N) in enumerate(tiles):
                LD(x.t[:, :, 0:N], fm(xT, KC)[:, :, t0:t0 + N], x, "o_x", reads=["xT"])
                LD(ao.t[:, :, 0:N], fm(aoT, 8)[:, :, t0:t0 + N], ao, "o_ao")
                LD(rn.t[:, :, 0:N], fm(rnT, KC)[:, :, t0:t0 + N], rn, "o_rn")
                LD(gm.t[:, :, 0:N], fm(gmT, KC)[:, :, t0:t0 + N], gm, "o_gm")
                LD(gr.t[:, :, 0:N], fm(grT, KC)[:, :, t0:t0 + N], gr, "o_gr")
                for cb in range(4):
                    wv, wk = WLD(wview(wmo, 512 * cb, 512), 8, 512)
                    for jj in range(4):
                        n = 4 * cb + jj
                        pb = ps[n % 2]
                        for k in range(8):
                            MM(pb.t[:, 0:N], wv[:, k, jj * 128:(jj + 1) * 128], ao.t[:, k, 0:N], k == 0, k == 7, [wk, "o_ao"], [pb.name])
                        TT_(m.t[:, n, 0:N], pb.t[:, 0:N], gm.t[:, n, 0:N], ALU.mult, [pb.name, "o_gm"], ["o_m"])
                for cb in range(4):
                    wv, wk = WLD(wview(wro, 512 * cb, 512), KC, 512)
                    for jj in range(4):
                        n = 4 * cb + jj
                        pb = ps[2 + n % 2]
                        for k in range(KC):
                            MM(pb.t[:, 0:N], wv[:, k, jj * 128:(jj + 1) * 128], rn.t[:, k, 0:N], k == 0, k == KC - 1, [wk, "o_rn"], [pb.name])
                        t_ = tmp[n % 2]
                        TT_(t_.t[:, 0:N], pb.t[:, 0:N], gr.t[:, n, 0:N], ALU.mult, [pb.name, "o_gr"], [t_.name])
                        TT_(m16.t[:, n, 0:N], t_.t[:, 0:N], m.t[:, n, 0:N], ALU.add, [t_.name, "o_m"], ["o_m16"])
                for cb in range(4):
                    wv, wk = WLD(wview(wo, 512 * cb, 512), KC, 512)
                    for jj in range(4):
                        n = 4 * cb + jj
                        pb = ps[4 + n % 2]
                        for k in range(KC):
                            MM(pb.t[:, 0:N], wv[:, k, jj * 128:(jj + 1) * 128], m16.t[:, k, 0:N], k == 0, k == KC - 1, [wk, "o_m16"], [pb.name])
                        TT_(x.t[:, n, 0:N], pb.t[:, 0:N], x.t[:, n, 0:N], ALU.add, [pb.name, "o_x"], ["o_x"])
                ST(fm(xT, KC)[:, :, t0:t0 + N], x.t[:, :, 0:N], x, "o_x", writes=["xT"])
            P.end_phase()

    def final_phase():
        with ExitStack() as ph:
            P.begin_phase(ph)
            xs = [Buf(P, f"z_x{i}", [128, KC, 512], F32) for i in range(2)]
            xn = Buf(P, "z_xn", [128, KC, 512], F32)
            sq = Buf(P, "z_sq", [128, KC, 512], BF16)
            rstd = Buf(P, "z_rstd", [128, 512], F32)
            yo = [Buf(P, f"z_yo{i}", [128, D], F32) for i in range(2)]
            xT_v = xT.rearrange("(k p) t -> p k t", p=128)
            for ti, (t0, N) in enumerate(tiles):
                x = xs[ti % 2]
                LD(x.t[:, :, 0:N], xT_v[:, :, t0:t0 + N], x, x.name, reads=["xT"])
                rmsnorm_fm(x.t, x.name, KC, N, 112, xn.t, "z_xn", sq, "z_sq", ps[6], rstd, D)
                for bi, (b0, nt) in enumerate(blocks(N)):
                    yb = yo[bi % 2]
                    for g in range(4):
                        pb = ps[g % 2]
                        for kk in range(4):
                            k = 4 * g + kk
                            TR(pb.t[0:nt, kk * 128:(kk + 1) * 128], xn.t[:, k, b0:b0 + nt], ident, ["z_xn", "cf32"], [pb.name])
                        ACT(yb.t[0:nt, 512 * g:512 * g + 512], pb.t[0:nt, :], AF.Copy, [pb.name], [yb.name])
                    ST(y_o[t0 + b0:t0 + b0 + nt, :], yb.t[0:nt, :], yb, yb.name)
            P.end_phase()

    for l in range(nlay):
        if stage >= 1 and stage != 3:
            ffn_phase(l, W["ffn1_w13"][l], W["ffn1_w2"][l], 56 * l + 0)
        if stage == 11:
            ffn_phase(l, W["ffn2_w13"][l], W["ffn2_w2"][l], 56 * l + 32)
            break
        if stage >= 2:
            proj_phase(l)
        if stage <= 3:
            break
        exch_phase([(kx[c], kxG[c]) for c in range(5)], "kx")
        if stage >= 5:
            att_phase(l)
        if stage >= 6:
            ret_phase(l, False)
            exch_phase([(stL[:, :], stG[:, :])], "stL")
            ret_phase(l, True)
        if stage >= 7:
            out_phase(l)
        if stage >= 8:
            ffn_phase(l, W["ffn2_w13"][l], W["ffn2_w2"][l], 56 * l + 32)
    if stage >= 9 or stage <= 1:
        final_phase()
    P.barrier()
    P.emit()
    es.close()
    return nc, list(W.keys())


def _host_tables(core):
    half = core % 2
    pos = np.concatenate([half * TP + np.arange(TP), np.tile(PAST + np.arange(16), 2)]).astype(np.float64)
    inv_m = 10000.0 ** (-np.arange(0, 64, 2, dtype=np.float64) / 64)
    fm = np.tile(inv_m, 4)
    ang = fm[:, None] * pos[None, :]
    mtab = np.stack([np.cos(ang), np.sin(ang)]).astype(np.float32)
    inv_r = 10000.0 ** (-np.arange(0, 128, 2, dtype=np.float64) / 128)
    fr = np.tile(inv_r, 2)
    angr = fr[:, None] * pos[None, :]
    cr, sr = np.cos(angr), np.sin(angr)
    ib = np.concatenate([np.arange(TP) % 128, np.tile(np.arange(16), 2)]).astype(np.float64)
    rtab = np.zeros((8, 4, 128, TT), np.float32)
    for h in range(8):
        lg = np.log1p(-np.exp2(-5.0 - h))
        dq = np.exp((ib + 1.0) * lg)
        dk = np.exp(-(ib + 1.0) * lg) * (128 ** -0.5)
        rtab[h, 0] = cr * dq[None]
        rtab[h, 1] = sr * dq[None]
        rtab[h, 2] = cr * dk[None]
        rtab[h, 3] = sr * dk[None]
    return mtab, rtab


def _consts():
    cm = np.zeros((4, 128, 128), np.float32)
    cm[0] = 1.0
    Rm = np.zeros((128, 128), np.float32)
    for blk in range(2):
        o = 64 * blk
        for m in range(32):
            Rm[o + m, o + m + 32] = -1.0
            Rm[o + m + 32, o + m] = 1.0
    Rr = np.zeros((128, 128), np.float32)
    for m in range(64):
        Rr[m, m + 64] = -1.0
        Rr[m + 64, m] = 1.0
    cm[1] = Rm.T
    cm[2] = Rr.T
    cf = np.zeros((2, 128, 128), np.float32)
    cf[0] = np.eye(128)
    cf[1] = (np.arange(128)[None, :] >= np.arange(128)[:, None]).astype(np.float32)
    return cm.astype(ml_dtypes.bfloat16), cf


STAGE = 99
NTL = 5


def _run(inputs, stage=STAGE, ntl=NTL, trace=False, nlay=NL, segs="q,c,kr,rq,rk,rv,g"):
    f = lambda k: np.ascontiguousarray(np.asarray(inputs[k], dtype=np.float32))
    x_prompt, x_sample = f("x_prompt"), f("x_sample")
    cache_ckv, cache_krope, state_ret = f("cache_ckv"), f("cache_krope"), f("state_ret")
    nc, wused = build(stage=stage, ntl=ntl, nlay=nlay, segs=segs)
    cm, cf = _consts()
    norms = np.zeros((128, 128), np.float32)
    col = lambda v: v.reshape(-1, 128).T
    for l in range(NL):
        b = 56 * l
        norms[:, b:b + 16] = col(f("ffn1_norm")[l])
        norms[:, b + 16:b + 32] = col(f("mix_norm")[l])
        norms[:, b + 32:b + 48] = col(f("ffn2_norm")[l])
        norms[:, b + 48:b + 52] = col(f("q_norm")[l])
        norms[:, b + 52:b + 56] = col(f("kv_norm")[l])
    norms[:, 112:128] = col(f("final_norm"))
    shared = {}
    for k in wused:
        a = f(k)
        if k in ("w_uk", "w_uv"):
            a = a.reshape(NL, 512, 1024)
        shared[k] = np.ascontiguousarray(a[:nlay])
    in_maps = []
    for c in range(8):
        b, hf = c // 2, c % 2
        mtab, rtab = _host_tables(c)
        xin = np.concatenate([x_prompt[b, hf * TP:(hf + 1) * TP], x_sample[2 * c:2 * c + 2].reshape(TS, D)], axis=0)
        flags = np.zeros((128, 2), np.float32)
        flags[:, 0] = 0.0 if hf == 1 else NEG
        flags[:, 1] = 1.0 if hf == 1 else 0.0
        m = dict(shared)
        m.update(xin=np.ascontiguousarray(xin), norms=norms, mtab=mtab, rtab=rtab, cmat=cm, cf32=cf, flags=flags,
                 cache_c=np.ascontiguousarray(cache_ckv[:, 2 * c:2 * c + 2]),
                 cache_k=np.ascontiguousarray(cache_krope[:, 2 * c:2 * c + 2]),
                 state=np.ascontiguousarray(state_ret[:, 2 * c:2 * c + 2]))
        in_maps.append(m)
    res = run_bass_kernel_spmd(nc, in_maps, core_ids=list(range(8)), trace=trace)
    R = res.results
    y_prompt = np.zeros((4, SEQ, D), np.float32)
    y_sample = np.zeros((16, 16, D), np.float32)
    ckv_p = np.zeros((NL, 4, SEQ, 512), np.float32)
    kr_p = np.zeros((NL, 4, SEQ, 64), np.float32)
    ret_p = np.zeros((NL, 4, 8, 128, 256), np.float32)
    ckv_s = np.zeros((NL, 16, 16, 512), np.float32)
    kr_s = np.zeros((NL, 16, 16, 64), np.float32)
    ret_s = np.zeros((NL, 16, 8, 128, 256), np.float32)
    for c in range(8):
        b, hf = c // 2, c % 2
        r = R[c]
        y_prompt[b, hf * TP:(hf + 1) * TP] = r["y"][:TP]
        y_sample[2 * c:2 * c + 2] = r["y"][TP:].reshape(2, 16, D)
        ckv_p[:, b, hf * TP:(hf + 1) * TP] = r["ckv"][:, :TP]
        kr_p[:, b, hf * TP:(hf + 1) * TP] = r["krope"][:, :TP]
        ckv_s[:, 2 * c:2 * c + 2] = r["ckv"][:, TP:].reshape(NL, 2, 16, 512)
        kr_s[:, 2 * c:2 * c + 2] = r["krope"][:, TP:].reshape(NL, 2, 16, 64)
        if hf == 1:
            ret_p[:, b] = r["ret"][:, 0]
        ret_s[:, 2 * c:2 * c + 2] = r["ret"][:, 1:3]
    return (y_prompt, y_sample, ckv_p, kr_p, ret_p, ckv_s, kr_s, ret_s), res


def kernel(**inputs):
    outs, _ = _run(inputs)
    return outs
```

```python
import math
from contextlib import ExitStack
import numpy as np
import ml_dtypes
import concourse.bass as bass
import concourse.mybir as mybir
from concourse.bass_utils import run_bass_kernel_spmd

F32 = mybir.dt.float32
BF16 = mybir.dt.bfloat16
AF = mybir.ActivationFunctionType
ALU = mybir.AluOpType

D = 2048
KC = 16
DFF = 5632
FC = 44
NL = 2
TP = 2048
TS = 32
TT = TP + TS
SEQ = 4096
PAST = 2048
INW = 11328
O_QL, O_CKV, O_KR, O_RQ, O_RK, O_RV, O_RG, O_GM, O_GR = 0, 512, 1024, 1088, 2112, 3136, 5184, 7232, 9280
MLA_SCALE = 192 ** -0.5
TILES = [(0, 512), (512, 512), (1024, 512), (1536, 512), (2048, 32)]
NEG = -30000.0


class Sem:
    def __init__(self, h):
        self.h = h
        self.n = 0


class Ins:
    __slots__ = ("eng", "fn", "waits", "signal", "idx", "dma", "sigidx")


class Prog:
    def __init__(self, nc, es):
        self.nc = nc
        self.es = es
        self.eng = {"pe": nc.tensor, "act": nc.scalar, "dve": nc.vector, "pool": nc.gpsimd, "sp": nc.sync}
        self.esem = {e: es.enter_context(nc.semaphore("s_" + e)) for e in self.eng}
        self.lists = []
        self.count = {e: 0 for e in self.eng}
        self.known = {e: {x: 0 for x in self.eng} for e in self.eng}
        self.knownd = {e: {} for e in self.eng}
        self.lastw = {}
        self.readers = {}
        self.last_comp = {e: None for e in self.eng}
        self.dsems = []
        self.ges = es
        self.freesems = []
        self.nsem = 0
        self.in_phase = False
        self.phase_sems = []
        self.sig = {e: 0 for e in self.eng}

    def newsem(self, name):
        if self.freesems:
            s = self.freesems.pop()
        else:
            self.nsem += 1
            s = Sem(self.ges.enter_context(self.nc.semaphore("d%d" % self.nsem)))
            self.dsems.append(s)
        if self.in_phase:
            self.phase_sems.append(s)
        return s

    def begin_phase(self, ph):
        self.es = ph
        self.in_phase = True
        self.phase_sems = []

    def end_phase(self):
        self.barrier()
        self.emit()
        self.freesems.extend(self.phase_sems)
        self.phase_sems = []
        self.in_phase = False
        self.es = self.ges

    def _deps(self, E, reads, writes):
        cmax = {}
        dmax = {}

        def add(ev):
            if ev[0] == "c":
                J = ev[1]
                if J.eng == E and E == "pe":
                    return
                if J.eng not in cmax or cmax[J.eng].idx < J.idx:
                    cmax[J.eng] = J
            else:
                s, v = ev[1], ev[2]
                if dmax.get(s, 0) < v:
                    dmax[s] = v

        for r in reads:
            w = self.lastw.get(r)
            if w is not None:
                add(w)
        for w_ in writes:
            w = self.lastw.get(w_)
            if w is not None:
                add(w)
            for ev in self.readers.get(w_, {}).values():
                add(ev)
        waits = []
        for X, J in cmax.items():
            if self.known[E][X] >= J.idx:
                continue
            self.known[E][X] = J.idx
            J.signal = True
            waits.append(("c", J))
        for s, v in dmax.items():
            if self.knownd[E].get(s, 0) >= v:
                continue
            self.knownd[E][s] = v
            waits.append(("d", s, v))
        return waits

    def _record(self, ev, reads, writes):
        key = ("c", ev[1].eng) if ev[0] == "c" else ("d", ev[1])
        for r in reads:
            self.readers.setdefault(r, {})[key] = ev
        for w in writes:
            self.lastw[w] = ev
            self.readers[w] = {}

    def op(self, E, fn, reads=(), writes=()):
        pr = [r for r in reads if isinstance(r, str) and r.startswith("ps") and r not in writes]
        if pr:
            writes = list(writes) + pr
        I = Ins()
        I.eng = E
        I.fn = fn
        I.signal = False
        I.dma = None
        self.count[E] += 1
        I.idx = self.count[E]
        I.waits = self._deps(E, reads, writes)
        self._record(("c", I), reads, writes)
        self.lists.append(I)
        self.last_comp[E] = I
        return I

    def dma(self, Q, fn, sem, reads=(), writes=()):
        I = Ins()
        I.eng = Q
        I.fn = fn
        I.signal = False
        self.count[Q] += 1
        I.idx = self.count[Q]
        I.waits = self._deps(Q, reads, writes)
        sem.n += 16
        I.dma = (sem, sem.n)
        self._record(("d", sem, sem.n), reads, writes)
        self.lists.append(I)
        return I

    def barrier(self):
        nc = self.nc
        waits = []
        for X in self.eng:
            J = self.last_comp[X]
            if J is None or X == "sp":
                continue
            if self.known["sp"][X] >= J.idx:
                continue
            self.known["sp"][X] = J.idx
            J.signal = True
            waits.append(("c", J))
        for s in self.dsems:
            if s.n > self.knownd["sp"].get(s, 0):
                self.knownd["sp"][s] = s.n
                waits.append(("d", s, s.n))
        I = Ins()
        I.eng = "sp"
        I.fn = lambda: nc.sync.nop()
        I.signal = True
        I.dma = None
        self.count["sp"] += 1
        I.idx = self.count["sp"]
        I.waits = waits
        self.lists.append(I)
        self.last_comp["sp"] = I
        for X in self.eng:
            if X == "sp":
                continue
            J = Ins()
            J.eng = X
            eng = self.eng[X]
            J.fn = (lambda e: (lambda: e.nop()))(eng)
            J.signal = False
            J.dma = None
            self.count[X] += 1
            J.idx = self.count[X]
            J.waits = [("c", I)]
            self.lists.append(J)
            self.last_comp[X] = J
        self.lastw = {}
        self.readers = {}
        for E in self.eng:
            for X in self.eng:
                self.known[E][X] = self.count[X]
            for s in self.dsems:
                self.knownd[E][s] = s.n

    def emit(self):
        sig = self.sig
        for I in self.lists:
            eng = self.eng[I.eng]
            for ev in I.waits:
                if ev[0] == "c":
                    eng.wait_ge(self.esem[ev[1].eng], ev[1].sigidx)
                else:
                    eng.wait_ge(ev[1].h, ev[2])
            bi = I.fn()
            if I.dma is not None:
                bi.then_inc(I.dma[0].h, 16)
            elif I.signal:
                sig[I.eng] += 1
                I.sigidx = sig[I.eng]
                bi.then_inc(self.esem[I.eng], 1)
        self.lists = []


class Buf:
    uid = 0

    def __init__(self, P, name, shape, dtype, psum=False):
        nc = P.nc
        self.name = name
        Buf.uid += 1
        tn = "%s_%d" % (name, Buf.uid)
        self.t = P.es.enter_context(nc.psum_tensor(tn, shape, dtype) if psum else nc.sbuf_tensor(tn, shape, dtype))
        self.P = P
        self._ld = None
        self._st = None

    @property
    def ld(self):
        if self._ld is None:
            self._ld = self.P.newsem("l_" + self.name)
        return self._ld

    @property
    def st(self):
        if self._st is None:
            self._st = self.P.newsem("t_" + self.name)
        return self._st


def build(stage=99, ntl=5, exch=True, nlay=NL, segs="q,c,kr,rq,rk,rv,g"):
    segs = set(segs.split(","))
    nc = bass.Bass("TRN2", target_bir_lowering=False)
    es = ExitStack()
    P = Prog(nc, es)
    dt_in = lambda n, s, d=F32: nc.dram_tensor(n, s, d, kind="ExternalInput").ap()
    dt_out = lambda n, s, d=F32: nc.dram_tensor(n, s, d, kind="ExternalOutput").ap()
    dt_scr = lambda n, s, d=BF16: nc.dram_tensor(n, s, d).ap()

    xin = dt_in("xin", [TT, D])
    WSHP = {"ffn1_w13": [D, 2 * DFF], "ffn1_w2": [DFF, D], "w_in": [D, INW],
            "w_uq": [512, 1536], "w_uk": [512, 1024], "w_uv": [512, 1024],
            "w_mla_out": [1024, D], "w_ret_out": [D, D], "w_out": [D, D],
            "ffn2_w13": [D, 2 * DFF], "ffn2_w2": [DFF, D]}

    class _W(dict):
        def __missing__(self, nm):
            self[nm] = dt_in(nm, [nlay] + WSHP[nm])
            return self[nm]
    W = _W()
    norms_d = dt_in("norms", [128, 128])
    mtab_d = dt_in("mtab", [2, 128, TT])
    rtab_d = dt_in("rtab", [8, 4, 128, TT])
    cmat_d = dt_in("cmat", [4, 128, 128], BF16)
    cf32_d = dt_in("cf32", [2, 128, 128])
    flags_d = dt_in("flags", [128, 2])
    cache_c = dt_in("cache_c", [NL, 2, PAST, 512])
    cache_k = dt_in("cache_k", [NL, 2, PAST, 64])
    state_d = dt_in("state", [NL, 2, 8, 128, 256])

    y_o = dt_out("y", [TT, D])
    ckv_o = dt_out("ckv", [NL, TT, 512])
    kr_o = dt_out("krope", [NL, TT, 64])
    ret_o = dt_out("ret", [NL, 3, 8, 128, 256])

    xT = dt_scr("xT", [D, TT], F32)
    qnT = dt_scr("qnT", [1024, TT])
    qrT = dt_scr("qrT", [512, TT])
    kvT = dt_scr("kvT", [640, TT])
    kx = dt_scr("kx", [5, 1024, 256], F32)
    kxG = dt_scr("kxG", [5, 2048, 256], F32)
    rqT = dt_scr("rqT", [1024, TT])
    rkT = dt_scr("rkT", [1024, TT])
    rk = dt_scr("rk", [TT, 1024])
    rv = dt_scr("rv", [TT, 2048])
    rgT = dt_scr("rgT", [D, TT])
    gmT = dt_scr("gmT", [D, TT])
    grT = dt_scr("grT", [D, TT])
    aoT = dt_scr("aoT", [1024, TT])
    rnT = dt_scr("rnT", [D, TT])
    stL = dt_scr("stL", [1024, 256], F32)
    stG = dt_scr("stG", [2048, 256], F32)

    ps = [Buf(P, f"ps{i}", [128, 512], F32, psum=True) for i in range(8)]
    norms = Buf(P, "norms_sb", [128, 128], F32)
    cmat = Buf(P, "cmat_sb", [128, 4, 128], BF16)
    cf32 = Buf(P, "cf32_sb", [128, 2, 128], F32)
    flags = Buf(P, "flags_sb", [128, 2], F32)
    wb = [Buf(P, f"wb{i}", [128, 8192], BF16) for i in range(4)]
    csem = P.newsem("csem")
    for b_ in wb:
        b_.ld
    P.dma("sp", lambda: nc.sync.dma_start(out=norms.t[:], in_=norms_d[:, :]), csem, writes=["norms"])
    P.dma("sp", lambda: nc.sync.dma_start(out=cmat.t[:], in_=cmat_d.rearrange("c p n -> p c n")), csem, writes=["cmat"])
    P.dma("sp", lambda: nc.sync.dma_start(out=cf32.t[:], in_=cf32_d.rearrange("c p n -> p c n")), csem, writes=["cf32"])
    P.dma("sp", lambda: nc.sync.dma_start(out=flags.t[:], in_=flags_d[:, :]), csem, writes=["flags"])
    ones = cmat.t[:, 0, :]
    RmT = cmat.t[:, 1, :]
    RrT = cmat.t[:, 2, :]
    ident = cf32.t[:, 0, :]
    cmask = cf32.t[:, 1, :]
    wrr = [0]

    def MM(out, lhsT, rhs, start, stop, reads, writes):
        return P.op("pe", lambda: nc.tensor.matmul(out, lhsT=lhsT, rhs=rhs, start=start, stop=stop), reads, writes)

    def TR(out, in_, idn, reads, writes):
        return P.op("pe", lambda: nc.tensor.transpose(out=out, in_=in_, identity=idn), reads, writes)

    def ACT(out, in_, func, reads, writes, scale=1.0, bias=None):
        if bias is None:
            return P.op("act", lambda: nc.scalar.activation(out=out, in_=in_, func=func, scale=scale), reads, writes)
        return P.op("act", lambda: nc.scalar.activation(out=out, in_=in_, func=func, scale=scale, bias=bias), reads, writes)

    def TT_(out, a, b, op, reads, writes, eng="dve"):
        e = nc.vector if eng == "dve" else nc.gpsimd
        return P.op(eng, lambda: e.tensor_tensor(out=out, in0=a, in1=b, op=op), reads, writes)

    def TS(out, a, s1, s2, op0, op1, reads, writes):
        if op1 is None:
            return P.op("dve", lambda: nc.vector.tensor_scalar(out=out, in0=a, scalar1=s1, scalar2=None, op0=op0), reads, writes)
        return P.op("dve", lambda: nc.vector.tensor_scalar(out=out, in0=a, scalar1=s1, scalar2=s2, op0=op0, op1=op1), reads, writes)

    def STT(out, a, s, b, op0, op1, reads, writes):
        return P.op("dve", lambda: nc.vector.scalar_tensor_tensor(out=out, in0=a, scalar=s, in1=b, op0=op0, op1=op1), reads, writes)

    def LD(dst_ap, src_ap, buf, key, reads=()):
        return P.dma("sp", lambda: nc.sync.dma_start(out=dst_ap, in_=src_ap), buf.ld, reads=list(reads), writes=[key])

    def ST(dst_ap, src_ap, buf, key, writes=()):
        return P.dma("sp", lambda: nc.sync.dma_start(out=dst_ap, in_=src_ap), buf.st, reads=[key], writes=list(writes))

    def WLD(src_ap, kc, nb):
        i = wrr[0] % len(wb)
        wrr[0] += 1
        b = wb[i]
        view = b.t[:, 0:kc * nb].rearrange("p (k n) -> p k n", k=kc)
        P.dma("pool", lambda: nc.gpsimd.dma_start(out=view, in_=src_ap), b.ld, writes=[b.name])
        return view, b.name

    def wview(w_l, c0, nb):
        return w_l.rearrange("(k p) n -> p k n", p=128)[:, :, c0:c0 + nb]

    def rmsnorm_fm(src, skey, kcn, N, gcol, dst, dkey, sq, sqkey, pbank, rstd, dim):
        ACT(sq.t[:, 0:kcn, 0:N], src[:, 0:kcn, 0:N], AF.Square, [skey], [sqkey])
        for k in range(kcn):
            MM(pbank.t[:, 0:N], ones, sq.t[:, k, 0:N], k == 0, k == kcn - 1, [sqkey, "cmat"], [pbank.name])
        ACT(rstd.t[:, 0:N], pbank.t[:, 0:N], AF.Sqrt, [pbank.name], [rstd.name], scale=1.0 / dim, bias=1e-6)
        P.op("dve", lambda: nc.vector.reciprocal(out=rstd.t[:, 0:N], in_=rstd.t[:, 0:N]), [rstd.name], [rstd.name])
        for k in range(kcn):
            STT(dst[:, k, 0:N], src[:, k, 0:N], norms.t[:, gcol + k:gcol + k + 1], rstd.t[:, 0:N], ALU.mult, ALU.mult,
                [skey, rstd.name, "norms"], [dkey])

    tiles = TILES[:ntl] if ntl < 5 else TILES
    if ntl < 5 and ntl > 0 and False:
        pass

    def blocks(N):
        return [(b * 128, min(128, N - b * 128)) for b in range((N + 127) // 128)]

    with ExitStack() as ph:
        P.begin_phase(ph)
        xtm = [Buf(P, f"in_xtm{i}", [128, D], F32) for i in range(2)]
        xfm = Buf(P, "in_xfm", [128, KC, 512], F32)
        for (t0, N) in tiles:
            for bi, (b0, nt) in enumerate(blocks(N)):
                xb = xtm[bi % 2]
                LD(xb.t[0:nt, :], xin[t0 + b0:t0 + b0 + nt, :], xb, xb.name)
                for g in range(4):
                    pb = ps[g % 2]
                    for kk in range(4):
                        k = 4 * g + kk
                        TR(pb.t[:, kk * 128:kk * 128 + nt], xb.t[0:nt, k * 128:(k + 1) * 128], ident[0:nt, 0:nt],
                           [xb.name, "cf32"], [pb.name])
                    ACT(xfm.t[:, 4 * g:4 * g + 4, b0:b0 + nt],
                        pb.t[:, :].rearrange("p (k n) -> p k n", k=4)[:, :, 0:nt], AF.Copy, [pb.name], ["xfm"])
            ST(xT.rearrange("(k p) t -> p k t", p=128)[:, :, t0:t0 + N], xfm.t[:, :, 0:N], xfm, "xfm", writes=["xT"])
        P.end_phase()

    def ffn_phase(l, w13, w2, gcol):
        with ExitStack() as ph:
            P.begin_phase(ph)
            xs = [Buf(P, f"f_x{i}", [128, KC, 512], F32) for i in range(2)]
            xn = Buf(P, "f_xn", [128, KC, 512], BF16)
            g = Buf(P, "f_g", [128, FC, 512], BF16)
            rstd = Buf(P, "f_rstd", [128, 512], F32)
            sa = [Buf(P, f"f_sa{i}", [128, 512], F32) for i in range(2)]
            xT_v = xT.rearrange("(k p) t -> p k t", p=128)
            use_r = len(tiles) == 5
            mtl = tiles[:4] if use_r else tiles
            if use_r:
                rt0, RN = tiles[4]
                xr = Buf(P, "f_xr", [128, KC, RN], F32)
                xnr = Buf(P, "f_xnr", [128, KC, RN], BF16)
                g_r = Buf(P, "f_gr", [128, FC, RN], BF16)
                rstd_r = Buf(P, "f_rstdr", [128, RN], F32)
                sa_r = [Buf(P, f"f_sar{i}", [128, RN], F32) for i in range(2)]
                pr = ps[7]

            def load(ti):
                t0, N = mtl[ti]
                b = xs[ti % 2]
                LD(b.t[:, :, 0:N], xT_v[:, :, t0:t0 + N], b, b.name, reads=["xT"])
            load(0)
            for ti, (t0, N) in enumerate(mtl):
                x = xs[ti % 2]
                rider = use_r and ti == len(mtl) - 1
                if ti + 1 < len(mtl):
                    load(ti + 1)
                if rider:
                    LD(xr.t[:, :, 0:RN], xT_v[:, :, rt0:rt0 + RN], xr, "f_xr", reads=["xT"])
                rmsnorm_fm(x.t, x.name, KC, N, gcol, xn.t, "f_xn", g, "f_g", ps[6], rstd, D)
                if rider:
                    rmsnorm_fm(xr.t, "f_xr", KC, RN, gcol, xnr.t, "f_xnr", g_r, "f_gr", ps[6], rstd_r, D)
                for jb in range(11):
                    wa, ka = WLD(wview(w13, 512 * jb, 512), KC, 512)
                    wbb, kb = WLD(wview(w13, DFF + 512 * jb, 512), KC, 512)
                    for jj in range(4):
                        j = 4 * jb + jj
                        pa, pb = ps[j % 2], ps[2 + j % 2]
                        for k in range(KC):
                            MM(pa.t[:, 0:N], wa[:, k, jj * 128:(jj + 1) * 128], xn.t[:, k, 0:N], k == 0, k == KC - 1,
                               [ka, "f_xn"], [pa.name])
                        for k in range(KC):
                            MM(pb.t[:, 0:N], wbb[:, k, jj * 128:(jj + 1) * 128], xn.t[:, k, 0:N], k == 0, k == KC - 1,
                               [kb, "f_xn"], [pb.name])
                        if rider:
                            o = (j % 2) * 128
                            for k in range(KC):
                                MM(pr.t[:, o:o + RN], wa[:, k, jj * 128:(jj + 1) * 128], xnr.t[:, k, 0:RN], k == 0, k == KC - 1,
                                   [ka, "f_xnr"], [pr.name])
                            for k in range(KC):
                                MM(pr.t[:, o + 64:o + 64 + RN], wbb[:, k, jj * 128:(jj + 1) * 128], xnr.t[:, k, 0:RN], k == 0, k == KC - 1,
                                   [kb, "f_xnr"], [pr.name])
                        s = sa[j % 2]
                        ACT(s.t[:, 0:N], pa.t[:, 0:N], AF.Silu, [pa.name], [s.name])
                        TT_(g.t[:, j, 0:N], s.t[:, 0:N], pb.t[:, 0:N], ALU.mult, [s.name, pb.name], ["f_g"])
                        if rider:
                            sr = sa_r[j % 2]
                            ACT(sr.t[:, 0:RN], pr.t[:, o:o + RN], AF.Silu, [pr.name], [sr.name])
                            TT_(g_r.t[:, j, 0:RN], sr.t[:, 0:RN], pr.t[:, o + 64:o + 64 + RN], ALU.mult, [sr.name, pr.name], ["f_gr"])
                for n in range(KC):
                    w2v, k2 = WLD(wview(w2, 128 * n, 128), FC, 128)
                    pc = ps[4 + n % 2]
                    for k in range(FC):
                        MM(pc.t[:, 0:N], w2v[:, k, :], g.t[:, k, 0:N], k == 0, k == FC - 1, [k2, "f_g"], [pc.name])
                    if rider:
                        o = 256 + (n % 2) * 64
                        for k in range(FC):
                            MM(pr.t[:, o:o + RN], w2v[:, k, :], g_r.t[:, k, 0:RN], k == 0, k == FC - 1, [k2, "f_gr"], [pr.name])
                    STT(x.t[:, n, 0:N], pc.t[:, 0:N], 0.5, x.t[:, n, 0:N], ALU.mult, ALU.add, [pc.name, x.name], [x.name])
                    if rider:
                        STT(xr.t[:, n, 0:RN], pr.t[:, o:o + RN], 0.5, xr.t[:, n, 0:RN], ALU.mult, ALU.add, [pr.name, "f_xr"], ["f_xr"])
                ST(xT_v[:, :, t0:t0 + N], x.t[:, :, 0:N], x, x.name, writes=["xT"])
                if rider:
                    ST(xT_v[:, :, rt0:rt0 + RN], xr.t[:, :, 0:RN], xr, "f_xr", writes=["xT"])
            P.end_phase()

    def proj_phase(l):
        win = W["w_in"][l]
        nb = 56 * l
        with ExitStack() as ph:
            P.begin_phase(ph)
            xs = [Buf(P, f"p_x{i}", [128, KC, 512], F32) for i in range(1)]
            u = Buf(P, "p_u", [128, KC, 512], BF16)
            sq = Buf(P, "p_sq", [128, KC, 512], BF16)
            rstd = Buf(P, "p_rstd", [128, 512], F32)
            lat = Buf(P, "p_lat", [128, 4, 512], F32)
            latn = lat
            qn = Buf(P, "p_qn", [128, 4, 512], BF16)
            cb16 = Buf(P, "p_cb16", [128, 4, 512], BF16)
            wq = Buf(P, "p_wq", [128, 4, 1536], BF16)
            wkr = Buf(P, "p_wkr", [128, KC, 128], BF16)
            mt = [Buf(P, f"p_mt{i}", [128, 2, 512], F32) for i in range(1)]
            rt = [Buf(P, f"p_rt{i}", [128, 4, 512], F32) for i in range(1)]
            xb = [Buf(P, f"p_xb{i}", [128, 512], BF16) for i in range(2)]
            t1 = [Buf(P, f"p_t1{i}", [128, 512], F32) for i in range(1)]
            t2 = [Buf(P, f"p_t2{i}", [128, 512], F32) for i in range(1)]
            kf = [Buf(P, f"p_kf{i}", [128, 512], F32) for i in range(1)]
            so = [Buf(P, f"p_so{i}", [128, 512], BF16) for i in range(4)]
            sf = [Buf(P, f"p_sf{i}", [128, 512], F32) for i in range(2)]
            sorr = [0]
            xT_v = xT.rearrange("(k p) t -> p k t", p=128)
            wuq = W["w_uq"][l].rearrange("(k p) (h c) -> p k h c", p=128, c=192)
            for k in range(4):
                P.dma("pool", lambda k=k: nc.gpsimd.dma_start(
                    out=wq.t[:, k, 0:1024].rearrange("p (h c) -> p h c", c=128), in_=wuq[:, k, :, 0:128]), wq.ld, writes=["p_wq"])
                P.dma("pool", lambda k=k: nc.gpsimd.dma_start(
                    out=wq.t[:, k, 1024:1536].rearrange("p (h c) -> p h c", c=64), in_=wuq[:, k, :, 128:192]), wq.ld, writes=["p_wq"])
            for hh in range(2):
                P.dma("pool", lambda hh=hh: nc.gpsimd.dma_start(
                    out=wkr.t[:, :, hh * 64:(hh + 1) * 64], in_=wview(win, O_KR, 64)), wkr.ld, writes=["p_wkr"])

            def load(ti):
                t0, N = tiles[ti]
                b = xs[0]
                LD(b.t[:, :, 0:N], xT_v[:, :, t0:t0 + N], b, b.name, reads=["xT"])

            def stage_store(dst_ap, src_psum, pkey, N, func=AF.Copy, scale=1.0, extra_reads=()):
                s = so[sorr[0] % 4]
                sorr[0] += 1
                ACT(s.t[:, 0:N], src_psum, func, [pkey] + list(extra_reads), [s.name], scale=scale)
                ST(dst_ap, s.t[:, 0:N], s, s.name)

            def rope_fm(pb, N, RT, cosap, sinap, tabkey, i, outbuf):
                x16 = xb[i % 2]
                ACT(x16.t[:, 0:N], pb.t[:, 0:N], AF.Copy, [pb.name], [x16.name])
                pr = ps[6 + i % 2]
                MM(pr.t[:, 0:N], RT, x16.t[:, 0:N], True, True, [x16.name, "cmat"], [pr.name])
                a = t1[0]
                b = t2[0]
                TT_(a.t[:, 0:N], pb.t[:, 0:N], cosap, ALU.mult, [pb.name, tabkey], [a.name])
                TT_(b.t[:, 0:N], pr.t[:, 0:N], sinap, ALU.mult, [pr.name, tabkey], [b.name])
                TT_(outbuf.t[:, 0:N], a.t[:, 0:N], b.t[:, 0:N], ALU.add, [a.name, b.name], [outbuf.name])

            load(0)
            for ti, (t0, N) in enumerate(tiles):
                x = xs[0]
                m = mt[0]
                LD(m.t[:, :, 0:N], mtab_d.rearrange("c p t -> p c t")[:, :, t0:t0 + N], m, m.name)
                rmsnorm_fm(x.t, x.name, KC, N, nb + 16, u.t, "p_u", sq, "p_sq", ps[5], rstd, D)
                if ti + 1 < len(tiles):
                    load(ti + 1)
                for seg, (c0, gc) in enumerate([(O_QL, nb + 48), (O_CKV, nb + 52)]):
                    if ("q" if seg == 0 else "c") not in segs:
                        continue
                    wv, wk = WLD(wview(win, c0, 512), KC, 512)
                    for j in range(4):
                        pb = ps[j % 2]
                        for k in range(KC):
                            MM(pb.t[:, 0:N], wv[:, k, j * 128:(j + 1) * 128], u.t[:, k, 0:N], k == 0, k == KC - 1, [wk, "p_u"], [pb.name])
                        ACT(lat.t[:, j, 0:N], pb.t[:, 0:N], AF.Copy, [pb.name], ["p_lat"])
                    rmsnorm_fm(lat.t, "p_lat", 4, N, gc, latn.t, "p_lat", sq, "p_sq", ps[5], rstd, 512)
                    if seg == 0:
                        ACT(qn.t[:, :, 0:N], latn.t[:, :, 0:N], AF.Copy, ["p_lat"], ["p_qn"])
                        for h in range(8):
                            pb = ps[h % 2]
                            for k in range(4):
                                MM(pb.t[:, 0:N], wq.t[:, k, 128 * h:128 * h + 128], qn.t[:, k, 0:N], k == 0, k == 3, ["p_wq", "p_qn"], [pb.name])
                            stage_store(qnT[128 * h:128 * h + 128, t0:t0 + N], pb.t[:, 0:N], pb.name, N)
                        for mm_ in range(4):
                            pb = ps[2 + mm_ % 2]
                            for k in range(4):
                                MM(pb.t[:, 0:N], wq.t[:, k, 1024 + 128 * mm_:1024 + 128 * mm_ + 128], qn.t[:, k, 0:N], k == 0, k == 3,
                                   ["p_wq", "p_qn"], [pb.name])
                            o = kf[0]
                            rope_fm(pb, N, RmT, m.t[:, 0, 0:N], m.t[:, 1, 0:N], m.name, mm_, o)
                            s = so[sorr[0] % 4]
                            sorr[0] += 1
                            ACT(s.t[:, 0:N], o.t[:, 0:N], AF.Copy, [o.name], [s.name])
                            ST(qrT[128 * mm_:128 * mm_ + 128, t0:t0 + N], s.t[:, 0:N], s, s.name)
                    else:
                        ACT(cb16.t[:, :, 0:N], latn.t[:, :, 0:N], AF.Copy, ["p_lat"], ["p_cb16"])
                        ST(kvT[0:512, :].rearrange("(k p) t -> p k t", p=128)[:, :, t0:t0 + N], cb16.t[:, :, 0:N], cb16, "p_cb16", writes=["kvT"])
                        if t0 < TP:
                            for k in range(4):
                                ST(kx[k].rearrange("(r q) c -> r (q c)", q=8)[:, t0:t0 + N], latn.t[:, k, 0:N], lat, "p_lat")
                        for bi, (b0, nt) in enumerate(blocks(N)):
                            pb = ps[2 + bi % 2]
                            for k in range(4):
                                TR(pb.t[0:nt, k * 128:(k + 1) * 128], latn.t[:, k, b0:b0 + nt], ident, ["p_lat", "cf32"], [pb.name])
                            s = sf[bi % 2]
                            ACT(s.t[0:nt, :], pb.t[0:nt, :], AF.Copy, [pb.name], [s.name])
                            ST(ckv_o[l, t0 + b0:t0 + b0 + nt, :], s.t[0:nt, :], s, s.name)
                pb = ps[0]
                if "kr" not in segs:
                    continue
                for k in range(KC):
                    MM(pb.t[:, 0:N], wkr.t[:, k, :], u.t[:, k, 0:N], k == 0, k == KC - 1, ["p_wkr", "p_u"], [pb.name])
                o = kf[0]
                rope_fm(pb, N, RmT, m.t[:, 0, 0:N], m.t[:, 1, 0:N], m.name, 0, o)
                s = so[sorr[0] % 4]
                sorr[0] += 1
                ACT(s.t[:, 0:N], o.t[:, 0:N], AF.Copy, [o.name], [s.name])
                ST(kvT[512:640, t0:t0 + N], s.t[:, 0:N], s, s.name, writes=["kvT"])
                if t0 < TP:
                    ST(kx[4].rearrange("(r q) c -> r (q c)", q=8)[:, t0:t0 + N], o.t[:, 0:N], o, o.name)
                for bi, (b0, nt) in enumerate(blocks(N)):
                    pb2 = ps[2 + bi % 2]
                    TR(pb2.t[0:nt, 0:128], o.t[:, b0:b0 + nt], ident, [o.name, "cf32"], [pb2.name])
                    s2 = sf[bi % 2]
                    ACT(s2.t[0:nt, 0:64], pb2.t[0:nt, 0:64], AF.Copy, [pb2.name], [s2.name])
                    ST(kr_o[l, t0 + b0:t0 + b0 + nt, :], s2.t[0:nt, 0:64], s2, s2.name)
                for which, c0 in enumerate([O_RQ, O_RK]):
                    if ("rq", "rk")[which] not in segs:
                        continue
                    for hb in range(2):
                        wv, wk = WLD(wview(win, c0 + 512 * hb, 512), KC, 512)
                        for jj in range(4):
                            h = 4 * hb + jj
                            r = rt[0]
                            LD(r.t[:, :, 0:N], rtab_d[h].rearrange("f p t -> p f t")[:, :, t0:t0 + N], r, r.name)
                            pb = ps[h % 2]
                            for k in range(KC):
                                MM(pb.t[:, 0:N], wv[:, k, jj * 128:(jj + 1) * 128], u.t[:, k, 0:N], k == 0, k == KC - 1, [wk, "p_u"], [pb.name])
                            o = kf[0]
                            rope_fm(pb, N, RrT, r.t[:, 2 * which, 0:N], r.t[:, 2 * which + 1, 0:N], r.name, h, o)
                            s = so[sorr[0] % 4]
                            sorr[0] += 1
                            ACT(s.t[:, 0:N], o.t[:, 0:N], AF.Copy, [o.name], [s.name])
                            dstT = rqT if which == 0 else rkT
                            ST(dstT[128 * h:128 * h + 128, t0:t0 + N], s.t[:, 0:N], s, s.name)
                            if which == 1:
                                for bi, (b0, nt) in enumerate(blocks(N)):
                                    pb2 = ps[2 + bi % 2]
                                    TR(pb2.t[0:nt, 0:128], o.t[:, b0:b0 + nt], ident, [o.name, "cf32"], [pb2.name])
                                    s2 = so[sorr[0] % 4]
                                    sorr[0] += 1
                                    cd_ = (1.0 - 2.0 ** (-5.0 - h)) ** (128 if N == 512 else 16)
                                    ACT(s2.t[0:nt, 0:128], pb2.t[0:nt, 0:128], AF.Copy, [pb2.name], [s2.name], scale=float(cd_))
                                    ST(rk[t0 + b0:t0 + b0 + nt, 128 * h:128 * h + 128], s2.t[0:nt, 0:128], s2, s2.name)
                for cb in range(4):
                    if "rv" not in segs:
                        continue
                    wv, wk = WLD(wview(win, O_RV + 512 * cb, 512), KC, 512)
                    for bi, (b0, nt) in enumerate(blocks(N)):
                        pb = ps[bi % 2]
                        for k in range(KC):
                            MM(pb.t[0:nt, :], u.t[:, k, b0:b0 + nt], wv[:, k, :], k == 0, k == KC - 1, [wk, "p_u"], [pb.name])
                        s = so[sorr[0] % 4]
                        sorr[0] += 1
                        ACT(s.t[0:nt, :], pb.t[0:nt, :], AF.Copy, [pb.name], [s.name])
                        ST(rv[t0 + b0:t0 + b0 + nt, 512 * cb:512 * cb + 512], s.t[0:nt, :], s, s.name)
                for c0, dstT, fn in [(O_GM, gmT, AF.Sigmoid), (O_GR, grT, AF.Sigmoid), (O_RG, rgT, AF.Silu)]:
                    if "g" not in segs:
                        continue
                    for cb in range(4):
                        wv, wk = WLD(wview(win, c0 + 512 * cb, 512), KC, 512)
                        for jj in range(4):
                            j = 4 * cb + jj
                            pb = ps[j % 2]
                            for k in range(KC):
                                MM(pb.t[:, 0:N], wv[:, k, jj * 128:(jj + 1) * 128], u.t[:, k, 0:N], k == 0, k == KC - 1, [wk, "p_u"], [pb.name])
                            stage_store(dstT[128 * j:128 * j + 128, t0:t0 + N], pb.t[:, 0:N], pb.name, N, func=fn)
            P.end_phase()


    GROUPS = [[0, 1], [2, 3], [4, 5], [6, 7]]
    GAM = [1.0 - 2.0 ** (-5.0 - h) for h in range(8)]

    def exch_phase(pairs, nm):
        with ExitStack() as ph:
            P.begin_phase(ph)
            for (src, dst) in pairs:
                P.op("pool", lambda src=src, dst=dst: nc.gpsimd.collective_compute(
                    "AllGather", ALU.bypass, replica_groups=GROUPS, ins=[src], outs=[dst]), [nm], [nm + "G"])
            P.end_phase()

    def att_phase(l):
        wuk, wuv = W["w_uk"][l], W["w_uv"][l]
        with ExitStack() as ph:
            P.begin_phase(ph)
            cT = Buf(P, "a_cT", [128, 4, 4096], BF16)
            krT_ = Buf(P, "a_krT", [128, 4096], BF16)
            KT = [Buf(P, f"a_KT{i}", [128, 4096], BF16) for i in range(2)]
            Vh = [Buf(P, f"a_V{i}", [128, 32, 128], BF16) for i in range(2)]
            qn_ = [Buf(P, f"a_qn{i}", [128, 512], BF16) for i in range(2)]
            qr_ = [Buf(P, f"a_qr{i}", [128, 512], BF16) for i in range(2)]
            pt = [Buf(P, f"a_pt{i}", [128, 512], BF16) for i in range(6)]
            rden = [Buf(P, f"a_rd{i}", [128, 512], F32) for i in range(2)]
            aob = [Buf(P, f"a_ao{i}", [128, 512], BF16) for i in range(2)]
            cc = [Buf(P, f"a_cc{i}", [128, 512], F32) for i in range(2)]
            ck = [Buf(P, f"a_ck{i}", [128, 128], F32) for i in range(2)]
            ctr = [0, 0, 0]

            def build_kv(h, nkeys):
                kt, vh = KT[h % 2], Vh[h % 2]
                wk_, kk_ = WLD(wview(wuk, 128 * h, 128), 4, 128)
                wv_, kv_ = WLD(wview(wuv, 128 * h, 128), 4, 128)
                c0 = 0
                while c0 < nkeys:
                    n = min(512, nkeys - c0)
                    pb = ps[6 + ctr[0] % 2]
                    ctr[0] += 1
                    for k in range(4):
                        MM(pb.t[:, 0:n], wk_[:, k, :], cT.t[:, k, c0:c0 + n], k == 0, k == 3, [kk_, "a_cT"], [pb.name])
                    ACT(kt.t[:, c0:c0 + n], pb.t[:, 0:n], AF.Copy, [pb.name], [kt.name])
                    c0 += n
                nblk = (nkeys + 127) // 128
                for g0 in range(0, nblk, 4):
                    pb = ps[6 + ctr[0] % 2]
                    ctr[0] += 1
                    gl = min(4, nblk - g0)
                    for gi in range(gl):
                        kb = g0 + gi
                        nk = min(128, nkeys - kb * 128)
                        for k in range(4):
                            MM(pb.t[0:nk, gi * 128:(gi + 1) * 128], cT.t[:, k, kb * 128:kb * 128 + nk], wv_[:, k, :], k == 0, k == 3,
                               [kv_, "a_cT"], [pb.name])
                    full = [gi for gi in range(gl) if min(128, nkeys - (g0 + gi) * 128) == 128]
                    if full:
                        nf = len(full)
                        ACT(vh.t[:, g0:g0 + nf, :], pb.t[:, 0:nf * 128].rearrange("p (g d) -> p g d", g=nf), AF.Copy, [pb.name], [vh.name])
                    if len(full) < gl:
                        gi = gl - 1
                        nk = nkeys - (g0 + gi) * 128
                        ACT(vh.t[0:nk, g0 + gi, :], pb.t[0:nk, gi * 128:(gi + 1) * 128], AF.Copy, [pb.name], [vh.name])

            def attend(h, q0, NQ, nkeys, nctx, causal_qt):
                kt, vh = KT[h % 2], Vh[h % 2]
                qn, qr = qn_[ctr[1] % 2], qr_[ctr[1] % 2]
                po_, pd_ = ps[2 + ctr[1] % 2], ps[4 + ctr[1] % 2]
                rd, ao = rden[ctr[1] % 2], aob[ctr[1] % 2]
                ctr[1] += 1
                LD(qn.t[:, 0:NQ], qnT[128 * h:128 * h + 128, q0:q0 + NQ], qn, qn.name)
                LD(qr.t[:, 0:NQ], qrT[128 * (h // 2):128 * (h // 2) + 128, q0:q0 + NQ], qr, qr.name)
                p0 = 64 * (h % 2)
                if causal_qt is None:
                    nvis = (nkeys + 127) // 128
                else:
                    nvis = nctx // 128 + 4 * causal_qt + 4
                steps = []
                for kb in range(nvis):
                    nk = min(128, nkeys - kb * 128)
                    off = 0
                    isctx = kb * 128 < nctx
                    diag = False
                    if causal_qt is not None and not isctx:
                        j = kb - nctx // 128
                        if j >= 4 * causal_qt:
                            off = 128 * (j - 4 * causal_qt)
                            diag = True
                    steps.append((kb, nk, off, isctx, diag))
                SB = [ps[0], ps[1], ps[6], ps[7]]

                def emit_S(i):
                    kb, nk, off, isctx, diag = steps[i]
                    pS = SB[ctr[2] % 4]
                    p_ = pt[ctr[2] % len(pt)]
                    ctr[2] += 1
                    MM(pS.t[0:nk, off:NQ], kt.t[:, kb * 128:kb * 128 + nk], qn.t[:, off:NQ], True, False, [kt.name, qn.name], [pS.name])
                    MM(pS.t[0:nk, off:NQ], krT_.t[p0:p0 + 64, kb * 128:kb * 128 + nk], qr.t[p0:p0 + 64, off:NQ], False, True,
                       ["a_krT", qr.name], [pS.name])
                    if isctx and causal_qt is not None:
                        ACT(p_.t[0:nk, off:NQ], pS.t[0:nk, off:NQ], AF.Exp, [pS.name, "flags"], [p_.name], scale=MLA_SCALE, bias=flags.t[0:nk, 0:1])
                    else:
                        ACT(p_.t[0:nk, off:NQ], pS.t[0:nk, off:NQ], AF.Exp, [pS.name], [p_.name], scale=MLA_SCALE)
                    if diag:
                        P.op("dve", lambda p_=p_, off=off: nc.vector.memset(p_.t[64:128, off:off + 64], 0.0), [], [p_.name])
                    return p_

                def emit_PV(i, p_):
                    kb, nk, off, isctx, diag = steps[i]
                    first, last = i == 0, i == len(steps) - 1
                    MM(po_.t[:, off:NQ], vh.t[0:nk, kb, :], p_.t[0:nk, off:NQ], first, last, [vh.name, p_.name], [po_.name])
                    MM(pd_.t[:, off:NQ], ones[0:nk, :], p_.t[0:nk, off:NQ], first, last, ["cmat", p_.name], [pd_.name])

                LOOK = 2
                pend = {}
                for i in range(min(LOOK, len(steps))):
                    pend[i] = emit_S(i)
                for i in range(len(steps)):
                    if i + LOOK < len(steps):
                        pend[i + LOOK] = emit_S(i + LOOK)
                    emit_PV(i, pend.pop(i))
                P.op("dve", lambda: nc.vector.reciprocal(out=rd.t[:, 0:NQ], in_=pd_.t[:, 0:NQ]), [pd_.name], [rd.name])
                TT_(ao.t[:, 0:NQ], po_.t[:, 0:NQ], rd.t[:, 0:NQ], ALU.mult, [po_.name, rd.name], [ao.name])
                ST(aoT[128 * h:128 * h + 128, q0:q0 + NQ], ao.t[:, 0:NQ], ao, ao.name)

            if ntl >= 4:
                for k in range(4):
                    P.dma("pool", lambda k=k: nc.gpsimd.dma_start(
                        out=cT.t[:, k, 0:2048], in_=kxG[k][0:1024, :].rearrange("(r q) c -> r (q c)", q=8)), cT.ld, writes=["a_cT"])
                LD(cT.t[:, :, 2048:4096], kvT[0:512, :].rearrange("(k p) t -> p k t", p=128)[:, :, 0:2048], cT, "a_cT")
                P.dma("pool", lambda: nc.gpsimd.dma_start(
                    out=krT_.t[:, 0:2048], in_=kxG[4][0:1024, :].rearrange("(r q) c -> r (q c)", q=8)), krT_.ld, writes=["a_krT"])
                LD(krT_.t[:, 2048:4096], kvT[512:640, 0:2048], krT_, "a_krT")
                for h in range(8):
                    build_kv(h, 4096)
                    for qt in range(4):
                        attend(h, 512 * qt, 512, 4096, 2048, qt)
            if ntl >= 5:
                for sidx in range(2):
                    for kb in range(16):
                        c_ = cc[kb % 2]
                        LD(c_.t[:, :], cache_c[l, sidx, kb * 128:(kb + 1) * 128, :], c_, c_.name)
                        pb = ps[6 + kb % 2]
                        for k in range(4):
                            TR(pb.t[:, k * 128:(k + 1) * 128], c_.t[:, k * 128:(k + 1) * 128], ident, [c_.name, "cf32"], [pb.name])
                        ACT(cT.t[:, 0:4, kb * 128:(kb + 1) * 128], pb.t[:, :].rearrange("p (k n) -> p k n", k=4), AF.Copy, [pb.name], ["a_cT"])
                        k_ = ck[kb % 2]
                        LD(k_.t[:, 0:64], cache_k[l, sidx, kb * 128:(kb + 1) * 128, :], k_, k_.name)
                        LD(k_.t[:, 64:128], cache_k[l, sidx, kb * 128:(kb + 1) * 128, :], k_, k_.name)
                        pb2 = ps[4 + kb % 2]
                        TR(pb2.t[:, 0:128], k_.t[:, :], ident, [k_.name, "cf32"], [pb2.name])
                        ACT(krT_.t[:, kb * 128:(kb + 1) * 128], pb2.t[:, 0:128], AF.Copy, [pb2.name], ["a_krT"])
                    q0 = TP + 16 * sidx
                    LD(cT.t[:, :, 2048:2064], kvT[0:512, :].rearrange("(k p) t -> p k t", p=128)[:, :, q0:q0 + 16], cT, "a_cT")
                    LD(krT_.t[:, 2048:2064], kvT[512:640, q0:q0 + 16], krT_, "a_krT")
                    for h in range(8):
                        build_kv(h, 2064)
                        attend(h, q0, 16, 2064, 2048, None)
            P.end_phase()

    def ret_phase(l, full):
        with ExitStack() as ph:
            P.begin_phase(ph)
            kt_ = [Buf(P, f"r_kt{i}", [128, 16, 128], BF16) for i in range(2)]
            vt_ = [Buf(P, f"r_vt{i}", [128, 16, 256], BF16) for i in range(2)]
            S = [Buf(P, f"r_S{i}", [128, 256], F32) for i in range(2)]
            if full:
                qT_ = [Buf(P, f"r_qT{i}", [128, 2048], BF16) for i in range(2)]
                kT_ = [Buf(P, f"r_kT{i}", [128, 2048], BF16) for i in range(2)]
                S16 = [Buf(P, f"r_S16{i}", [128, 256], BF16) for i in range(2)]
                at = [Buf(P, f"r_at{i}", [128, 128], BF16) for i in range(2)]
                o32 = Buf(P, "r_o32", [128, 2, 512], F32)
                o16 = Buf(P, "r_o16", [128, 2, 512], BF16)
                q16 = Buf(P, "r_q16", [128, 2, 512], BF16)
                mean = Buf(P, "r_mean", [128, 512], F32)
                var = Buf(P, "r_var", [128, 512], F32)
                rg_ = [Buf(P, f"r_rg{i}", [128, 2, 512], BF16) for i in range(2)]
                rn_ = [Buf(P, f"r_rn{i}", [128, 2, 512], BF16) for i in range(2)]
            cnt = [0]

            def scan(h, t0, nblk, nt, cdec, s0_src, s0_flag, out_dst):
                i = cnt[0] % 2
                cnt[0] += 1
                kt, vt, S_ = kt_[i], vt_[i], S[i]
                LD(kt.t[0:nt, 0:nblk, :], rk[t0:t0 + nblk * nt, 128 * h:128 * h + 128].rearrange("(n p) d -> p n d", p=nt), kt, kt.name)
                LD(vt.t[0:nt, 0:nblk, :], rv[t0:t0 + nblk * nt, 256 * h:256 * h + 256].rearrange("(n p) d -> p n d", p=nt), vt, vt.name)
                if s0_src is None:
                    P.op("dve", lambda: nc.vector.memset(S_.t[:, :], 0.0), [], [S_.name])
                else:
                    LD(S_.t[:, :], s0_src, S_, S_.name)
                    if s0_flag:
                        TS(S_.t[:, :], S_.t[:, :], flags.t[:, 1:2], None, ALU.mult, None, [S_.name, "flags"], [S_.name])
                if full:
                    qT, kT, s16 = qT_[i], kT_[i], S16[i]
                    W_ = nblk * nt
                    LD(qT.t[:, 0:W_], rqT[128 * h:128 * h + 128, t0:t0 + W_], qT, qT.name)
                    LD(kT.t[:, 0:W_], rkT[128 * h:128 * h + 128, t0:t0 + W_], kT, kT.name)
                    ACT(s16.t[:, :], S_.t[:, :], AF.Copy, [S_.name], [s16.name])
                for n in range(nblk):
                    if full:
                        c0 = n * nt
                        g4 = n % 4
                        pa = ps[n % 2]
                        MM(pa.t[0:nt, 0:nt], kT.t[:, c0:c0 + nt], qT.t[:, c0:c0 + nt], True, True, [kT.name, qT.name], [pa.name])
                        a_ = at[n % 2]
                        TT_(a_.t[0:nt, 0:nt], pa.t[0:nt, 0:nt], cmask[0:nt, 0:nt], ALU.mult, [pa.name, "cf32"], [a_.name])
                        for e in range(2):
                            po_ = ps[2 + e]
                            MM(po_.t[:, g4 * 128:g4 * 128 + nt], vt.t[0:nt, n, e * 128:(e + 1) * 128], a_.t[0:nt, 0:nt], True, False,
                               [vt.name, a_.name], [po_.name])
                            MM(po_.t[:, g4 * 128:g4 * 128 + nt], s16.t[:, e * 128:(e + 1) * 128], qT.t[:, c0:c0 + nt], False, True,
                               [s16.name, qT.name], [po_.name])
                    pst = ps[4 + n % 2]
                    MM(pst.t[:, 0:256], kt.t[0:nt, n, :], vt.t[0:nt, n, :], True, True, [kt.name, vt.name], [pst.name])
                    STT(S_.t[:, :], S_.t[:, :], float(cdec), pst.t[:, 0:256], ALU.mult, ALU.add, [S_.name, pst.name], [S_.name])
                    if full:
                        ACT(s16.t[:, :], S_.t[:, :], AF.Copy, [S_.name], [s16.name])
                        if n % 4 == 3 or n == nblk - 1:
                            nb0 = n - (n % 4)
                            Wd = (n % 4) * 128 + nt
                            cg = t0 + nb0 * nt
                            rg = rg_[(n // 4) % 2]
                            rn = rn_[(n // 4) % 2]
                            LD(rg.t[:, :, 0:Wd], rgT[256 * h:256 * h + 256, :].rearrange("(e p) t -> p e t", p=128)[:, :, cg:cg + Wd], rg, rg.name)
                            for e in range(2):
                                ACT(o32.t[:, e, 0:Wd], ps[2 + e].t[:, 0:Wd], AF.Copy, [ps[2 + e].name], ["r_o32"])
                            ACT(o16.t[:, :, 0:Wd], o32.t[:, :, 0:Wd], AF.Copy, ["r_o32"], ["r_o16"])
                            ACT(q16.t[:, :, 0:Wd], o32.t[:, :, 0:Wd], AF.Square, ["r_o32"], ["r_q16"])
                            pm, pq = ps[6], ps[7]
                            for e in range(2):
                                MM(pm.t[:, 0:Wd], ones, o16.t[:, e, 0:Wd], e == 0, e == 1, ["cmat", "r_o16"], [pm.name])
                            for e in range(2):
                                MM(pq.t[:, 0:Wd], ones, q16.t[:, e, 0:Wd], e == 0, e == 1, ["cmat", "r_q16"], [pq.name])
                            TS(mean.t[:, 0:Wd], pm.t[:, 0:Wd], 1.0 / 256, None, ALU.mult, None, [pm.name], ["r_mean"])
                            TT_(var.t[:, 0:Wd], mean.t[:, 0:Wd], mean.t[:, 0:Wd], ALU.mult, ["r_mean"], ["r_var"])
                            STT(var.t[:, 0:Wd], pq.t[:, 0:Wd], 1.0 / 256, var.t[:, 0:Wd], ALU.mult, ALU.subtract, [pq.name, "r_var"], ["r_var"])
                            ACT(var.t[:, 0:Wd], var.t[:, 0:Wd], AF.Sqrt, ["r_var"], ["r_var"], scale=1.0, bias=1e-5)
                            P.op("dve", lambda Wd=Wd: nc.vector.reciprocal(out=var.t[:, 0:Wd], in_=var.t[:, 0:Wd]), ["r_var"], ["r_var"])
                            for e in range(2):
                                TT_(o32.t[:, e, 0:Wd], o32.t[:, e, 0:Wd], mean.t[:, 0:Wd], ALU.subtract, ["r_o32", "r_mean"], ["r_o32"])
                                TT_(o32.t[:, e, 0:Wd], o32.t[:, e, 0:Wd], var.t[:, 0:Wd], ALU.mult, ["r_o32", "r_var"], ["r_o32"])
                                TT_(rn.t[:, e, 0:Wd], o32.t[:, e, 0:Wd], rg.t[:, e, 0:Wd], ALU.mult, ["r_o32", rg.name], [rn.name])
                            ST(rnT[256 * h:256 * h + 256, :].rearrange("(e p) t -> p e t", p=128)[:, :, cg:cg + Wd], rn.t[:, :, 0:Wd], rn, rn.name)
                ST(out_dst, S_.t[:, :], S_, S_.name)

            for h in range(8):
                g = GAM[h]
                if ntl >= 4:
                    if full:
                        scan(h, 0, 16, 128, g ** 128, stG[128 * h:128 * h + 128, :], True, ret_o[l, 0, h])
                    else:
                        scan(h, 0, 16, 128, g ** 128, None, False, stL[128 * h:128 * h + 128, :])
                if full and ntl >= 5:
                    for sidx in range(2):
                        scan(h, TP + 16 * sidx, 1, 16, g ** 16, state_d[l, sidx, h], False, ret_o[l, 1 + sidx, h])
            P.end_phase()

    def out_phase(l):
        wmo, wro, wo = W["w_mla_out"][l], W["w_ret_out"][l], W["w_out"][l]
        with ExitStack() as ph:
            P.begin_phase(ph)
            x = Buf(P, "o_x", [128, KC, 512], F32)
            ao = Buf(P, "o_ao", [128, 8, 512], BF16)
            rn = Buf(P, "o_rn", [128, KC, 512], BF16)
            gm = Buf(P, "o_gm", [128, KC, 512], BF16)
            gr = Buf(P, "o_gr", [128, KC, 512], BF16)
            m = Buf(P, "o_m", [128, KC, 512], F32)
            m16 = Buf(P, "o_m16", [128, KC, 512], BF16)
            tmp = [Buf(P, f"o_tmp{i}", [128, 512], F32) for i in range(2)]
            fm = lambda T_, k: T_.rearrange("(k p) t -> p k t", p=128)
            for ti, (t0, N) in enumerate(tiles):
                LD(x.t[:, :, 0:N], fm(xT, KC)[:, :, t0:t0 + N], x, "o_x", reads=["xT"])
                LD(ao.t[:, :, 0:N], fm(aoT, 8)[:, :, t0:t0 + N], ao, "o_ao")
                LD(rn.t[:, :, 0:N], fm(rnT, KC)[:, :, t0:t0 + N], rn, "o_rn")
                LD(gm.t[:, :, 0:N], fm(gmT, KC)[:, :, t0:t0 + N], gm, "o_gm")
                LD(gr.t[:, :, 0:N], fm(grT, KC)[:, :, t0:t0 + N], gr, "o_gr")
                for cb in range(4):
                    wv, wk = WLD(wview(wmo, 512 * cb, 512), 8, 512)
                    for jj in range(4):
                        n = 4 * cb + jj
                        pb = ps[n % 2]
                        for k in range(8):
                            MM(pb.t[:, 0:N], wv[:, k, jj * 128:(jj + 1) * 128], ao.t[:, k, 0:N], k == 0, k == 7, [wk, "o_ao"], [pb.name])
                        TT_(m.t[:, n, 0:N], pb.t[:, 0:N], gm.t[:, n, 0:N], ALU.mult, [pb.name, "o_gm"], ["o_m"])
                for cb in range(4):
                    wv, wk = WLD(wview(wro, 512 * cb, 512), KC, 512)
                    for jj in range(4):
                        n = 4 * cb + jj
                        pb = ps[2 + n % 2]
                        for k in range(KC):
                            MM(pb.t[:, 0:N], wv[:, k, jj * 128:(jj + 1) * 128], rn.t[:, k, 0:N], k == 0, k == KC - 1, [wk, "o_rn"], [pb.name])
                        t_ = tmp[n % 2]
                        TT_(t_.t[:, 0:N], pb.t[:, 0:N], gr.t[:, n, 0:N], ALU.mult, [pb.name, "o_gr"], [t_.name])
                        TT_(m16.t[:, n, 0:N], t_.t[:, 0:N], m.t[:, n, 0:N], ALU.add, [t_.name, "o_m"], ["o_m16"])
                for cb in range(4):
                    wv, wk = WLD(wview(wo, 512 * cb, 512), KC, 512)
                    for jj in range(4):
                        n = 4 * cb + jj
                        pb = ps[4 + n % 2]
                        for k in range(KC):
                            MM(pb.t[:, 0:N], wv[:, k, jj * 128:(jj + 1) * 128], m16.t[:, k, 0:N], k == 0, k == KC - 1, [wk, "o_m16"], [pb.name])
                        TT_(x.t[:, n, 0:N], pb.t[:, 0:N], x.t[:, n, 0:N], ALU.add, [pb.name, "o_x"], ["o_x"])
                ST(fm(xT, KC)[:, :, t0:t0 + N], x.t[:, :, 0:N], x, "o_x", writes=["xT"])
            P.end_phase()

    def final_phase():
        with ExitStack() as ph:
            P.begin_phase(ph)
            xs = [Buf(P, f"z_x{i}", [128, KC, 512], F32) for i in range(2)]
            xn = Buf(P, "z_xn", [128, KC, 512], F32)
            sq = Buf(P, "z_sq", [128, KC, 512], BF16)
            rstd = Buf(P, "z_rstd", [128, 512], F32)
            yo = [Buf(P, f"z_yo{i}", [128, D], F32) for i in range(2)]
            xT_v = xT.rearrange("(k p) t -> p k t", p=128)
            for ti, (t0, N) in enumerate(tiles):
                x = xs[ti % 2]
                LD(x.t[:, :, 0:N], xT_v[:, :, t0:t0 + N], x, x.name, reads=["xT"])
                rmsnorm_fm(x.t, x.name, KC, N, 112, xn.t, "z_xn", sq, "z_sq", ps[6], rstd, D)
                for bi, (b0, nt) in enumerate(blocks(N)):
                    yb = yo[bi % 2]
                    for g in range(4):
                        pb = ps[g % 2]
                        for kk in range(4):
                            k = 4 * g + kk
                            TR(pb.t[0:nt, kk * 128:(kk + 1) * 128], xn.t[:, k, b0:b0 + nt], ident, ["z_xn", "cf32"], [pb.name])
                        ACT(yb.t[0:nt, 512 * g:512 * g + 512], pb.t[0:nt, :], AF.Copy, [pb.name], [yb.name])
                    ST(y_o[t0 + b0:t0 + b0 + nt, :], yb.t[0:nt, :], yb, yb.name)
            P.end_phase()

    for l in range(nlay):
        if stage >= 1 and stage != 3:
            ffn_phase(l, W["ffn1_w13"][l], W["ffn1_w2"][l], 56 * l + 0)
        if stage == 11:
            ffn_phase(l, W["ffn2_w13"][l], W["ffn2_w2"][l], 56 * l + 32)
            break
        if stage >= 2:
            proj_phase(l)
        if stage <= 3:
            break
        exch_phase([(kx[c], kxG[c]) for c in range(5)], "kx")
        if stage >= 5:
            att_phase(l)
        if stage >= 6:
            ret_phase(l, False)
            exch_phase([(stL[:, :], stG[:, :])], "stL")
            ret_phase(l, True)
        if stage >= 7:
            out_phase(l)
        if stage >= 8:
            ffn_phase(l, W["ffn2_w13"][l], W["ffn2_w2"][l], 56 * l + 32)
    if stage >= 9 or stage <= 1:
        final_phase()
    P.barrier()
    P.emit()
    es.close()
    return nc, list(W.keys())


def _host_tables(core):
    half = core % 2
    pos = np.concatenate([half * TP + np.arange(TP), np.tile(PAST + np.arange(16), 2)]).astype(np.float64)
    inv_m = 10000.0 ** (-np.arange(0, 64, 2, dtype=np.float64) / 64)
    fm = np.tile(inv_m, 4)
    ang = fm[:, None] * pos[None, :]
    mtab = np.stack([np.cos(ang), np.sin(ang)]).astype(np.float32)
    inv_r = 10000.0 ** (-np.arange(0, 128, 2, dtype=np.float64) / 128)
    fr = np.tile(inv_r, 2)
    angr = fr[:, None] * pos[None, :]
    cr, sr = np.cos(angr), np.sin(angr)
    ib = np.concatenate([np.arange(TP) % 128, np.tile(np.arange(16), 2)]).astype(np.float64)
    rtab = np.zeros((8, 4, 128, TT), np.float32)
    for h in range(8):
        lg = np.log1p(-np.exp2(-5.0 - h))
        dq = np.exp((ib + 1.0) * lg)
        dk = np.exp(-(ib + 1.0) * lg) * (128 ** -0.5)
        rtab[h, 0] = cr * dq[None]
        rtab[h, 1] = sr * dq[None]
        rtab[h, 2] = cr * dk[None]
        rtab[h, 3] = sr * dk[None]
    return mtab, rtab


def _consts():
    cm = np.zeros((4, 128, 128), np.float32)
    cm[0] = 1.0
    Rm = np.zeros((128, 128), np.float32)
    for blk in range(2):
        o = 64 * blk
        for m in range(32):
            Rm[o + m, o + m + 32] = -1.0
            Rm[o + m + 32, o + m] = 1.0
    Rr = np.zeros((128, 128), np.float32)
    for m in range(64):
        Rr[m, m + 64] = -1.0
        Rr[m + 64, m] = 1.0
    cm[1] = Rm.T
    cm[2] = Rr.T
    cf = np.zeros((2, 128, 128), np.float32)
    cf[0] = np.eye(128)
    cf[1] = (np.arange(128)[None, :] >= np.arange(128)[:, None]).astype(np.float32)
    return cm.astype(ml_dtypes.bfloat16), cf


STAGE = 99
NTL = 5


def _run(inputs, stage=STAGE, ntl=NTL, trace=False, nlay=NL, segs="q,c,kr,rq,rk,rv,g"):
    f = lambda k: np.ascontiguousarray(np.asarray(inputs[k], dtype=np.float32))
    x_prompt, x_sample = f("x_prompt"), f("x_sample")
    cache_ckv, cache_krope, state_ret = f("cache_ckv"), f("cache_krope"), f("state_ret")
    nc, wused = build(stage=stage, ntl=ntl, nlay=nlay, segs=segs)
    cm, cf = _consts()
    norms = np.zeros((128, 128), np.float32)
    col = lambda v: v.reshape(-1, 128).T
    for l in range(NL):
        b = 56 * l
        norms[:, b:b + 16] = col(f("ffn1_norm")[l])
        norms[:, b + 16:b + 32] = col(f("mix_norm")[l])
        norms[:, b + 32:b + 48] = col(f("ffn2_norm")[l])
        norms[:, b + 48:b + 52] = col(f("q_norm")[l])
        norms[:, b + 52:b + 56] = col(f("kv_norm")[l])
    norms[:, 112:128] = col(f("final_norm"))
    shared = {}
    for k in wused:
        a = f(k)
        if k in ("w_uk", "w_uv"):
            a = a.reshape(NL, 512, 1024)
        shared[k] = np.ascontiguousarray(a[:nlay])
    in_maps = []
    for c in range(8):
        b, hf = c // 2, c % 2
        mtab, rtab = _host_tables(c)
        xin = np.concatenate([x_prompt[b, hf * TP:(hf + 1) * TP], x_sample[2 * c:2 * c + 2].reshape(TS, D)], axis=0)
        flags = np.zeros((128, 2), np.float32)
        flags[:, 0] = 0.0 if hf == 1 else NEG
        flags[:, 1] = 1.0 if hf == 1 else 0.0
        m = dict(shared)
        m.update(xin=np.ascontiguousarray(xin), norms=norms, mtab=mtab, rtab=rtab, cmat=cm, cf32=cf, flags=flags,
                 cache_c=np.ascontiguousarray(cache_ckv[:, 2 * c:2 * c + 2]),
                 cache_k=np.ascontiguousarray(cache_krope[:, 2 * c:2 * c + 2]),
                 state=np.ascontiguousarray(state_ret[:, 2 * c:2 * c + 2]))
        in_maps.append(m)
    res = run_bass_kernel_spmd(nc, in_maps, core_ids=list(range(8)), trace=trace)
    R = res.results
    y_prompt = np.zeros((4, SEQ, D), np.float32)
    y_sample = np.zeros((16, 16, D), np.float32)
    ckv_p = np.zeros((NL, 4, SEQ, 512), np.float32)
    kr_p = np.zeros((NL, 4, SEQ, 64), np.float32)
    ret_p = np.zeros((NL, 4, 8, 128, 256), np.float32)
    ckv_s = np.zeros((NL, 16, 16, 512), np.float32)
    kr_s = np.zeros((NL, 16, 16, 64), np.float32)
    ret_s = np.zeros((NL, 16, 8, 128, 256), np.float32)
    for c in range(8):
        b, hf = c // 2, c % 2
        r = R[c]
        y_prompt[b, hf * TP:(hf + 1) * TP] = r["y"][:TP]
        y_sample[2 * c:2 * c + 2] = r["y"][TP:].reshape(2, 16, D)
        ckv_p[:, b, hf * TP:(hf + 1) * TP] = r["ckv"][:, :TP]
        kr_p[:, b, hf * TP:(hf + 1) * TP] = r["krope"][:, :TP]
        ckv_s[:, 2 * c:2 * c + 2] = r["ckv"][:, TP:].reshape(NL, 2, 16, 512)
        kr_s[:, 2 * c:2 * c + 2] = r["krope"][:, TP:].reshape(NL, 2, 16, 64)
        if hf == 1:
            ret_p[:, b] = r["ret"][:, 0]
        ret_s[:, 2 * c:2 * c + 2] = r["ret"][:, 1:3]
    return (y_prompt, y_sample, ckv_p, kr_p, ret_p, ckv_s, kr_s, ret_s), res


def kernel(**inputs):
    outs, _ = _run(inputs)
    return outs
```
